# Optimizing a Trainium2 kernel written in Bass

```python
import jax
import jax.numpy as jnp
from jax import lax
import numpy as np

D_MODEL = 1024
BATCH = 32
SEQ = 2048
DEPTH = 4

CTX_LEN = 256
GRID_W = 64
N_DIR = 2
NORM_EPS = 1e-6

HG_HEADS = 4
HG_HEAD_K = 128
HG_HEAD_V = 128
HG_WIDTH = HG_HEADS * HG_HEAD_V
HG_CHUNK = 32
HG_COLS = 5 * HG_WIDTH

RW_HEADS = 8
RW_HEAD = 64
RW_WIDTH = RW_HEADS * RW_HEAD
RW_DECAY_LORA = 64
RW_AAA_LORA = 64
RW_GATE_LORA = 128
RW_GN_EPS = 64e-5
RW_SPLITS = [RW_WIDTH, 2 * RW_WIDTH, 3 * RW_WIDTH,
             3 * RW_WIDTH + N_DIR * RW_DECAY_LORA,
             3 * RW_WIDTH + N_DIR * (RW_DECAY_LORA + RW_AAA_LORA)]
RW_COLS = RW_SPLITS[-1] + RW_GATE_LORA

IN_COLS = HG_COLS + RW_COLS + 2 * D_MODEL
FFN_HIDDEN = -((-8 * D_MODEL) // (3 * 256)) * 256

kernel_name = "hgrn2_rwkv7_gated_hybrid_dit"


def rms_norm(x, g, eps=NORM_EPS):
    xf = x.astype(jnp.float32)
    y = xf * lax.rsqrt(jnp.mean(xf * xf, axis=-1, keepdims=True) + eps)
    return (y * g.astype(jnp.float32)).astype(x.dtype)


def modulate(h, shift, scale):
    return h * (1 + scale) + shift


def adaln_mod(cond, w, b):
    m = (jax.nn.silu(cond) @ w + b)[..., None, :]
    return jnp.split(m, 6, axis=-1)


def grid_shift(p):
    B, T, C = p.shape
    rows = T // GRID_W
    g = p.reshape(B, rows, GRID_W, C // 4, 4)
    left = jnp.pad(g[:, :, :-1, :, 0], ((0, 0), (0, 0), (1, 0), (0, 0)))
    right = jnp.pad(g[:, :, 1:, :, 1], ((0, 0), (0, 0), (0, 1), (0, 0)))
    up = jnp.pad(g[:, :-1, :, :, 2], ((0, 0), (1, 0), (0, 0), (0, 0)))
    down = jnp.pad(g[:, 1:, :, :, 3], ((0, 0), (0, 1), (0, 0), (0, 0)))
    return jnp.stack([left, right, up, down], axis=-1).reshape(B, T, C)


def seq_shift(p):
    B, L, C = p.shape
    g = p.reshape(B, L, C // 2, 2)
    prev = jnp.pad(g[:, :-1, :, 0], ((0, 0), (1, 0), (0, 0)))
    nxt = jnp.pad(g[:, 1:, :, 1], ((0, 0), (0, 1), (0, 0)))
    return jnp.stack([prev, nxt], axis=-1).reshape(B, L, C)


def gla_chunked(k, v, log_f, s0, q):
    T = k.shape[-2]
    n = T // HG_CHUNK

    def blocks(a):
        a = a.reshape(a.shape[:-2] + (n, HG_CHUNK, a.shape[-1]))
        return jnp.moveaxis(a, -3, 0)

    lower = jnp.tril(jnp.ones((HG_CHUNK, HG_CHUNK), dtype=bool))[:, :, None]

    def step(S, xs):
        kc, vc, gc = xs[0], xs[1], xs[2]
        b = jnp.cumsum(gc, axis=-2)
        b_end = b[..., -1:, :]
        S_new = (jnp.swapaxes(jnp.exp(b_end), -1, -2) * S
                 + jnp.einsum('...ck,...cv->...kv', kc * jnp.exp(b_end - b), vc))
        if q is None:
            return S_new, None
        qc = xs[3]
        dec = jnp.exp(jnp.where(lower, b[..., :, None, :] - b[..., None, :, :], -jnp.inf))
        scores = jnp.einsum('...ik,...jk,...ijk->...ij', qc, kc, dec)
        o = (jnp.einsum('...ij,...jv->...iv', scores, vc)
             + jnp.einsum('...ik,...kv->...iv', qc * jnp.exp(b), S))
        return S_new, o

    xs = (blocks(k), blocks(v), blocks(log_f))
    if q is not None:
        xs = xs + (blocks(q),)
    s_final, o = lax.scan(step, s0, xs)
    if q is None:
        return None, s_final
    o = jnp.moveaxis(o, 0, -3)
    return o.reshape(o.shape[:-3] + (T, o.shape[-1])), s_final


def hgrn2_branch(p, lb, norm_g, s0, read):
    B, T, _ = p.shape
    q, f, i, og = jnp.split(p, [HG_WIDTH, 3 * HG_WIDTH, 4 * HG_WIDTH], axis=-1)
    f = f.astype(jnp.float32).reshape(B, T, N_DIR, HG_WIDTH)
    log_f = jnp.logaddexp(jnp.log(lb), jnp.log1p(-lb) + jax.nn.log_sigmoid(f))
    k = (1.0 - lb) * jax.nn.sigmoid(-f)

    heads = lambda a: a.reshape(B, T, HG_HEADS, -1).transpose(0, 2, 1, 3)
    per_dir = lambda a: jnp.stack([heads(a[:, :, 0]), jnp.flip(heads(a[:, :, 1]), axis=2)])
    shared = lambda a: jnp.stack([a, jnp.flip(a, axis=2)])

    v = shared(heads(i.astype(jnp.float32)))
    qd = shared(heads(jax.nn.silu(q.astype(jnp.float32)) * HG_HEAD_K ** -0.5)) if read else None
    o, s_final = gla_chunked(per_dir(k), v, per_dir(log_f), s0, qd)
    if not read:
        return None, s_final
    o = rms_norm(o[0] + jnp.flip(o[1], axis=2), norm_g)
    o = o.transpose(0, 2, 1, 3).reshape(B, T, HG_WIDTH) * jax.nn.silu(og.astype(jnp.float32))
    return o.astype(p.dtype), s_final


def rwkv7_branch(p, shift, rw, s0, read):
    mu, w0, w2, a0, a2, g2, k_k, k_a, r_k, gn_g, gn_b = rw
    B, T, _ = p.shape
    pf = p.astype(jnp.float32)
    pf = pf + mu * (shift(pf) - pf)
    r, k, v, wd, ad, gd = jnp.split(pf, RW_SPLITS, axis=-1)
    w = -jax.nn.softplus(-(w0 + jnp.einsum('btpl,plc->btpc',
                                           jnp.tanh(wd.reshape(B, T, N_DIR, RW_DECAY_LORA)), w2))) - 0.5
    decay = jnp.exp(-jnp.exp(w))
    a = jax.nn.sigmoid(a0 + jnp.einsum('btpl,plc->btpc', ad.reshape(B, T, N_DIR, RW_AAA_LORA), a2))
    kk = (k * k_k).reshape(B, T, RW_HEADS, RW_HEAD)
    kk = (kk / jnp.maximum(jnp.linalg.norm(kk, axis=-1, keepdims=True), 1e-12)).reshape(B, T, RW_WIDTH)
    k_dir = k[:, :, None] * (1 + (a - 1) * k_a)
    b_dir = kk[:, :, None] * a

    def time_major(fwd, bwd):
        s = jnp.stack([fwd, jnp.flip(bwd, axis=1)])
        return s.reshape(N_DIR, B, T, RW_HEADS, RW_HEAD).transpose(2, 0, 1, 3, 4)

    per_dir = lambda t: time_major(t[:, :, 0], t[:, :, 1])
    shared = lambda t: time_major(t, t)
    xs = (per_dir(decay), per_dir(k_dir), per_dir(b_dir), shared(kk), shared(v))
    if read:
        xs = xs + (shared(r),)

    def step(S, xt):
        w_t, k_t, b_t, kk_t, v_t = xt[:5]
        sa = -jnp.einsum('...vk,...k->...v', S, kk_t)
        S = S * w_t[..., None, :] + sa[..., None] * b_t[..., None, :] + v_t[..., None] * k_t[..., None, :]
        if not read:
            return S, None
        return S, jnp.einsum('...vk,...k->...v', S, xt[5])

    s_final, ys = lax.scan(step, s0, xs)
    if not read:
        return None, s_final
    y = (ys[:, 0] + jnp.flip(ys[:, 1], axis=0)).transpose(1, 0, 2, 3)
    mean = jnp.mean(y, axis=-1, keepdims=True)
    var = jnp.mean(jnp.square(y - mean), axis=-1, keepdims=True)
    y = ((y - mean) * lax.rsqrt(var + RW_GN_EPS)).reshape(B, T, RW_WIDTH) * gn_g + gn_b
    hs = (B, T, RW_HEADS, RW_HEAD)
    bonus = jnp.sum(r.reshape(hs) * k_dir.sum(2).reshape(hs) * r_k, axis=-1, keepdims=True) * v.reshape(hs)
    g = jax.nn.sigmoid(gd) @ g2
    return ((y + bonus.reshape(B, T, RW_WIDTH)) * g).astype(p.dtype), s_final


def gated_merge(gates, o_hg, o_rw, proj_a, proj_b, w_out):
    g_hg, g_rw = jnp.split(gates, 2, axis=-1)
    m = jax.nn.sigmoid(g_hg) * (o_hg @ proj_a) + jax.nn.sigmoid(g_rw) * (o_rw @ proj_b)
    return m @ w_out


def ffn_sublayer(x, shift, scale, gate, g_pre, g_post, w1, w2):
    gt, up = jnp.split(modulate(rms_norm(x, g_pre), shift, scale) @ w1, 2, axis=-1)
    return x + gate * rms_norm((jax.nn.silu(gt) * up) @ w2, g_post)


def setup_inputs(seed: int = 0) -> dict:
    key = jax.random.key(seed)
    ks = jax.random.split(key, 26)
    D, L = D_MODEL, DEPTH
    nrm = lambda k, shape, s: jax.random.normal(k, shape, jnp.float32) * s
    return {
        "x": nrm(ks[0], (BATCH, SEQ, D), 1.0),
        "c": nrm(ks[1], (BATCH, D), 1.0),
        "ctx": nrm(ks[2], (BATCH, CTX_LEN, D), 1.0),
        "c_ctx": nrm(ks[3], (D,), 1.0),
        "ada_w": nrm(ks[4], (L, D, 6 * D), 0.5 * D ** -0.5),
        "ada_b": nrm(ks[5], (L, 6 * D), 0.02),
        "norm_g": 1.0 + nrm(ks[6], (L, 4, D), 0.05),
        "w_in": nrm(ks[7], (L, D, IN_COLS), D ** -0.5),
        "hg_lb_logits": 1.0 + nrm(ks[8], (L, N_DIR, HG_WIDTH), 0.5),
        "hg_norm_g": 1.0 + nrm(ks[9], (L, HG_HEAD_V), 0.05),
        "rw_mu": jax.random.uniform(ks[10], (L, RW_COLS), jnp.float32, 0.0, 1.0),
        "rw_w0": jax.random.uniform(ks[11], (L, N_DIR, RW_WIDTH), jnp.float32, -4.0, 0.0),
        "rw_w2": nrm(ks[12], (L, N_DIR, RW_DECAY_LORA, RW_WIDTH), 0.5 * RW_DECAY_LORA ** -0.5),
        "rw_a0": nrm(ks[13], (L, N_DIR, RW_WIDTH), 0.3),
        "rw_a2": nrm(ks[14], (L, N_DIR, RW_AAA_LORA, RW_WIDTH), 0.5 * RW_AAA_LORA ** -0.5),
        "rw_g2": nrm(ks[15], (L, RW_GATE_LORA, RW_WIDTH), RW_GATE_LORA ** -0.5),
        "rw_kk": 0.85 + nrm(ks[16], (L, RW_WIDTH), 0.05),
        "rw_ka": 1.0 + nrm(ks[17], (L, RW_WIDTH), 0.05),
        "rw_rk": nrm(ks[18], (L, RW_HEADS, RW_HEAD), 0.1),
        "rw_gn_g": 1.0 + nrm(ks[19], (L, RW_WIDTH), 0.05),
        "rw_gn_b": nrm(ks[20], (L, RW_WIDTH), 0.02),
        "proj_a": nrm(ks[21], (L, HG_WIDTH, D), HG_WIDTH ** -0.5),
        "proj_b": nrm(ks[22], (L, RW_WIDTH, D), RW_WIDTH ** -0.5),
        "w_out": nrm(ks[23], (L, D, D), D ** -0.5),
        "ffn_w1": nrm(ks[24], (L, D, 2 * FFN_HIDDEN), D ** -0.5),
        "ffn_w2": nrm(ks[25], (L, FFN_HIDDEN, D), FFN_HIDDEN ** -0.5),
    }


def reference(x, c, ctx, c_ctx, ada_w, ada_b, norm_g, w_in, hg_lb_logits, hg_norm_g,
              rw_mu, rw_w0, rw_w2, rw_a0, rw_a2, rw_g2, rw_kk, rw_ka, rw_rk, rw_gn_g, rw_gn_b,
              proj_a, proj_b, w_out, ffn_w1, ffn_w2):
    lb_cum = jnp.cumsum(jax.nn.softmax(hg_lb_logits.astype(jnp.float32), axis=0), axis=0)
    hg_lb = lb_cum - lb_cum[:1]
    B = x.shape[0]
    s_hg0 = jnp.zeros((N_DIR, B, HG_HEADS, HG_HEAD_K, HG_HEAD_V), jnp.float32)
    s_rw0 = jnp.zeros((N_DIR, B, RW_HEADS, RW_HEAD, RW_HEAD), jnp.float32)
    split_cols = lambda p: jnp.split(p, [HG_COLS, HG_COLS + RW_COLS], axis=-1)
    x_ctx = ctx
    for l in range(DEPTH):
        read_ctx = l < DEPTH - 1
        m_lat = adaln_mod(c, ada_w[l], ada_b[l])
        m_ctx = adaln_mod(c_ctx, ada_w[l], ada_b[l])
        rw = (rw_mu[l], rw_w0[l], rw_w2[l], rw_a0[l], rw_a2[l], rw_g2[l], rw_kk[l], rw_ka[l],
              rw_rk[l], rw_gn_g[l], rw_gn_b[l])

        hg_c, rw_c, gt_c = split_cols(modulate(rms_norm(x_ctx, norm_g[l, 0]), m_ctx[0], m_ctx[1]) @ w_in[l])
        hg_x, rw_x, gt_x = split_cols(modulate(rms_norm(x, norm_g[l, 0]), m_lat[0], m_lat[1]) @ w_in[l])

        o_hg_c, s_hg = hgrn2_branch(hg_c, hg_lb[l], hg_norm_g[l], s_hg0, read_ctx)
        o_hg_x, _ = hgrn2_branch(hg_x, hg_lb[l], hg_norm_g[l], s_hg, True)
        o_rw_c, s_rw = rwkv7_branch(rw_c, seq_shift, rw, s_rw0, read_ctx)
        o_rw_x, _ = rwkv7_branch(rw_x, grid_shift, rw, s_rw, True)

        y_x = gated_merge(gt_x, o_hg_x, o_rw_x, proj_a[l], proj_b[l], w_out[l])
        x = x + m_lat[2] * rms_norm(y_x, norm_g[l, 1])
        x = ffn_sublayer(x, m_lat[3], m_lat[4], m_lat[5], norm_g[l, 2], norm_g[l, 3], ffn_w1[l], ffn_w2[l])

        if read_ctx:
            y_c = gated_merge(gt_c, o_hg_c, o_rw_c, proj_a[l], proj_b[l], w_out[l])
            x_ctx = x_ctx + m_ctx[2] * rms_norm(y_c, norm_g[l, 1])
            x_ctx = ffn_sublayer(x_ctx, m_ctx[3], m_ctx[4], m_ctx[5], norm_g[l, 2], norm_g[l, 3],
                                 ffn_w1[l], ffn_w2[l])
    return x
```

```python
import numpy as np
from contextlib import ExitStack
import concourse.bass as bass
import concourse.mybir as mybir
from concourse.bass_utils import run_bass_kernel_spmd

F32 = mybir.dt.float32
BF16 = mybir.dt.bfloat16
AF = mybir.ActivationFunctionType
ALU = mybir.AluOpType

D = 1024
TT = 2304
NCTX = 256
TILES = [(0, 256)] + [(256 + 512 * i, 512) for i in range(4)]
NCH = TT // 64
NT128 = TT // 128
HC = 32
NHC = TT // HC
DEPTH = 4
KAPPA = float(np.exp(-0.5))

SM = {}
_o = 0
for _n, _w in [("ng", 32), ("adab", 48), ("lbl", 32), ("lsel", 4), ("hgng", 1), ("mu", 15), ("w0", 8), ("a0", 8),
               ("kk", 4), ("ka", 4), ("gng", 4), ("gnb", 4), ("rk", 4), ("cls", 6), ("eps6", 1), ("gneps", 1),
               ("tiny", 1)]:
    SM[_n] = _o
    _o += _w
NS = _o


class Sched:
    def __init__(self, nc, es, n_dma_sems=8, same_sync=True):
        self.nc = nc
        self.same_sync = same_sync
        self.engs = {'pe': nc.tensor, 'act': nc.scalar, 'dve': nc.vector, 'pool': nc.gpsimd, 'sp': nc.sync}
        self.semobj = {}
        self.cnt = {}
        for e in ('pe', 'act', 'dve', 'pool'):
            self.semobj[e] = es.enter_context(nc.semaphore("sem_" + e))
            self.cnt[e] = 0
        self.dma_sems = {}
        self.dma_rr = {}
        for q in ('sp', 'pool'):
            names = []
            for j in range(n_dma_sems):
                nm = "dma_%s_%d" % (q, j)
                self.semobj[nm] = es.enter_context(nc.semaphore(nm))
                self.cnt[nm] = 0
                names.append(nm)
            self.dma_sems[q] = names
            self.dma_rr[q] = 0
        self.seen = {e: {} for e in self.engs}
        self.last_w = {}
        self.readers = {}
        self.n_inst = 0
        self.n_wait = 0
        self.bank = 0
        self.bank7 = 0
        self.bank6 = 0

    def next_bank(self):
        b = self.bank
        self.bank = (self.bank + 1) % 8
        return b

    def next_bank6(self):
        b = self.bank6
        self.bank6 = (self.bank6 + 1) % 6
        return b

    def next_bank7(self):
        b = self.bank7
        self.bank7 = (self.bank7 + 1) % 7
        return b

    def _wait(self, eng, toks):
        best = {}
        for (s, v) in toks:
            if best.get(s, 0) < v:
                best[s] = v
        for s, v in best.items():
            if s == eng and (eng == 'pe' or not self.same_sync):
                continue
            if self.seen[eng].get(s, 0) < v:
                self.engs[eng].wait_ge(self.semobj[s], v)
                self.seen[eng][s] = v
                self.n_wait += 1

    def _deps(self, reads, writes):
        toks = []
        for k in reads:
            t = self.last_w.get(k)
            if t is not None:
                toks.append(t)
        for k in writes:
            t = self.last_w.get(k)
            if t is not None:
                toks.append(t)
            r = self.readers.get(k)
            if r:
                toks.extend(r.items())
        return toks

    def _commit(self, tok, reads, writes):
        for k in writes:
            self.last_w[k] = tok
            self.readers[k] = {}
        for k in reads:
            if k in writes:
                continue
            r = self.readers.setdefault(k, {})
            if r.get(tok[0], 0) < tok[1]:
                r[tok[0]] = tok[1]

    def op(self, eng, fn, reads=(), writes=()):
        psr = [k for k in reads if isinstance(k, tuple) and k and k[0] == 'ps']
        if psr:
            reads = [k for k in reads if k not in psr]
            writes = list(writes) + psr
        self._wait(eng, self._deps(reads, writes))
        inst = fn(self.engs[eng])
        self.cnt[eng] += 1
        inst.then_inc(self.semobj[eng], 1)
        self._commit((eng, self.cnt[eng]), reads, writes)
        self.n_inst += 1
        return inst

    def dma(self, q, out, in_, reads=(), writes=(), **kw):
        toks = self._deps(reads, writes)
        j = self.dma_rr[q]
        self.dma_rr[q] = (j + 1) % len(self.dma_sems[q])
        nm = self.dma_sems[q][j]
        if self.cnt[nm] > 0:
            toks.append((nm, self.cnt[nm]))
        self._wait(q, toks)
        self.engs[q].dma_start(out=out, in_=in_, **kw).then_inc(self.semobj[nm], 16)
        self.cnt[nm] += 16
        self._commit((nm, self.cnt[nm]), reads, writes)
        self.n_inst += 1

    def _all_toks(self):
        toks = [(e, self.cnt[e]) for e in ('pe', 'act', 'dve', 'pool') if self.cnt[e] > 0]
        for q in self.dma_sems:
            toks += [(nm, self.cnt[nm]) for nm in self.dma_sems[q] if self.cnt[nm] > 0]
        return toks

    def barrier(self):
        toks = self._all_toks()
        for e in ('sp', 'pool', 'pe', 'act', 'dve'):
            self._wait(e, toks)
        self.last_w = {k: v for k, v in self.last_w.items() if isinstance(k, tuple) and k and k[0] == 'dram'}
        self.readers = {k: v for k, v in self.readers.items() if k in self.last_w}

    def barrier_soft(self, engs=('sp', 'pool', 'pe', 'act', 'dve')):
        toks = self._all_toks()
        for e in engs:
            self._wait(e, toks)

    def finish(self):
        self._wait('sp', self._all_toks())


class StopPhase(Exception):
    pass


import os
BSTOP = int(os.environ.get('BSTOP', '99'))
STGBAR = os.environ.get('STGBAR', '0') == '1'
STGENG = tuple(os.environ.get('STGENG', 'sp,pool,pe,act,dve').split(','))


class Prog:
    def __init__(self, NB, last_layer=False, dbg=False, phases="ABCDE", same_sync=True):
        self.NB = NB
        self.last = last_layer
        self.dbg = dbg
        self.phases = phases
        self.nc = nc = bass.Bass("TRN2", target_bir_lowering=False)
        self.es = ExitStack()
        self.S = Sched(nc, self.es, same_sync=same_sync)
        ext = lambda n, s, dt=F32: nc.dram_tensor(n, s, dt, kind="ExternalInput").ap()
        scr_kind = "ExternalOutput" if dbg else "Internal"
        scr = lambda n, s, dt: nc.dram_tensor(n, s, dt, kind=scr_kind).ap()
        J = NB + 1
        self.J = J
        d = self.d = {}
        d['xT'] = ext("xT", [NB, D, TT])
        d['xo'] = nc.dram_tensor("xo", [NB, D, TT], F32, kind="ExternalOutput").ap()
        d['cT'] = ext("cT", [128, 8, J])
        d['smalls'] = ext("smalls", [128, NS])
        d['consts'] = ext("consts", [128, 3, 128])
        d['hmask'] = ext("hmask", [128, 2, 128])
        d['rmask'] = ext("rmask", [128, 2, 640])
        d['rst'] = ext("rst", [128, TT])
        d['rst32'] = ext("rst32", [128, TT])
        d['ada_w'] = ext("ada_w", [D, 6 * D])
        d['w_in'] = ext("w_in", [D, 6528])
        d['rw_w2'] = ext("rw_w2", [128, 512])
        d['rw_a2'] = ext("rw_a2", [128, 512])
        d['rw_g2'] = ext("rw_g2", [128, 512])
        d['proj_a'] = ext("proj_a", [512, D])
        d['proj_b'] = ext("proj_b", [512, D])
        d['w_out'] = ext("w_out", [D, D])
        d['ffn_w1'] = ext("ffn_w1", [D, 5632])
        d['ffn_w2'] = ext("ffn_w2", [2816, D])
        d['hgq'] = scr("hgq", [NB, 512, TT], BF16)
        d['hgf'] = scr("hgf", [NB, 1024, TT], F32)
        d['hgi'] = scr("hgi", [NB, TT, 512], BF16)
        d['hgo'] = scr("hgo", [NB, 512, TT], BF16)
        d['rwp'] = scr("rwp", [NB, 1920, TT], BF16)
        d['gat'] = scr("gat", [NB, 2048, TT], BF16)
        d['ohg'] = scr("ohg", [NB, 512, TT], BF16)
        d['orw'] = scr("orw", [NB, 512, TT], BF16)
        d['usc'] = scr("usc", [NB, 2816, TT], BF16)
        self.ps = self.es.enter_context(nc.psum_tensor("ps", [128, 8, 512], F32))

    def sb(self, st, name, shape, dt):
        return st.enter_context(self.nc.sbuf_tensor("s_" + name, shape, dt))

    def build(self):
        S, nc, d = self.S, self.nc, self.d
        es = self.es
        self.prologue()
        S.barrier()
        if 'A' in self.phases:
            self.phase_a()
            S.barrier()
        if 'B' in self.phases:
            try:
                self.phase_b()
            except StopPhase:
                pass
            S.barrier()
        if 'C' in self.phases:
            self.phase_c()
            S.barrier()
        if 'D' in self.phases:
            self.phase_d()
            S.barrier()
        if 'E' in self.phases:
            self.phase_e1()
            S.barrier()
            self.phase_e2()
            S.barrier()
        S.finish()
        es.close()
        return nc

    def prologue(self):
        S, nc, d, es, J = self.S, self.nc, self.d, self.es, self.J
        sm = self.sm = self.sb(es, "smalls", [128, NS], F32)
        S.dma('sp', sm[:], d['smalls'][:, :], writes=['smalls'])
        cT = self.sb(es, "cT", [128, 8, J], F32)
        S.dma('sp', cT[:], d['cT'][:, :, :], writes=['cT'])
        self.cbf = self.sb(es, "cbf", [128, 3, 128], BF16)
        S.dma('pool', self.cbf[:], d['consts'][:, :, :], writes=['cbf'])
        self.ident = self.cbf[:, 0, :]
        self.ones = self.cbf[:, 1, :]
        self.bones = self.cbf[:, 2, :]
        self.mod = mod = self.sb(es, "mod", [128, 48, J], F32)
        self.A1 = self.sb(es, "A1", [128, 8, J], F32)
        self.G1 = self.sb(es, "G1", [128, 8, J], F32)
        self.A2 = self.sb(es, "A2", [128, 8, J], F32)
        self.G2 = self.sb(es, "G2", [128, 8, J], F32)
        self.lbv = self.sb(es, "lbv", [128, 3, 8], F32)
        self.muv = self.sb(es, "muv", [128, 7, 15], F32)
        self.kav = self.sb(es, "kav", [128, 4], F32)
        with ExitStack() as ph:
            sc = self.sb(ph, "silu_c", [128, 8, J], F32)
            S.op('act', lambda e: e.activation(out=sc[:], in_=cT[:], func=AF.Silu), reads=['cT'], writes=['silu_c'])
            aw = self.sb(ph, "adaw", [128, 2, 8, 768], F32)
            for g in range(8):
                buf = g % 2
                for kc in range(8):
                    S.dma('sp' if kc % 2 == 0 else 'pool', aw[:, buf, kc, :],
                          d['ada_w'][kc * 128:(kc + 1) * 128, g * 768:(g + 1) * 768], writes=[('adaw', buf, kc)])
                bank = S.next_bank()
                for c6 in range(6):
                    for kc in range(8):
                        S.op('pe', lambda e, c6=c6, kc=kc: e.matmul(
                            self.ps[:, bank, c6 * J:(c6 + 1) * J], lhsT=aw[:, buf, kc, c6 * 128:(c6 + 1) * 128],
                            rhs=sc[:, kc, :], start=(kc == 0), stop=(kc == 7)),
                            reads=[('adaw', buf, kc), 'silu_c'], writes=[('ps', bank)])
                S.op('dve', lambda e: e.tensor_tensor(
                    out=mod[:, g * 6:(g + 1) * 6, :],
                    in0=self.ps[:, bank, 0:6 * J].rearrange("p (c j) -> p c j", j=J),
                    in1=sm[:, SM['adab'] + g * 6:SM['adab'] + (g + 1) * 6].unsqueeze(2).to_broadcast([128, 6, J]),
                    op=ALU.add), reads=[('ps', bank), 'smalls'], writes=['mod'])
            ng = lambda w: sm[:, SM['ng'] + w * 8:SM['ng'] + (w + 1) * 8].unsqueeze(2).to_broadcast([128, 8, J])
            tmp = self.sb(ph, "ptmp", [128, 8, J], F32)
            for (dst, mi, gi, plus1, nm) in [(self.A1, 1, 0, True, 'A1'), (self.G1, 2, 1, False, 'G1'),
                                             (self.A2, 4, 2, True, 'A2'), (self.G2, 5, 3, False, 'G2')]:
                src = mod[:, mi * 8:(mi + 1) * 8, :]
                if plus1:
                    S.op('dve', lambda e, src=src: e.tensor_scalar(out=tmp[:], in0=src, scalar1=1.0, scalar2=None,
                                                                   op0=ALU.add), reads=['mod'], writes=['ptmp'])
                    src = tmp[:]
                S.op('dve', lambda e, src=src, dst=dst, gi=gi: e.tensor_tensor(out=dst[:], in0=src, in1=ng(gi),
                                                                               op=ALU.mult),
                     reads=['mod', 'ptmp', 'smalls'], writes=[nm])
            ex = self.sb(ph, "lb_ex", [128, 4, 8], F32)
            den = self.sb(ph, "lb_den", [128, 8], F32)
            num = self.sb(ph, "lb_num", [128, 8], F32)
            S.op('act', lambda e: e.activation(out=ex[:], in_=sm[:, SM['lbl']:SM['lbl'] + 32].rearrange(
                "p (l c) -> p l c", c=8), func=AF.Exp), reads=['smalls'], writes=['lb_ex'])
            S.op('dve', lambda e: e.tensor_tensor(out=den[:], in0=ex[:, 0, :], in1=ex[:, 1, :], op=ALU.add),
                 reads=['lb_ex'], writes=['lb_den'])
            for l in (2, 3):
                S.op('dve', lambda e, l=l: e.tensor_tensor(out=den[:], in0=den[:], in1=ex[:, l, :], op=ALU.add),
                     reads=['lb_ex', 'lb_den'], writes=['lb_den'])
            S.op('dve', lambda e: e.tensor_scalar(out=num[:], in0=ex[:, 0, :], scalar1=sm[:, SM['lsel']:SM['lsel'] + 1],
                                                  scalar2=None, op0=ALU.mult), reads=['lb_ex', 'smalls'],
                 writes=['lb_num'])
            for l in (1, 2, 3):
                S.op('dve', lambda e, l=l: e.scalar_tensor_tensor(
                    out=num[:], in0=ex[:, l, :], scalar=sm[:, SM['lsel'] + l:SM['lsel'] + l + 1], in1=num[:],
                    op0=ALU.mult, op1=ALU.add), reads=['lb_ex', 'lb_num', 'smalls'], writes=['lb_num'])
            S.op('dve', lambda e: e.reciprocal(out=den[:], in_=den[:]), reads=['lb_den'], writes=['lb_den'])
            S.op('dve', lambda e: e.tensor_tensor(out=self.lbv[:, 0, :], in0=num[:], in1=den[:], op=ALU.mult),
                 reads=['lb_num', 'lb_den'], writes=['lbv'])
            S.op('dve', lambda e: e.tensor_scalar(out=self.lbv[:, 1, :], in0=self.lbv[:, 0, :], scalar1=-1.0,
                                                  scalar2=1.0, op0=ALU.mult, op1=ALU.add), reads=['lbv'],
                 writes=['lbv'])
            S.op('dve', lambda e: e.tensor_scalar(out=self.lbv[:, 2, :], in0=self.lbv[:, 0, :], scalar1=1.0,
                                                  scalar2=-1.0, op0=ALU.mult, op1=ALU.add), reads=['lbv'],
                 writes=['lbv'])
            mu = sm[:, SM['mu']:SM['mu'] + 15]
            S.op('dve', lambda e: e.tensor_scalar(out=self.muv[:, 0, :], in0=mu, scalar1=-1.0, scalar2=1.0,
                                                  op0=ALU.mult, op1=ALU.add), reads=['smalls'], writes=['muv'])
            for c in range(6):
                S.op('dve', lambda e, c=c: e.tensor_scalar(out=self.muv[:, 1 + c, :], in0=mu,
                                                           scalar1=sm[:, SM['cls'] + c:SM['cls'] + c + 1],
                                                           scalar2=None, op0=ALU.mult), reads=['smalls'],
                     writes=['muv'])
            S.op('dve', lambda e: e.tensor_scalar(out=self.kav[:], in0=sm[:, SM['ka']:SM['ka'] + 4], scalar1=-1.0,
                                                  scalar2=1.0, op0=ALU.mult, op1=ALU.add), reads=['smalls'],
                 writes=['kav'])
            S.barrier()

    def rms_mod_tile(self, ph, pfx, xt, xkeys, n, A, B, Akey, Bkey, j, bufs):
        S = self.S
        sq, rstd, h, hkey = bufs
        S.op('act', lambda e: e.activation(out=sq[:, :, :n], in_=xt[:, :, :n], func=AF.Square),
             reads=xkeys, writes=[pfx + 'sq'])
        bank = S.next_bank()
        for kc in range(8):
            S.op('pe', lambda e, kc=kc: e.matmul(self.ps[:, bank, :n], lhsT=self.ones, rhs=sq[:, kc, :n],
                                                 start=(kc == 0), stop=(kc == 7)),
                 reads=[pfx + 'sq', 'cbf'], writes=[('ps', bank)])
        S.op('act', lambda e: e.activation(out=rstd[:, :n], in_=self.ps[:, bank, :n], func=AF.Sqrt,
                                           scale=1.0 / D, bias=self.sm[:, SM['eps6']:SM['eps6'] + 1]),
             reads=[('ps', bank), 'smalls'], writes=[pfx + 'rstd'])
        S.op('dve', lambda e: e.reciprocal(out=rstd[:, :n], in_=rstd[:, :n]), reads=[pfx + 'rstd'],
             writes=[pfx + 'rstd'])
        S.op('dve', lambda e: e.tensor_tensor(out=xt[:, :, :n], in0=xt[:, :, :n],
                                              in1=rstd[:, :n].unsqueeze(1).to_broadcast([128, 8, n]), op=ALU.mult),
             reads=xkeys + [pfx + 'rstd'], writes=xkeys)
        for kc in range(8):
            S.op('pool', lambda e, kc=kc: e.tensor_scalar(out=h[:, kc, :n], in0=xt[:, kc, :n],
                                                          scalar1=A[:, kc, j:j + 1], scalar2=B[:, kc, j:j + 1],
                                                          op0=ALU.mult, op1=ALU.add),
                 reads=xkeys + [Akey, Bkey], writes=[hkey])

    def post_norm_resid(self, pfx, ysb, sq, rstd, xt, xkey, n, G, Gkey, j):
        S = self.S
        bank = S.next_bank()
        for kc in range(8):
            S.op('pe', lambda e, kc=kc: e.matmul(self.ps[:, bank, :n], lhsT=self.ones, rhs=sq[:, kc, :n],
                                                 start=(kc == 0), stop=(kc == 7)),
                 reads=[pfx + 'sq', 'cbf'], writes=[('ps', bank)])
        S.op('act', lambda e: e.activation(out=rstd[:, :n], in_=self.ps[:, bank, :n], func=AF.Sqrt,
                                           scale=1.0 / D, bias=self.sm[:, SM['eps6']:SM['eps6'] + 1]),
             reads=[('ps', bank), 'smalls'], writes=[pfx + 'rstd'])
        S.op('dve', lambda e: e.reciprocal(out=rstd[:, :n], in_=rstd[:, :n]), reads=[pfx + 'rstd'],
             writes=[pfx + 'rstd'])
        for cb in range(8):
            S.op('dve', lambda e, cb=cb: e.scalar_tensor_tensor(out=ysb[:, cb, :n], in0=ysb[:, cb, :n],
                                                                scalar=G[:, cb, j:j + 1], in1=rstd[:, :n],
                                                                op0=ALU.mult, op1=ALU.mult),
                 reads=[pfx + 'ysb', pfx + 'rstd', Gkey], writes=[pfx + 'ysb'])
        S.op('pool', lambda e: e.tensor_tensor(out=xt[:, :, :n], in0=xt[:, :, :n], in1=ysb[:, :, :n], op=ALU.add),
             reads=[pfx + 'ysb', xkey], writes=[xkey])

    def seq_tiles(self, skip_ctx=False):
        out = []
        for b in range(self.NB):
            for ti, (t0, n) in enumerate(TILES):
                if skip_ctx and ti == 0:
                    continue
                out.append((b, ti, t0, n, self.NB if ti == 0 else b))
        return out

    def load_w_bf(self, dst, dram, nk, key):
        for kc in range(nk):
            self.S.dma('pool', dst[:, kc, :], dram[kc * 128:(kc + 1) * 128, :], writes=[(key, kc)])

    def phase_a(self):
        S, nc, d = self.S, self.nc, self.d
        FUNC = {}
        for cb in range(51):
            if cb < 4 or 16 <= cb < 20:
                FUNC[cb] = AF.Silu
            elif 4 <= cb < 12 or cb >= 35:
                FUNC[cb] = AF.Sigmoid
            else:
                FUNC[cb] = AF.Copy
        groups = [('hgq', 0, BF16, [0, 1, 2, 3]), ('hgf', 0, F32, [4, 5, 6, 7]), ('hgf', 512, F32, [8, 9, 10, 11]),
                  ('hgo', 0, BF16, [16, 17, 18, 19])]
        for g in range(4):
            cbs = list(range(20 + 4 * g, min(20 + 4 * g + 4, 35)))
            groups.append(('rwp', 512 * g, BF16, cbs))
        for g in range(4):
            groups.append(('gat', 512 * g, BF16, list(range(35 + 4 * g, 39 + 4 * g))))
        with ExitStack() as ph:
            wbf = self.sb(ph, "a_w", [128, 8, 6528], BF16)
            self.load_w_bf(wbf, d['w_in'], 8, 'a_w')
            wkeys = [('a_w', kc) for kc in range(8)]
            xts = self.sb(ph, "a_x", [128, 1, 8, 512], F32)
            sq = self.sb(ph, "a_sq", [128, 8, 512], BF16)
            rstd = self.sb(ph, "a_rstd", [128, 512], F32)
            hs = self.sb(ph, "a_h", [128, 1, 8, 512], BF16)
            stb = self.sb(ph, "a_stb", [128, 3, 4, 512], BF16)
            stf = self.sb(ph, "a_stf", [128, 2, 4, 512], F32)
            sti = self.sb(ph, "a_sti", [128, 2, 4, 512], BF16)
            nb16 = 0
            nf32 = 0
            tiles = self.seq_tiles()
            for it, (b, ti, t0, n, j) in enumerate(tiles):
                xb = 0
                xkey = ('a_x', xb)
                hkey = ('a_h', xb)
                S.dma('sp', xts[:, xb, :, :n], d['xT'][b, :, t0:t0 + n].rearrange("(kc p) t -> p kc t", p=128),
                      reads=[('dram', 'xT', b, ti)], writes=[xkey])
                self.rms_mod_tile(ph, 'a_', xts[:, xb], [xkey], n, self.A1, self.mod, 'A1', 'mod', j,
                                  (sq, rstd, hs[:, xb], hkey))
                h = hs[:, xb]
                for (dn, roff, dt, cbs) in groups:
                    if dt == BF16:
                        sl = nb16 % 3
                        nb16 += 1
                        st = stb[:, sl]
                        skey = ('a_stb', sl)
                    else:
                        sl = nf32 % 2
                        nf32 += 1
                        st = stf[:, sl]
                        skey = ('a_stf', sl)
                    for ci, cb in enumerate(cbs):
                        bank = S.next_bank()
                        for kc in range(8):
                            S.op('pe', lambda e, kc=kc, cb=cb, bank=bank: e.matmul(
                                self.ps[:, bank, :n], lhsT=wbf[:, kc, cb * 128:(cb + 1) * 128], rhs=h[:, kc, :n],
                                start=(kc == 0), stop=(kc == 7)), reads=[('a_w', kc), hkey], writes=[('ps', bank)])
                        S.op('act', lambda e, ci=ci, cb=cb, bank=bank, st=st: e.activation(
                            out=st[:, ci, :n], in_=self.ps[:, bank, :n], func=FUNC[cb]),
                            reads=[('ps', bank)], writes=[skey])
                    nb = len(cbs)
                    S.dma('sp', d[dn][b, roff:roff + nb * 128, t0:t0 + n].rearrange("(c p) t -> p c t", p=128),
                          st[:, :nb, :n], reads=[skey], writes=[('dram', dn, b, ti)])
                sl = it % 2
                for s in range(n // 128):
                    bank = S.next_bank()
                    for kc in range(8):
                        S.op('pe', lambda e, kc=kc, s=s, bank=bank: e.matmul(
                            self.ps[:, bank, :], lhsT=h[:, kc, s * 128:(s + 1) * 128], rhs=wbf[:, kc, 1536:2048],
                            start=(kc == 0), stop=(kc == 7)), reads=[('a_w', kc), hkey], writes=[('ps', bank)])
                    S.op('dve', lambda e, s=s, bank=bank: e.tensor_copy(out=sti[:, sl, s, :], in_=self.ps[:, bank, :]),
                         reads=[('ps', bank)], writes=[('a_sti', sl)])
                ns = n // 128
                S.dma('sp', d['hgi'][b, t0:t0 + n, :].rearrange("(s p) c -> p s c", p=128), sti[:, sl, :ns, :],
                      reads=[('a_sti', sl)], writes=[('dram', 'hgi', b, ti)])

    def phase_b(self):
        S, d = self.S, self.d
        QS = float(128 ** -0.5)
        with ExitStack() as ph:
            sb = lambda n, s_, dt: self.sb(ph, n, s_, dt)
            hm = sb("b_hm", [128, 2, 128], F32)
            S.dma('sp', hm[:], d['hmask'][:, :, :], writes=['b_hm'])
            rst = sb("b_rst", [128, TT], F32)
            S.dma('sp', rst[:], d['rst32'][:, :], writes=['b_rst'])
            q = sb("b_q", [128, TT], BF16)
            og = sb("b_og", [128, TT], BF16)
            sf = sb("b_sf", [128, TT], F32)
            V = sb("b_V", [128, NT128, 128], BF16)
            lf = sb("b_lf", [128, TT], F32)
            kf = sb("b_kf", [128, TT], F32)
            bcs = sb("b_bcs", [128, TT], F32)
            tmp = sb("b_tmp", [128, TT], F32)
            eX = sb("b_eX", [128, TT], F32)
            qs = sb("b_qs", [128, 2, TT], BF16)
            kh = sb("b_kh", [128, 2, TT], BF16)
            qe = sb("b_qe", [128, 2, TT], BF16)
            ke = sb("b_ke", [128, 2, TT], BF16)
            khT = sb("b_khT", [128, 2, NT128, 128], BF16)
            Vbd = sb("b_Vbd", [128, NT128, 4, 128], BF16)
            S.op('pool', lambda e: e.memset(Vbd[:], 0.0), writes=['b_Vbd'])
            ebe = sb("b_ebe", [128, 2, NHC], F32)
            cend = sb("b_cend", [128, NHC], F32)
            Sst = sb("b_S", [128, 2, 128], F32)
            Sall = sb("b_Sall", [128, 2, NHC, 128], BF16)
            P = sb("b_P", [128, 2, 4, 128], BF16)
            osb = sb("b_osb", [128, 512], F32)
            osq = sb("b_osq", [128, 512], BF16)
            rstd = sb("b_rstd", [128, 512], F32)
            ost = sb("b_ost", [128, 2, 512], BF16)
            v3 = lambda t: t[:].rearrange("p (c i) -> p c i", i=HC)
            nout = 0
            for b in range(self.NB):
                for h in range(4):
                    rows = slice(h * 128, (h + 1) * 128)
                    allt = [('dram', 'hgq', b, ti) for ti in range(5)]
                    S.dma('sp', q[:], d['hgq'][b, rows, :], reads=[('dram', 'hgq', b, ti) for ti in range(5)],
                          writes=['b_q'])
                    S.dma('sp', og[:], d['hgo'][b, rows, :], reads=[('dram', 'hgo', b, ti) for ti in range(5)],
                          writes=['b_og'])
                    S.dma('pool', V[:], d['hgi'][b, :, rows].rearrange("(t p) c -> p t c", p=128),
                          reads=[('dram', 'hgi', b, ti) for ti in range(5)], writes=['b_V'])
                    for cc in range(4):
                        S.dma('pool' if cc % 2 else 'sp', Vbd[cc * 32:(cc + 1) * 32, :, cc, :],
                              d['hgi'][b, :, rows].rearrange("(t p) c -> p t c", p=128)[cc * 32:(cc + 1) * 32],
                              reads=[('dram', 'hgi', b, ti) for ti in range(5)], writes=['b_Vbd'])
                    for dr in range(2):
                        ie = HC - 1 if dr == 0 else 0
                        im = HC // 2 - 1 if dr == 0 else HC // 2
                        li = dr * 4 + h
                        S.dma('sp', sf[:], d['hgf'][b, dr * 512 + h * 128:dr * 512 + (h + 1) * 128, :],
                              reads=[('dram', 'hgf', b, ti) for ti in range(5)], writes=['b_sf'])
                        S.op('act', lambda e: e.activation(out=lf[:], in_=sf[:], func=AF.Ln,
                                                           scale=self.lbv[:, 1, li:li + 1],
                                                           bias=self.lbv[:, 0, li:li + 1]),
                             reads=['b_sf', 'lbv'], writes=['b_lf'])
                        S.op('pool', lambda e: e.tensor_scalar(out=kf[:], in0=sf[:], scalar1=self.lbv[:, 2, li:li + 1],
                                                               scalar2=self.lbv[:, 1, li:li + 1], op0=ALU.mult,
                                                               op1=ALU.add), reads=['b_sf', 'lbv'], writes=['b_kf'])
                        S.op('dve', lambda e: e.tensor_tensor_scan(out=bcs[:], data0=rst[:], data1=lf[:], initial=0.0,
                                                                   op0=ALU.mult, op1=ALU.add),
                             reads=['b_rst', 'b_lf'], writes=['b_bcs'])
                        if dr == 0:
                            Bd, Bkey = bcs, 'b_bcs'
                        else:
                            S.op('act', lambda e: e.activation(out=cend[:], in_=v3(bcs)[:, :, HC - 1], func=AF.Copy),
                                 reads=['b_bcs'], writes=['b_cend'])
                            S.op('dve', lambda e: e.tensor_tensor(out=tmp[:], in0=lf[:], in1=bcs[:], op=ALU.subtract),
                                 reads=['b_lf', 'b_bcs'], writes=['b_tmp'])
                            S.op('dve', lambda e: e.tensor_tensor(
                                out=v3(bcs), in0=v3(tmp), in1=cend[:].unsqueeze(2).to_broadcast([128, NHC, HC]),
                                op=ALU.add), reads=['b_tmp', 'b_cend'], writes=['b_bcs'])
                            Bd, Bkey = bcs, 'b_bcs'
                        S.op('act', lambda e: e.activation(out=eX[:], in_=Bd[:], func=AF.Exp), reads=[Bkey],
                             writes=['b_eX'])
                        S.op('act', lambda e: e.activation(out=ebe[:, dr, :], in_=v3(eX)[:, :, ie], func=AF.Copy),
                             reads=['b_eX'], writes=[('b_ebe', dr)])
                        S.op('dve', lambda e: e.scalar_tensor_tensor(out=qs[:, dr, :], in0=q[:], scalar=QS, in1=eX[:],
                                                                     op0=ALU.mult, op1=ALU.mult),
                             reads=['b_q', 'b_eX'], writes=[('b_qs', dr)])
                        S.op('dve', lambda e: e.tensor_tensor(
                            out=v3(tmp), in0=v3(Bd)[:, :, ie:ie + 1].to_broadcast([128, NHC, HC]), in1=v3(Bd),
                            op=ALU.subtract), reads=[Bkey], writes=['b_tmp'])
                        S.op('act', lambda e: e.activation(out=eX[:], in_=tmp[:], func=AF.Exp), reads=['b_tmp'],
                             writes=['b_eX'])
                        S.op('pool', lambda e: e.tensor_tensor(out=kh[:, dr, :], in0=kf[:], in1=eX[:], op=ALU.mult),
                             reads=['b_kf', 'b_eX'], writes=[('b_kh', dr)])
                        S.op('dve', lambda e: e.tensor_tensor(
                            out=v3(tmp), in0=v3(Bd), in1=v3(Bd)[:, :, im:im + 1].to_broadcast([128, NHC, HC]),
                            op=ALU.subtract), reads=[Bkey], writes=['b_tmp'])
                        S.op('act', lambda e: e.activation(out=eX[:], in_=tmp[:], func=AF.Exp), reads=['b_tmp'],
                             writes=['b_eX'])
                        S.op('dve', lambda e: e.scalar_tensor_tensor(out=qe[:, dr, :], in0=q[:], scalar=QS, in1=eX[:],
                                                                     op0=ALU.mult, op1=ALU.mult),
                             reads=['b_q', 'b_eX'], writes=[('b_qe', dr)])
                        S.op('act', lambda e: e.activation(out=eX[:], in_=tmp[:], func=AF.Exp, scale=-1.0),
                             reads=['b_tmp'], writes=['b_eX'])
                        S.op('pool', lambda e: e.tensor_tensor(out=ke[:, dr, :], in0=kf[:], in1=eX[:], op=ALU.mult),
                             reads=['b_kf', 'b_eX'], writes=[('b_ke', dr)])
                        if BSTOP <= 1:
                            return
                        pbf = self.ps[:, 4, :].bitcast(BF16)
                        for g in range(3):
                            for tt_ in range(6):
                                n_ = g * 6 + tt_
                                S.op('pe', lambda e, n_=n_, tt_=tt_: e.transpose(
                                    out=pbf[:, tt_ * 128:(tt_ + 1) * 128], in_=kh[:, dr, n_ * 128:(n_ + 1) * 128],
                                    identity=self.ident), reads=[('b_kh', dr), 'cbf'], writes=[('ps', 4)])
                            S.op('act', lambda e, g=g: e.activation(
                                out=khT[:, dr, g * 6:(g + 1) * 6, :],
                                in_=pbf[:, 0:768].rearrange("p (t c) -> p t c", c=128), func=AF.Copy),
                                reads=[('ps', 4)], writes=[('b_khT', dr)])

                    if BSTOP <= 2:
                        return
                    S.op('pool', lambda e: e.memset(Sst[:], 0.0), writes=[('b_S', 0), ('b_S', 1)])
                    nctx = NCTX // HC
                    orders = [list(range(NHC)), list(range(nctx - 1, -1, -1)) + list(range(NHC - 1, nctx - 1, -1))]
                    ntl = [0, 0]
                    cur_bank = [0, 0]
                    for i in range(NHC):
                        for dr in range(2):
                            c = orders[dr][i]
                            n_, cc = c // 4, c % 4
                            S.op('act', lambda e, c=c, dr=dr: e.activation(out=Sall[:, dr, c, :], in_=Sst[:, dr, :],
                                                                           func=AF.Copy),
                                 reads=[('b_S', dr)], writes=[('b_Sall', dr, c)])
                            if cc == (0 if dr == 0 else 3):
                                bank = dr * 2 + ntl[dr] % 2
                                ntl[dr] += 1
                                cur_bank[dr] = bank
                                S.op('pe', lambda e, n_=n_, dr=dr, bank=bank: e.matmul(
                                    self.ps[:, bank, :], lhsT=khT[:, dr, n_, :],
                                    rhs=Vbd[:, n_, :, :].rearrange("p a b -> p (a b)"), start=True, stop=True),
                                    reads=[('b_khT', dr), 'b_Vbd'], writes=[('ps', bank)])
                            bank = cur_bank[dr]
                            S.op('dve', lambda e, c=c, dr=dr, bank=bank, cc=cc: e.scalar_tensor_tensor(
                                out=Sst[:, dr, :], in0=Sst[:, dr, :], scalar=ebe[:, dr, c:c + 1],
                                in1=self.ps[:, bank, cc * 128:(cc + 1) * 128], op0=ALU.mult, op1=ALU.add),
                                reads=[('b_S', dr), ('b_ebe', dr), ('ps', bank)], writes=[('b_S', dr)])
                    if BSTOP <= 3:
                        return
                    for ti, (t0, n) in enumerate(TILES):
                        ng_ = n // 128
                        n0 = t0 // 128
                        for dr in range(2):
                            bk = 5 + dr
                            for g in range(ng_):
                                cs_ = slice((n0 + g) * 128, (n0 + g + 1) * 128)
                                S.op('pe', lambda e, g=g, cs_=cs_, dr=dr, bk=bk: e.matmul(
                                    self.ps[:, bk, g * 128:(g + 1) * 128], lhsT=ke[:, dr, cs_], rhs=qe[:, dr, cs_],
                                    start=True, stop=True), reads=[('b_ke', dr), ('b_qe', dr)], writes=[('ps', bk)])
                            S.op('dve', lambda e, dr=dr, bk=bk: e.tensor_tensor(
                                out=P[:, dr, :ng_, :],
                                in0=self.ps[:, bk, :ng_ * 128].rearrange("p (g i) -> p g i", i=128),
                                in1=hm[:, dr, :].unsqueeze(1).to_broadcast([128, ng_, 128]), op=ALU.mult),
                                reads=[('ps', bk), 'b_hm'], writes=[('b_P', dr)])
                        if BSTOP <= 4:
                            continue
                        for g in range(ng_):
                            n_ = n0 + g
                            ocols = slice(g * 128, (g + 1) * 128)
                            S.op('pe', lambda e, g=g, n_=n_, ocols=ocols: e.matmul(
                                self.ps[:, 7, ocols], lhsT=V[:, n_, :], rhs=P[:, 0, g, :], start=True, stop=False),
                                reads=['b_V', ('b_P', 0)], writes=[('ps', 7)])
                            S.op('pe', lambda e, g=g, n_=n_, ocols=ocols: e.matmul(
                                self.ps[:, 7, ocols], lhsT=V[:, n_, :], rhs=P[:, 1, g, :], start=False, stop=False),
                                reads=['b_V', ('b_P', 1)], writes=[('ps', 7)])
                            for cc in range(4):
                                c = 4 * n_ + cc
                                for dr in range(2):
                                    S.op('pe', lambda e, g=g, c=c, cc=cc, dr=dr: e.matmul(
                                        self.ps[:, 7, g * 128 + cc * 32:g * 128 + (cc + 1) * 32],
                                        lhsT=Sall[:, dr, c, :], rhs=qs[:, dr, c * 32:(c + 1) * 32], start=False,
                                        stop=(cc == 3 and dr == 1)),
                                        reads=[('b_Sall', dr, c), ('b_qs', dr)], writes=[('ps', 7)])
                        if BSTOP <= 5:
                            continue
                        S.op('act', lambda e: e.activation(out=osb[:, :n], in_=self.ps[:, 7, :n], func=AF.Copy),
                             reads=[('ps', 7)], writes=['b_osb'])
                        S.op('act', lambda e: e.activation(out=osq[:, :n], in_=self.ps[:, 7, :n], func=AF.Square),
                             reads=[('ps', 7)], writes=['b_osq'])
                        S.op('pe', lambda e: e.matmul(self.ps[:, 4, :n], lhsT=self.ones, rhs=osq[:, :n], start=True,
                                                      stop=True), reads=['b_osq', 'cbf'], writes=[('ps', 4)])
                        S.op('act', lambda e: e.activation(out=rstd[:, :n], in_=self.ps[:, 4, :n], func=AF.Sqrt,
                                                           scale=1.0 / 128, bias=self.sm[:, SM['eps6']:SM['eps6'] + 1]),
                             reads=[('ps', 4), 'smalls'], writes=['b_rstd'])
                        S.op('dve', lambda e: e.reciprocal(out=rstd[:, :n], in_=rstd[:, :n]), reads=['b_rstd'],
                             writes=['b_rstd'])
                        S.op('dve', lambda e: e.scalar_tensor_tensor(
                            out=osb[:, :n], in0=osb[:, :n], scalar=self.sm[:, SM['hgng']:SM['hgng'] + 1],
                            in1=rstd[:, :n], op0=ALU.mult, op1=ALU.mult), reads=['b_osb', 'b_rstd', 'smalls'],
                            writes=['b_osb'])
                        sl = nout % 2
                        nout += 1
                        S.op('pool', lambda e, sl=sl: e.tensor_tensor(out=ost[:, sl, :n], in0=osb[:, :n],
                                                                      in1=og[:, t0:t0 + n], op=ALU.mult),
                             reads=['b_osb', 'b_og'], writes=[('b_ost', sl)])
                        S.dma('sp', d['ohg'][b, rows, t0:t0 + n], ost[:, sl, :n], reads=[('b_ost', sl)],
                              writes=[('dram', 'ohg', b, ti, h)])

    def phase_c(self):
        S, d = self.S, self.d
        K_ = KAPPA

        def cap(base, off, dims):
            return bass.AP(tensor=base.tensor, offset=base.offset + off, ap=[list(base.ap[0])] + [list(x) for x in dims])

        with ExitStack() as ph:
            sb = lambda n, s_, dt: self.sb(ph, n, s_, dt)
            rmk = sb("c_rmk", [128, 2, 640], F32)
            S.dma('sp', rmk[:], d['rmask'][:, :, :], writes=['c_rmk'])
            rst = sb("c_rst", [128, TT], F32)
            S.dma('sp', rst[:], d['rst'][:, :], writes=['c_rst'])
            w2z = sb("c_w2z", [128, 2, 512], BF16)
            a2z = sb("c_a2z", [128, 2, 512], BF16)
            g2b = sb("c_g2", [128, 512], BF16)
            S.op('pool', lambda e: e.memset(w2z[:], 0.0), writes=['c_w2z'])
            S.op('pool', lambda e: e.memset(a2z[:], 0.0), writes=['c_a2z'])
            for dr in range(2):
                S.dma('pool', w2z[dr * 64:(dr + 1) * 64, dr, :], d['rw_w2'][dr * 64:(dr + 1) * 64, :], writes=['c_w2z'])
                S.dma('pool', a2z[dr * 64:(dr + 1) * 64, dr, :], d['rw_a2'][dr * 64:(dr + 1) * 64, :], writes=['c_a2z'])
            S.dma('pool', g2b[:], d['rw_g2'][:, :], writes=['c_g2'])
            twd = sb("c_twd", [128, TT], BF16)
            adm = sb("c_adm", [128, TT], BF16)
            sgd = sb("c_sgd", [128, TT], BF16)
            rm = sb("c_rm", [128, TT], BF16)
            km = sb("c_km", [128, TT], BF16)
            vm = sb("c_vm", [128, TT], BF16)
            kkb = sb("c_kkb", [128, TT], BF16)
            ksum = sb("c_ksum", [128, TT], BF16)
            yacc = sb("c_yacc", [128, TT], F32)
            raw = sb("c_raw", [128, TT], BF16)
            Ts = sb("c_Ts", [128, TT], F32)
            Ta = sb("c_Ta", [128, TT], F32)
            TG = sb("c_TG", [128, TT], F32)
            Tx = sb("c_Tx", [128, TT], F32)
            kd = sb("c_kd", [128, TT], BF16)
            bd = sb("c_bd", [128, TT], BF16)
            ARt = sb("c_ARt", [128, NT128, 2, 128], BF16)
            btz = sb("c_btz", [128, 2, TT], BF16)
            ktz = sb("c_ktz", [128, 2, TT], BF16)
            bhn = sb("c_bhn", [128, TT], BF16)
            khh = sb("c_kh", [128, TT], BF16)
            gam = sb("c_gam", [128, NCH], F32)
            cend = sb("c_cend", [128, NCH], F32)
            S.op('pool', lambda e: e.memset(btz[:], 0.0), writes=['c_btz'])
            S.op('pool', lambda e: e.memset(ktz[:], 0.0), writes=['c_ktz'])
            W2 = sb("c_W2", [128, 2, 2, 2, 128], BF16)
            XX = sb("c_XX", [128, 2, 2, 2, 2, 128], BF16)
            MA = sb("c_MA", [128, 2, 2, 256], BF16)
            MB = sb("c_MB", [128, 2, 2, 256], BF16)
            NNb = sb("c_NN", [128, 2, 2, 128], BF16)
            PQ = sb("c_PQ", [128, 2, 2, 2, 128], BF16)
            bhnbd = sb("c_bhnbd", [128, 2, 2, 2, 128], BF16)
            khbd = sb("c_khbd", [128, 2, 2, 2, 128], BF16)
            Vpad = sb("c_Vpad", [128, 2, 2, 128], BF16)
            RpT = sb("c_RpT", [128, 2, 128], BF16)
            GmT = sb("c_GmT", [128, 2, 2, 128], BF16)
            Zst = sb("c_Z", [128, 128], F32)
            Zbd = sb("c_Zbd", [128, 128], BF16)
            for t_, k_ in [(PQ, 'c_PQ'), (bhnbd, 'c_bhnbd'), (khbd, 'c_khbd'), (Vpad, 'c_Vpad')]:
                S.op('pool', lambda e, t_=t_: e.memset(t_[:], 0.0), writes=[(k_, 0), (k_, 1)])
            e_yc = sb("c_eyc", [128, 512], F32)
            e_t = sb("c_et", [128, 512], F32)
            e_bf = sb("c_ebf", [128, 512], BF16)
            e_rstd = sb("c_erstd", [128, 512], F32)
            e_out = sb("c_eout", [128, 2, 512], BF16)
            v3 = lambda t: t[:].rearrange("p (c i) -> p c i", i=64)
            g3 = lambda t: t[:, NCTX:].rearrange("p (r c) -> p r c", c=64)
            sm = self.sm
            muv = self.muv

            def mix(blk, acc, acckey):
                col = lambda c: muv[:, c, blk:blk + 1]
                S.op('dve', lambda e: e.tensor_scalar(out=acc[:], in0=raw[:], scalar1=col(0), scalar2=None,
                                                      op0=ALU.mult), reads=['c_raw', 'muv'], writes=[acckey])
                ga, gr = g3(acc), g3(raw)
                views = [(ga[:, :, 1:64], gr[:, :, 0:63], 1), (ga[:, :, 0:63], gr[:, :, 1:64], 2),
                         (ga[:, 1:32, :], gr[:, 0:31, :], 3), (ga[:, 0:31, :], gr[:, 1:32, :], 4),
                         (acc[:, 1:NCTX], raw[:, 0:NCTX - 1], 5), (acc[:, 0:NCTX - 1], raw[:, 1:NCTX], 6)]
                for (o_, i_, c) in views:
                    S.op('dve', lambda e, o_=o_, i_=i_, c=c: e.scalar_tensor_tensor(
                        out=o_, in0=i_, scalar=col(c), in1=o_, op0=ALU.mult, op1=ALU.add),
                        reads=['c_raw', 'muv', acckey], writes=[acckey])

            def load_raw(b, blk):
                S.dma('sp', raw[:], d['rwp'][b, blk * 128:(blk + 1) * 128, :],
                      reads=[('dram', 'rwp', b, ti) for ti in range(5)], writes=['c_raw'])

            nout = 0
            for b in range(self.NB):
                for (blk, dst, fn, key) in [(12, twd, AF.Tanh, 'c_twd'), (13, adm, AF.Copy, 'c_adm'),
                                            (14, sgd, AF.Sigmoid, 'c_sgd')]:
                    load_raw(b, blk)
                    mix(blk, Ts, 'c_Ts')
                    S.op('act', lambda e, dst=dst, fn=fn: e.activation(out=dst[:], in_=Ts[:], func=fn),
                         reads=['c_Ts'], writes=[key])
                for hp in range(4):
                    hcols = slice(hp * 128, (hp + 1) * 128)
                    load_raw(b, hp)
                    mix(hp, Ts, 'c_Ts')
                    S.op('act', lambda e: e.activation(out=rm[:], in_=Ts[:], func=AF.Copy), reads=['c_Ts'],
                         writes=['c_rm'])
                    load_raw(b, 8 + hp)
                    mix(8 + hp, Ts, 'c_Ts')
                    S.op('act', lambda e: e.activation(out=vm[:], in_=Ts[:], func=AF.Copy), reads=['c_Ts'],
                         writes=['c_vm'])
                    load_raw(b, 4 + hp)
                    mix(4 + hp, Ts, 'c_Ts')
                    S.op('act', lambda e: e.activation(out=km[:], in_=Ts[:], func=AF.Copy), reads=['c_Ts'],
                         writes=['c_km'])
                    S.op('dve', lambda e: e.tensor_scalar(out=Ta[:], in0=Ts[:], scalar1=sm[:, SM['kk'] + hp:SM['kk'] + hp + 1],
                                                          scalar2=None, op0=ALU.mult), reads=['c_Ts', 'smalls'],
                         writes=['c_Ta'])
                    S.op('act', lambda e: e.activation(out=raw[:], in_=Ta[:], func=AF.Square), reads=['c_Ta'],
                         writes=['c_raw'])
                    for (t0, n) in TILES:
                        bank = S.next_bank7()
                        S.op('pe', lambda e, bank=bank: e.matmul(self.ps[:, bank, :n], lhsT=self.bones,
                                                                 rhs=raw[:, t0:t0 + n], start=True, stop=True),
                             reads=['c_raw', 'cbf'], writes=[('ps', bank)])
                        S.op('act', lambda e, bank=bank: e.activation(out=TG[:, t0:t0 + n], in_=self.ps[:, bank, :n],
                                                                      func=AF.Sqrt), reads=[('ps', bank)],
                             writes=['c_TG'])
                    S.op('dve', lambda e: e.tensor_scalar(out=TG[:], in0=TG[:], scalar1=1e-12, scalar2=None,
                                                          op0=ALU.max), reads=['c_TG'], writes=['c_TG'])
                    S.op('dve', lambda e: e.reciprocal(out=TG[:], in_=TG[:]), reads=['c_TG'], writes=['c_TG'])
                    S.op('dve', lambda e: e.tensor_tensor(out=kkb[:], in0=Ta[:], in1=TG[:], op=ALU.mult),
                         reads=['c_Ta', 'c_TG'], writes=['c_kkb'])
                    for dr in range(2):
                        ie = 63 if dr == 0 else 0
                        wi = dr * 4 + hp
                        for (wz, wkey, src, skey, bcol, dst, dkey) in [
                                (w2z, 'c_w2z', twd, 'c_twd', SM['w0'] + wi, Ts, 'c_Ts'),
                                (a2z, 'c_a2z', adm, 'c_adm', SM['a0'] + wi, Ta, 'c_Ta')]:
                            for (t0, n) in TILES:
                                bank = S.next_bank7()
                                S.op('pe', lambda e, bank=bank, wz=wz, src=src: e.matmul(
                                    self.ps[:, bank, :n], lhsT=wz[:, dr, hcols], rhs=src[:, t0:t0 + n], start=True,
                                    stop=True), reads=[wkey, skey], writes=[('ps', bank)])
                                S.op('act', lambda e, bank=bank, dst=dst, bcol=bcol: e.activation(
                                    out=dst[:, t0:t0 + n], in_=self.ps[:, bank, :n], func=AF.Sigmoid,
                                    bias=sm[:, bcol:bcol + 1]), reads=[('ps', bank), 'smalls'], writes=[dkey])
                        S.op('dve', lambda e: e.tensor_scalar(out=Tx[:], in0=Ta[:],
                                                              scalar1=sm[:, SM['ka'] + hp:SM['ka'] + hp + 1],
                                                              scalar2=self.kav[:, hp:hp + 1], op0=ALU.mult,
                                                              op1=ALU.add), reads=['c_Ta', 'smalls', 'kav'],
                             writes=['c_Tx'])
                        S.op('dve', lambda e: e.tensor_tensor(out=kd[:], in0=km[:], in1=Tx[:], op=ALU.mult),
                             reads=['c_km', 'c_Tx'], writes=['c_kd'])
                        S.op('pool', lambda e: e.tensor_tensor(out=bd[:], in0=kkb[:], in1=Ta[:], op=ALU.mult),
                             reads=['c_kkb', 'c_Ta'], writes=['c_bd'])
                        if dr == 0:
                            S.op('pool', lambda e: e.tensor_copy(out=ksum[:], in_=kd[:]), reads=['c_kd'],
                                 writes=['c_ksum'])
                        else:
                            S.op('pool', lambda e: e.tensor_tensor(out=ksum[:], in0=ksum[:], in1=kd[:], op=ALU.add),
                                 reads=['c_kd', 'c_ksum'], writes=['c_ksum'])
                        S.op('dve', lambda e: e.tensor_tensor_scan(out=TG[:], data0=rst[:], data1=Ts[:], initial=0.0,
                                                                   op0=ALU.mult, op1=ALU.add),
                             reads=['c_rst', 'c_Ts'], writes=['c_TG'])
                        if dr == 1:
                            S.op('act', lambda e: e.activation(out=cend[:], in_=v3(TG)[:, :, 63], func=AF.Copy),
                                 reads=['c_TG'], writes=['c_cend'])
                            S.op('dve', lambda e: e.tensor_tensor(out=Tx[:], in0=Ts[:], in1=TG[:], op=ALU.subtract),
                                 reads=['c_Ts', 'c_TG'], writes=['c_Tx'])
                            S.op('dve', lambda e: e.tensor_tensor(
                                out=v3(TG), in0=v3(Tx), in1=cend[:].unsqueeze(2).to_broadcast([128, NCH, 64]),
                                op=ALU.add), reads=['c_Tx', 'c_cend'], writes=['c_TG'])
                        S.op('act', lambda e: e.activation(out=gam[:], in_=v3(TG)[:, :, ie], func=AF.Exp, scale=-K_),
                             reads=['c_TG'], writes=['c_gam'])
                        S.op('dve', lambda e: e.tensor_tensor(out=Tx[:], in0=TG[:], in1=Ts[:], op=ALU.subtract),
                             reads=['c_TG', 'c_Ts'], writes=['c_Tx'])
                        S.op('act', lambda e: e.activation(out=Tx[:], in_=Tx[:], func=AF.Exp, scale=-K_),
                             reads=['c_Tx'], writes=['c_Tx'])
                        t128 = lambda t: t[:].rearrange("p (n i) -> p n i", i=128)
                        S.op('dve', lambda e: e.tensor_tensor(out=ARt[:, :, 0, :], in0=t128(kkb), in1=t128(Tx),
                                                              op=ALU.mult), reads=['c_kkb', 'c_Tx'],
                             writes=['c_ARt'])
                        S.op('act', lambda e: e.activation(out=Tx[:], in_=TG[:], func=AF.Exp, scale=-K_),
                             reads=['c_TG'], writes=['c_Tx'])
                        S.op('pool', lambda e: e.tensor_tensor(out=ARt[:, :, 1, :], in0=t128(rm), in1=t128(Tx),
                                                               op=ALU.mult), reads=['c_rm', 'c_Tx'],
                             writes=['c_ARt'])
                        S.op('act', lambda e: e.activation(out=Tx[:], in_=TG[:], func=AF.Exp, scale=K_),
                             reads=['c_TG'], writes=['c_Tx'])
                        for hh in range(2):
                            pr = slice(hh * 64, (hh + 1) * 64)
                            S.op('dve', lambda e, hh=hh, pr=pr: e.tensor_tensor(out=btz[pr, hh, :], in0=bd[pr, :],
                                                                                in1=Tx[pr, :], op=ALU.mult),
                                 reads=['c_bd', 'c_Tx'], writes=['c_btz'])
                            S.op('pool', lambda e, hh=hh, pr=pr: e.tensor_tensor(out=ktz[pr, hh, :], in0=kd[pr, :],
                                                                                 in1=Tx[pr, :], op=ALU.mult),
                                 reads=['c_kd', 'c_Tx'], writes=['c_ktz'])
                        S.op('dve', lambda e: e.tensor_tensor(
                            out=v3(Tx), in0=v3(TG)[:, :, ie:ie + 1].to_broadcast([128, NCH, 64]), in1=v3(TG),
                            op=ALU.subtract), reads=['c_TG'], writes=['c_Tx'])
                        S.op('act', lambda e: e.activation(out=Tx[:], in_=Tx[:], func=AF.Exp, scale=-K_),
                             reads=['c_Tx'], writes=['c_Tx'])
                        S.op('dve', lambda e: e.scalar_tensor_tensor(out=bhn[:], in0=bd[:], scalar=-1.0, in1=Tx[:],
                                                                     op0=ALU.mult, op1=ALU.mult),
                             reads=['c_bd', 'c_Tx'], writes=['c_bhn'])
                        S.op('pool', lambda e: e.tensor_tensor(out=khh[:], in0=kd[:], in1=Tx[:], op=ALU.mult),
                             reads=['c_kd', 'c_Tx'], writes=['c_kh'])
                        if BSTOP <= 1:
                            continue
                        S.op('pool', lambda e: e.memset(Zst[:], 0.0), writes=['c_Z'])
                        S.op('pool', lambda e: e.memset(Zbd[:], 0.0), writes=['c_Zbd'])
                        tiles = list(range(NT128)) if dr == 0 else [1, 0] + list(range(NT128 - 1, 1, -1))
                        YB = [7, 6]
                        nb6 = lambda: S.next_bank6()

                        def st_T(n_, w):
                            tc = slice(n_ * 128, (n_ + 1) * 128)
                            bt_ = nb6()
                            pbf = self.ps[:, bt_, :].bitcast(BF16)
                            for qi, (src, skey) in enumerate([(ARt[:, n_, 0, :], 'c_ARt'), (bhn[:, tc], 'c_bhn'),
                                                              (khh[:, tc], 'c_kh'), (vm[:, tc], 'c_vm')]):
                                S.op('pe', lambda e, qi=qi, src=src: e.transpose(
                                    out=pbf[:, qi * 128:(qi + 1) * 128], in_=src, identity=self.ident),
                                    reads=[skey, 'cbf'], writes=[('ps', bt_)])
                            S.op('act', lambda e: e.activation(
                                out=W2[:, w, 0, :, 0:64], in_=pbf[:, 0:128].rearrange("p (h k) -> p h k", h=2),
                                func=AF.Copy), reads=[('ps', bt_)], writes=[('c_W2', w, 0)])
                            for cc in range(2):
                                pr = slice(cc * 64, (cc + 1) * 64)
                                S.op('dve', lambda e, cc=cc, pr=pr: e.tensor_copy(
                                    out=cap(bhnbd[pr, w], cc * 128, [[320, 2], [1, 64]]),
                                    in_=pbf[pr, 128:256].rearrange("p (h k) -> p h k", h=2)),
                                    reads=[('ps', bt_)], writes=[('c_bhnbd', w)])
                                S.op('act', lambda e, cc=cc, pr=pr: e.activation(
                                    out=cap(khbd[pr, w], cc * 128, [[320, 2], [1, 64]]),
                                    in_=pbf[pr, 256:384].rearrange("p (h k) -> p h k", h=2), func=AF.Copy),
                                    reads=[('ps', bt_)], writes=[('c_khbd', w)])
                            S.op('dve', lambda e: e.tensor_copy(
                                out=cap(Vpad[:, w], 0, [[192, 2], [1, 64]]),
                                in_=pbf[:, 384:512].rearrange("p (h k) -> p h k", h=2)),
                                reads=[('ps', bt_)], writes=[('c_Vpad', w)])

                        def st_S1(n_, w):
                            tc = slice(n_ * 128, (n_ + 1) * 128)
                            bA, bB, bC = nb6(), nb6(), nb6()
                            ar = ARt[:, n_, :, :].rearrange("p a b -> p (a b)")
                            for hh in range(2):
                                S.op('pe', lambda e, hh=hh: e.matmul(self.ps[:, bA, hh * 256:(hh + 1) * 256],
                                                                     lhsT=btz[:, hh, tc], rhs=ar, start=True, stop=True),
                                     reads=['c_btz', 'c_ARt'], writes=[('ps', bA)])
                            for hh in range(2):
                                S.op('pe', lambda e, hh=hh: e.matmul(self.ps[:, bB, hh * 256:(hh + 1) * 256],
                                                                     lhsT=ktz[:, hh, tc], rhs=ar, start=True, stop=True),
                                     reads=['c_ktz', 'c_ARt'], writes=[('ps', bB)])
                            for hh in range(2):
                                S.op('pe', lambda e, hh=hh: e.matmul(self.ps[:, bC, hh * 128:(hh + 1) * 128],
                                                                     lhsT=ARt[:, n_, 0, :], rhs=btz[:, hh, tc],
                                                                     start=True, stop=True),
                                     reads=['c_btz', 'c_ARt'], writes=[('ps', bC)])
                            S.op('dve', lambda e: e.tensor_tensor(
                                out=MA[:, w], in0=self.ps[:, bA, :].rearrange("p (h c) -> p h c", h=2),
                                in1=rmk[:, dr, 0:256].unsqueeze(1).to_broadcast([128, 2, 256]), op=ALU.mult),
                                reads=[('ps', bA), 'c_rmk'], writes=[('c_MA', w)])
                            S.op('dve', lambda e: e.tensor_tensor(
                                out=MB[:, w], in0=self.ps[:, bB, :].rearrange("p (h c) -> p h c", h=2),
                                in1=rmk[:, dr, 256:512].unsqueeze(1).to_broadcast([128, 2, 256]), op=ALU.mult),
                                reads=[('ps', bB), 'c_rmk'], writes=[('c_MB', w)])
                            S.op('dve', lambda e: e.tensor_tensor(
                                out=NNb[:, w], in0=self.ps[:, bC, 0:256].rearrange("p (h c) -> p h c", h=2),
                                in1=rmk[:, dr, 512:640].unsqueeze(1).to_broadcast([128, 2, 128]), op=ALU.mult),
                                reads=[('ps', bC), 'c_rmk'], writes=[('c_NN', w)])

                        def st_S2(n_, w):
                            bL = nb6()
                            for hh in range(2):
                                S.op('pe', lambda e, hh=hh: e.matmul(
                                    self.ps[:, bL, 0:128], lhsT=MB[:, w, hh, 0:128], rhs=Vpad[:, w, hh, :],
                                    start=(hh == 0), stop=(hh == 1)),
                                    reads=[('c_MB', w), ('c_Vpad', w)], writes=[('ps', bL)])
                            S.op('act', lambda e: e.activation(
                                out=W2[:, w, 0, :, 64:128],
                                in_=self.ps[:, bL, 0:128].rearrange("p (h k) -> p h k", h=2),
                                func=AF.Copy), reads=[('ps', bL)], writes=[('c_W2', w, 0)])

                        def st_S3(n_, w, lv):
                            wi_, wo_ = lv % 2, (lv + 1) % 2
                            if lv == 0:
                                Xh = lambda hh: NNb[:, w, hh, :]
                                XTh = lambda hh: MA[:, w, hh, 0:128]
                                xkeys = [('c_NN', w), ('c_MA', w)]
                            else:
                                xi = lv % 2
                                Xh = lambda hh, xi=xi: XX[:, w, xi, hh, 0, :]
                                XTh = lambda hh, xi=xi: XX[:, w, xi, hh, 1, :]
                                xkeys = [('c_XX', w, xi)]
                            bW = nb6()
                            for hh in range(2):
                                S.op('pe', lambda e, hh=hh: e.matmul(
                                    self.ps[:, bW, hh * 128:(hh + 1) * 128], lhsT=XTh(hh), rhs=W2[:, w, wi_, hh, :],
                                    start=True, stop=True), reads=xkeys + [('c_W2', w, wi_)], writes=[('ps', bW)])
                            if lv < 5:
                                S.op('dve', lambda e: e.tensor_tensor(
                                    out=W2[:, w, wo_, :, :],
                                    in0=self.ps[:, bW, 0:256].rearrange("p (h c) -> p h c", h=2),
                                    in1=W2[:, w, wi_, :, :], op=ALU.add), reads=[('ps', bW), ('c_W2', w, wi_)],
                                    writes=[('c_W2', w, wo_)])
                                bX = nb6()
                                xo_ = (lv + 1) % 2
                                for hh in range(2):
                                    if lv < 4:
                                        S.op('pe', lambda e, hh=hh: e.matmul(
                                            self.ps[:, bX, hh * 256:hh * 256 + 128], lhsT=XTh(hh), rhs=Xh(hh),
                                            start=True, stop=True), reads=xkeys, writes=[('ps', bX)])
                                    S.op('pe', lambda e, hh=hh: e.matmul(
                                        self.ps[:, bX, hh * 256 + 128:hh * 256 + 256], lhsT=Xh(hh), rhs=XTh(hh),
                                        start=True, stop=True), reads=xkeys, writes=[('ps', bX)])
                                if lv < 4:
                                    S.op('act', lambda e: e.activation(
                                        out=XX[:, w, xo_].rearrange("p h x c -> p (h x c)"), in_=self.ps[:, bX, :],
                                        func=AF.Copy), reads=[('ps', bX)], writes=[('c_XX', w, xo_)])
                                else:
                                    S.op('act', lambda e: e.activation(
                                        out=XX[:, w, xo_, :, 1, :],
                                        in_=self.ps[:, bX, :].rearrange("p (h x c) -> p h x c", h=2, x=2)[:, :, 1, :],
                                        func=AF.Copy), reads=[('ps', bX)], writes=[('c_XX', w, xo_)])
                            else:
                                for hh in range(2):
                                    S.op('dve', lambda e, hh=hh: e.tensor_tensor(
                                        out=cap(PQ[:, w], hh * 128 + hh * 64, [[256, 2], [1, 64]]),
                                        in0=self.ps[:, bW, hh * 128:(hh + 1) * 128].rearrange(
                                            "p (q k) -> p q k", q=2),
                                        in1=W2[:, w, wi_, hh, :].rearrange("p (q k) -> p q k", q=2), op=ALU.add),
                                        reads=[('ps', bW), ('c_W2', w, wi_)], writes=[('c_PQ', w)])

                        def st_S4(n_, w):
                            bR = nb6()
                            for hh in range(2):
                                S.op('pe', lambda e, hh=hh: e.matmul(self.ps[:, bR, 0:128], lhsT=PQ[:, w, 0, hh, :],
                                                                     rhs=MA[:, w, hh, 128:256], start=(hh == 0),
                                                                     stop=(hh == 1)),
                                     reads=[('c_PQ', w), ('c_MA', w)], writes=[('ps', bR)])
                            S.op('dve', lambda e: e.tensor_tensor(out=RpT[:, w, :], in0=self.ps[:, bR, 0:128],
                                                                  in1=ARt[:, n_, 1, :], op=ALU.add),
                                 reads=[('ps', bR), 'c_ARt'], writes=[('c_RpT', w)])
                            bG = nb6()
                            for hh in range(2):
                                S.op('pe', lambda e, hh=hh: e.matmul(
                                    self.ps[:, bG, 0:256], lhsT=PQ[:, w, 0, hh, :],
                                    rhs=bhnbd[:, w, hh, :, :].rearrange("p a b -> p (a b)"), start=(hh == 0),
                                    stop=(hh == 1)), reads=[('c_PQ', w), ('c_bhnbd', w)], writes=[('ps', bG)])
                            S.op('act', lambda e: e.activation(
                                out=GmT[:, w].rearrange("p a b -> p (a b)"), in_=self.ps[:, bG, 0:256], func=AF.Copy),
                                reads=[('ps', bG)], writes=[('c_GmT', w)])
                            yb = YB[w]
                            for hh in range(2):
                                S.op('pe', lambda e, hh=hh: e.matmul(self.ps[:, yb, 0:128], lhsT=Vpad[:, w, hh, :],
                                                                     rhs=MB[:, w, hh, 128:256], start=(hh == 0),
                                                                     stop=False),
                                     reads=[('c_Vpad', w), ('c_MB', w)], writes=[('ps', yb)])
                                S.op('pe', lambda e, hh=hh: e.matmul(self.ps[:, yb, 0:128], lhsT=PQ[:, w, 1, hh, :],
                                                                     rhs=MA[:, w, hh, 128:256], start=False, stop=False),
                                     reads=[('c_PQ', w), ('c_MA', w)], writes=[('ps', yb)])

                        def st_chain(n_, w):
                            tc = slice(n_ * 128, (n_ + 1) * 128)
                            yb = YB[w]
                            ccs = [0, 1] if dr == 0 else [1, 0]
                            for ci, cc in enumerate(ccs):
                                c = 2 * n_ + cc
                                S.op('pe', lambda e, cc=cc, ci=ci: e.matmul(
                                    self.ps[:, yb, cc * 64:(cc + 1) * 64], lhsT=Zbd[:],
                                    rhs=RpT[:, w, cc * 64:(cc + 1) * 64],
                                    start=False, stop=(ci == 1)), reads=['c_Zbd', ('c_RpT', w)], writes=[('ps', yb)])
                                bZ = nb6()
                                for hh in range(2):
                                    S.op('pe', lambda e, hh=hh, cc=cc: e.matmul(
                                        self.ps[:, bZ, 0:128], lhsT=khbd[:, w, hh, cc, :], rhs=Vpad[:, w, hh, :],
                                        start=(hh == 0), stop=False), reads=[('c_khbd', w), ('c_Vpad', w)],
                                        writes=[('ps', bZ)])
                                    S.op('pe', lambda e, hh=hh, cc=cc: e.matmul(
                                        self.ps[:, bZ, 0:128], lhsT=bhnbd[:, w, hh, cc, :], rhs=PQ[:, w, 1, hh, :],
                                        start=False, stop=False), reads=[('c_bhnbd', w), ('c_PQ', w)],
                                        writes=[('ps', bZ)])
                                S.op('pe', lambda e, cc=cc: e.matmul(self.ps[:, bZ, 0:128], lhsT=GmT[:, w, cc, :],
                                                                     rhs=Zbd[:], start=False, stop=True),
                                     reads=[('c_GmT', w), 'c_Zbd'], writes=[('ps', bZ)])
                                S.op('dve', lambda e, c=c: e.scalar_tensor_tensor(
                                    out=Zst[:], in0=Zst[:], scalar=gam[:, c:c + 1], in1=self.ps[:, bZ, 0:128],
                                    op0=ALU.mult, op1=ALU.add), reads=['c_Z', 'c_gam', ('ps', bZ)], writes=['c_Z'])
                                S.op('act', lambda e: e.activation(out=Zbd[:], in_=Zst[:], func=AF.Copy),
                                     reads=['c_Z'], writes=['c_Zbd'])
                            if dr == 0:
                                S.op('act', lambda e: e.activation(out=yacc[:, tc], in_=self.ps[:, yb, 0:128],
                                                                   func=AF.Copy), reads=[('ps', yb)],
                                     writes=['c_yacc'])
                            else:
                                S.op('dve', lambda e: e.tensor_tensor(out=yacc[:, tc], in0=yacc[:, tc],
                                                                      in1=self.ps[:, yb, 0:128], op=ALU.add),
                                     reads=[('ps', yb), 'c_yacc'], writes=['c_yacc'])

                        for i_ in range(0, len(tiles), 2):
                            pair = tiles[i_:i_ + 2]
                            for w, n_ in enumerate(pair):
                                st_T(n_, w)
                            for w, n_ in enumerate(pair):
                                st_S1(n_, w)
                            for w, n_ in enumerate(pair):
                                st_S2(n_, w)
                            for lv in range(6):
                                for w, n_ in enumerate(pair):
                                    st_S3(n_, w, lv)
                            for w, n_ in enumerate(pair):
                                st_S4(n_, w)
                            for w, n_ in enumerate(pair):
                                st_chain(n_, w)
                    if BSTOP <= 8:
                        continue
                    for ti, (t0, n) in enumerate(TILES):
                        ts_ = slice(t0, t0 + n)
                        S.op('act', lambda e: e.activation(out=e_bf[:, :n], in_=yacc[:, ts_], func=AF.Copy),
                             reads=['c_yacc'], writes=['c_ebf'])
                        bk = S.next_bank7()
                        S.op('pe', lambda e: e.matmul(self.ps[:, bk, :n], lhsT=self.bones, rhs=e_bf[:, :n], start=True,
                                                      stop=True), reads=['c_ebf', 'cbf'], writes=[('ps', bk)])
                        S.op('dve', lambda e: e.scalar_tensor_tensor(out=e_yc[:, :n], in0=self.ps[:, bk, :n],
                                                                     scalar=-1.0 / 64, in1=yacc[:, ts_], op0=ALU.mult,
                                                                     op1=ALU.add), reads=[('ps', bk), 'c_yacc'],
                             writes=['c_eyc'])
                        S.op('act', lambda e: e.activation(out=e_bf[:, :n], in_=e_yc[:, :n], func=AF.Square),
                             reads=['c_eyc'], writes=['c_ebf'])
                        bk2 = S.next_bank7()
                        S.op('pe', lambda e: e.matmul(self.ps[:, bk2, :n], lhsT=self.bones, rhs=e_bf[:, :n], start=True,
                                                      stop=True), reads=['c_ebf', 'cbf'], writes=[('ps', bk2)])
                        S.op('act', lambda e: e.activation(out=e_rstd[:, :n], in_=self.ps[:, bk2, :n], func=AF.Sqrt,
                                                           scale=1.0 / 64, bias=sm[:, SM['gneps']:SM['gneps'] + 1]),
                             reads=[('ps', bk2), 'smalls'], writes=['c_erstd'])
                        S.op('dve', lambda e: e.reciprocal(out=e_rstd[:, :n], in_=e_rstd[:, :n]), reads=['c_erstd'],
                             writes=['c_erstd'])
                        S.op('dve', lambda e: e.tensor_tensor(out=e_yc[:, :n], in0=e_yc[:, :n], in1=e_rstd[:, :n],
                                                              op=ALU.mult), reads=['c_eyc', 'c_erstd'],
                             writes=['c_eyc'])
                        S.op('pool', lambda e: e.tensor_scalar(out=e_yc[:, :n], in0=e_yc[:, :n],
                                                               scalar1=sm[:, SM['gng'] + hp:SM['gng'] + hp + 1],
                                                               scalar2=sm[:, SM['gnb'] + hp:SM['gnb'] + hp + 1],
                                                               op0=ALU.mult, op1=ALU.add), reads=['c_eyc', 'smalls'],
                             writes=['c_eyc'])
                        S.op('dve', lambda e: e.scalar_tensor_tensor(out=e_bf[:, :n], in0=rm[:, ts_],
                                                                     scalar=sm[:, SM['rk'] + hp:SM['rk'] + hp + 1],
                                                                     in1=ksum[:, ts_], op0=ALU.mult, op1=ALU.mult),
                             reads=['c_rm', 'c_ksum', 'smalls', 'c_ebf'], writes=['c_ebf'])
                        bk3 = S.next_bank7()
                        S.op('pe', lambda e: e.matmul(self.ps[:, bk3, :n], lhsT=self.bones, rhs=e_bf[:, :n], start=True,
                                                      stop=True), reads=['c_ebf', 'cbf'], writes=[('ps', bk3)])
                        S.op('dve', lambda e: e.tensor_tensor(out=e_t[:, :n], in0=self.ps[:, bk3, :n], in1=vm[:, ts_],
                                                              op=ALU.mult), reads=[('ps', bk3), 'c_vm'],
                             writes=['c_et'])
                        S.op('pool', lambda e: e.tensor_tensor(out=e_t[:, :n], in0=e_t[:, :n], in1=e_yc[:, :n],
                                                               op=ALU.add), reads=['c_et', 'c_eyc'], writes=['c_et'])
                        bk4 = S.next_bank7()
                        S.op('pe', lambda e: e.matmul(self.ps[:, bk4, :n], lhsT=g2b[:, hcols], rhs=sgd[:, ts_],
                                                      start=True, stop=True), reads=['c_g2', 'c_sgd'],
                             writes=[('ps', bk4)])
                        sl = nout % 2
                        nout += 1
                        S.op('dve', lambda e, sl=sl: e.tensor_tensor(out=e_out[:, sl, :n], in0=e_t[:, :n],
                                                                     in1=self.ps[:, bk4, :n], op=ALU.mult),
                             reads=['c_et', ('ps', bk4)], writes=[('c_eout', sl)])
                        S.dma('sp', d['orw'][b, hcols, t0:t0 + n], e_out[:, sl, :n], reads=[('c_eout', sl)],
                              writes=[('dram', 'orw', b, ti, hp)])

    def phase_d(self):
        S, nc, d = self.S, self.nc, self.d
        with ExitStack() as ph:
            pa = self.sb(ph, "d_pa", [128, 4, D], BF16)
            pb = self.sb(ph, "d_pb", [128, 4, D], BF16)
            wo = self.sb(ph, "d_wo", [128, 8, D], BF16)
            self.load_w_bf(pa, d['proj_a'], 4, 'd_pa')
            self.load_w_bf(pb, d['proj_b'], 4, 'd_pb')
            self.load_w_bf(wo, d['w_out'], 8, 'd_wo')
            xts = self.sb(ph, "d_x", [128, 2, 8, 512], F32)
            oh = self.sb(ph, "d_oh", [128, 2, 4, 512], BF16)
            orr = self.sb(ph, "d_or", [128, 2, 4, 512], BF16)
            gt = self.sb(ph, "d_gt", [128, 2, 16, 512], BF16)
            t1 = self.sb(ph, "d_t1", [128, 2, 512], F32)
            t2 = self.sb(ph, "d_t2", [128, 2, 512], F32)
            m = self.sb(ph, "d_m", [128, 8, 512], BF16)
            ysb = self.sb(ph, "d_ysb", [128, 8, 512], F32)
            sq = self.sb(ph, "d_sq", [128, 8, 512], BF16)
            rstd = self.sb(ph, "d_rstd", [128, 512], F32)
            tiles = self.seq_tiles(skip_ctx=self.last)
            for it, (b, ti, t0, n, j) in enumerate(tiles):
                xb = it % 2
                xkey = ('d_x', xb)
                S.dma('sp', xts[:, xb, :, :n], d['xT'][b, :, t0:t0 + n].rearrange("(kc p) t -> p kc t", p=128),
                      reads=[('dram', 'xT', b, ti)], writes=[xkey])
                S.dma('sp', oh[:, xb, :, :n], d['ohg'][b, :, t0:t0 + n].rearrange("(c p) t -> p c t", p=128),
                      reads=[('dram', 'ohg', b, ti, hh) for hh in range(4)], writes=[('d_oh', xb)])
                S.dma('sp', orr[:, xb, :, :n], d['orw'][b, :, t0:t0 + n].rearrange("(c p) t -> p c t", p=128),
                      reads=[('dram', 'orw', b, ti, hh) for hh in range(4)], writes=[('d_or', xb)])
                S.dma('pool', gt[:, xb, :, :n], d['gat'][b, :, t0:t0 + n].rearrange("(c p) t -> p c t", p=128),
                      reads=[('dram', 'gat', b, ti)], writes=[('d_gt', xb)])
                for cb in range(8):
                    ba = S.next_bank()
                    for k in range(4):
                        S.op('pe', lambda e, k=k, cb=cb, ba=ba: e.matmul(
                            self.ps[:, ba, :n], lhsT=pa[:, k, cb * 128:(cb + 1) * 128], rhs=oh[:, xb, k, :n],
                            start=(k == 0), stop=(k == 3)), reads=[('d_pa', k), ('d_oh', xb)], writes=[('ps', ba)])
                    bb = S.next_bank()
                    for k in range(4):
                        S.op('pe', lambda e, k=k, cb=cb, bb=bb: e.matmul(
                            self.ps[:, bb, :n], lhsT=pb[:, k, cb * 128:(cb + 1) * 128], rhs=orr[:, xb, k, :n],
                            start=(k == 0), stop=(k == 3)), reads=[('d_pb', k), ('d_or', xb)], writes=[('ps', bb)])
                    tb = cb % 2
                    S.op('dve', lambda e, cb=cb, ba=ba, tb=tb: e.tensor_tensor(
                        out=t1[:, tb, :n], in0=self.ps[:, ba, :n], in1=gt[:, xb, cb, :n], op=ALU.mult),
                        reads=[('ps', ba), ('d_gt', xb)], writes=[('d_t1', tb)])
                    S.op('dve', lambda e, cb=cb, bb=bb, tb=tb: e.tensor_tensor(
                        out=t2[:, tb, :n], in0=self.ps[:, bb, :n], in1=gt[:, xb, 8 + cb, :n], op=ALU.mult),
                        reads=[('ps', bb), ('d_gt', xb)], writes=[('d_t2', tb)])
                    S.op('pool', lambda e, cb=cb, tb=tb: e.tensor_tensor(
                        out=m[:, cb, :n], in0=t1[:, tb, :n], in1=t2[:, tb, :n], op=ALU.add),
                        reads=[('d_t1', tb), ('d_t2', tb)], writes=[('d_m', cb)])
                for cb in range(8):
                    bank = S.next_bank()
                    for kc in range(8):
                        S.op('pe', lambda e, kc=kc, cb=cb, bank=bank: e.matmul(
                            self.ps[:, bank, :n], lhsT=wo[:, kc, cb * 128:(cb + 1) * 128], rhs=m[:, kc, :n],
                            start=(kc == 0), stop=(kc == 7)), reads=[('d_wo', kc), ('d_m', kc)],
                            writes=[('ps', bank)])
                    S.op('act', lambda e, cb=cb, bank=bank: e.activation(out=ysb[:, cb, :n], in_=self.ps[:, bank, :n],
                                                                         func=AF.Copy),
                         reads=[('ps', bank)], writes=['d_ysb'])
                    S.op('act', lambda e, cb=cb, bank=bank: e.activation(out=sq[:, cb, :n], in_=self.ps[:, bank, :n],
                                                                         func=AF.Square),
                         reads=[('ps', bank)], writes=['d_sq'])
                self.post_norm_resid('d_', ysb, sq, rstd, xts[:, xb], xkey, n, self.G1, 'G1', j)
                S.dma('sp', d['xo'][b, :, t0:t0 + n].rearrange("(kc p) t -> p kc t", p=128), xts[:, xb, :, :n],
                      reads=[xkey], writes=[('dram', 'xo', b, ti)])

    def phase_e1(self):
        S, nc, d = self.S, self.nc, self.d
        with ExitStack() as ph:
            w1 = self.sb(ph, "e_w1", [128, 8, 5632], BF16)
            self.load_w_bf(w1, d['ffn_w1'], 8, 'e_w1')
            xts = self.sb(ph, "e_x", [128, 2, 8, 512], F32)
            sq = self.sb(ph, "e_sq", [128, 8, 512], BF16)
            rstd = self.sb(ph, "e_rstd", [128, 512], F32)
            hs = self.sb(ph, "e_h", [128, 2, 8, 512], BF16)
            sg = self.sb(ph, "e_sg", [128, 2, 512], F32)
            us = self.sb(ph, "e_us", [128, 2, 11, 512], BF16)
            tiles = self.seq_tiles(skip_ctx=self.last)
            nu = 0
            for it, (b, ti, t0, n, j) in enumerate(tiles):
                xb = it % 2
                xkey = ('e_x', xb)
                hkey = ('e_h', xb)
                S.dma('sp', xts[:, xb, :, :n], d['xo'][b, :, t0:t0 + n].rearrange("(kc p) t -> p kc t", p=128),
                      reads=[('dram', 'xo', b, ti)], writes=[xkey])
                self.rms_mod_tile(ph, 'e_', xts[:, xb], [xkey], n, self.A2, self.mod[:, 24:32, :], 'A2', 'mod', j,
                                  (sq, rstd, hs[:, xb], hkey))
                h = hs[:, xb]
                for half in range(2):
                    sl = nu % 2
                    nu += 1
                    for fi in range(11):
                        fb = half * 11 + fi
                        bg = S.next_bank()
                        for kc in range(8):
                            S.op('pe', lambda e, kc=kc, fb=fb, bg=bg: e.matmul(
                                self.ps[:, bg, :n], lhsT=w1[:, kc, fb * 128:(fb + 1) * 128], rhs=h[:, kc, :n],
                                start=(kc == 0), stop=(kc == 7)), reads=[('e_w1', kc), hkey], writes=[('ps', bg)])
                        bu = S.next_bank()
                        for kc in range(8):
                            S.op('pe', lambda e, kc=kc, fb=fb, bu=bu: e.matmul(
                                self.ps[:, bu, :n], lhsT=w1[:, kc, 2816 + fb * 128:2816 + (fb + 1) * 128],
                                rhs=h[:, kc, :n], start=(kc == 0), stop=(kc == 7)),
                                reads=[('e_w1', kc), hkey], writes=[('ps', bu)])
                        gb = fb % 2
                        S.op('act', lambda e, bg=bg, gb=gb: e.activation(out=sg[:, gb, :n], in_=self.ps[:, bg, :n],
                                                                         func=AF.Silu),
                             reads=[('ps', bg)], writes=[('e_sg', gb)])
                        S.op('dve', lambda e, bu=bu, gb=gb, fi=fi, sl=sl: e.tensor_tensor(
                            out=us[:, sl, fi, :n], in0=sg[:, gb, :n], in1=self.ps[:, bu, :n], op=ALU.mult),
                            reads=[('ps', bu), ('e_sg', gb)], writes=[('e_us', sl)])
                    S.dma('sp', d['usc'][b, half * 1408:(half + 1) * 1408, t0:t0 + n].rearrange(
                        "(c p) t -> p c t", p=128), us[:, sl, :, :n], reads=[('e_us', sl)],
                        writes=[('dram', 'usc', b, ti)])

    def phase_e2(self):
        S, nc, d = self.S, self.nc, self.d
        with ExitStack() as ph:
            w2 = self.sb(ph, "f_w2", [128, 22, D], BF16)
            self.load_w_bf(w2, d['ffn_w2'], 22, 'f_w2')
            xts = self.sb(ph, "f_x", [128, 2, 8, 512], F32)
            us = self.sb(ph, "f_us", [128, 2, 22, 512], BF16)
            ysb = self.sb(ph, "f_ysb", [128, 8, 512], F32)
            sq = self.sb(ph, "f_sq", [128, 8, 512], BF16)
            rstd = self.sb(ph, "f_rstd", [128, 512], F32)
            tiles = self.seq_tiles(skip_ctx=self.last)
            for it, (b, ti, t0, n, j) in enumerate(tiles):
                xb = it % 2
                xkey = ('f_x', xb)
                S.dma('sp', xts[:, xb, :, :n], d['xo'][b, :, t0:t0 + n].rearrange("(kc p) t -> p kc t", p=128),
                      reads=[('dram', 'xo', b, ti)], writes=[xkey])
                for half in range(2):
                    S.dma('pool' if half else 'sp', us[:, xb, half * 11:(half + 1) * 11, :n],
                          d['usc'][b, half * 1408:(half + 1) * 1408, t0:t0 + n].rearrange("(c p) t -> p c t", p=128),
                          reads=[('dram', 'usc', b, ti)], writes=[('f_us', xb, half)])
                for cb in range(8):
                    bank = S.next_bank()
                    for k in range(22):
                        S.op('pe', lambda e, k=k, cb=cb, bank=bank: e.matmul(
                            self.ps[:, bank, :n], lhsT=w2[:, k, cb * 128:(cb + 1) * 128], rhs=us[:, xb, k, :n],
                            start=(k == 0), stop=(k == 21)), reads=[('f_w2', k), ('f_us', xb, k // 11)],
                            writes=[('ps', bank)])
                    S.op('act', lambda e, cb=cb, bank=bank: e.activation(out=ysb[:, cb, :n], in_=self.ps[:, bank, :n],
                                                                         func=AF.Copy),
                         reads=[('ps', bank)], writes=['f_ysb'])
                    S.op('act', lambda e, cb=cb, bank=bank: e.activation(out=sq[:, cb, :n], in_=self.ps[:, bank, :n],
                                                                         func=AF.Square),
                         reads=[('ps', bank)], writes=['f_sq'])
                self.post_norm_resid('f_', ysb, sq, rstd, xts[:, xb], xkey, n, self.G2, 'G2', j)
                S.dma('sp', d['xo'][b, :, t0:t0 + n].rearrange("(kc p) t -> p kc t", p=128), xts[:, xb, :, :n],
                      reads=[xkey], writes=[('dram', 'xo', b, ti)])


def _blk(v):
    v = np.asarray(v, np.float32).reshape(-1, 128)
    return np.ascontiguousarray(v.T)


def make_consts():
    ident = np.eye(128, dtype=np.float32)
    ones = np.ones((128, 128), np.float32)
    bones = np.zeros((128, 128), np.float32)
    bones[:64, :64] = 1
    bones[64:, 64:] = 1
    consts = np.ascontiguousarray(np.stack([ident, ones, bones], 1))
    j = np.arange(128)[:, None]
    i = np.arange(128)[None, :]
    same = (j // 64) == (i // 64)
    same32 = (j // HC) == (i // HC)
    hmask = np.stack([(same32 & (i >= j)), (same32 & (i <= j))], 1).astype(np.float32)
    rm = np.zeros((128, 2, 640), np.float32)
    for dr in range(2):
        strict = same & ((i > j) if dr == 0 else (i < j))
        incl = same & ((i >= j) if dr == 0 else (i <= j))
        rm[:, dr, 0:128] = -strict.astype(np.float32)
        rm[:, dr, 128:256] = -incl.astype(np.float32)
        rm[:, dr, 256:384] = strict
        rm[:, dr, 384:512] = incl
        rm[:, dr, 512:640] = -strict.T.astype(np.float32)
    rst = np.ones((128, TT), np.float32)
    rst[:, ::64] = 0
    rst32 = np.ones((128, TT), np.float32)
    rst32[:, ::HC] = 0
    return consts, np.ascontiguousarray(hmask), rm, rst, rst32


def make_smalls(inp, l):
    sm = np.zeros((128, NS), np.float32)

    def put(name, arr):
        arr = np.asarray(arr, np.float32)
        sm[:, SM[name]:SM[name] + arr.shape[1]] = arr
    put('ng', np.concatenate([_blk(inp['norm_g'][l, w]) for w in range(4)], 1))
    put('adab', _blk(inp['ada_b'][l]))
    lbl = inp['hg_lb_logits']
    put('lbl', np.concatenate([_blk(lbl[ll].reshape(-1)) for ll in range(DEPTH)], 1))
    lsel = np.zeros((128, 4), np.float32)
    lsel[:, 1:l + 1] = 1.0
    put('lsel', lsel)
    put('hgng', _blk(inp['hg_norm_g'][l]))
    put('mu', _blk(inp['rw_mu'][l]))
    put('w0', _blk(inp['rw_w0'][l].reshape(-1)))
    put('a0', _blk(inp['rw_a0'][l].reshape(-1)))
    put('kk', _blk(inp['rw_kk'][l]))
    put('ka', _blk(inp['rw_ka'][l]))
    put('gng', _blk(inp['rw_gn_g'][l]))
    put('gnb', _blk(inp['rw_gn_b'][l]))
    put('rk', _blk(inp['rw_rk'][l].reshape(-1)))
    p = np.arange(128)
    cls = np.stack([(p % 4 == 0), (p % 4 == 1), (p % 4 == 2), (p % 4 == 3), (p % 2 == 0), (p % 2 == 1)], 1)
    put('cls', cls.astype(np.float32))
    sm[:, SM['eps6']] = 1e-6
    sm[:, SM['gneps']] = 64e-5
    sm[:, SM['tiny']] = 1e-12
    return sm


def layer_inputs(inp, l, consts):
    c = {}
    c['smalls'] = make_smalls(inp, l)
    c['consts'], c['hmask'], c['rmask'], c['rst'], c['rst32'] = consts
    c['ada_w'] = np.ascontiguousarray(inp['ada_w'][l])
    c['w_in'] = np.ascontiguousarray(inp['w_in'][l])
    c['rw_w2'] = np.ascontiguousarray(inp['rw_w2'][l].reshape(128, 512))
    c['rw_a2'] = np.ascontiguousarray(inp['rw_a2'][l].reshape(128, 512))
    c['rw_g2'] = np.ascontiguousarray(inp['rw_g2'][l])
    c['proj_a'] = np.ascontiguousarray(inp['proj_a'][l])
    c['proj_b'] = np.ascontiguousarray(inp['proj_b'][l])
    c['w_out'] = np.ascontiguousarray(inp['w_out'][l])
    c['ffn_w1'] = np.ascontiguousarray(inp['ffn_w1'][l])
    c['ffn_w2'] = np.ascontiguousarray(inp['ffn_w2'][l])
    return c


def make_cT(c_rows, c_ctx):
    m = np.concatenate([c_rows, c_ctx[None, :]], 0).astype(np.float32)
    return np.ascontiguousarray(m.T.reshape(8, 128, -1).transpose(1, 0, 2))


_PROG_CACHE = {}


def get_prog(NB, last):
    key = (NB, last)
    if key not in _PROG_CACHE:
        p = Prog(NB, last_layer=last)
        p.build()
        _PROG_CACHE[key] = p
    return _PROG_CACHE[key]


def kernel(**inp):
    inp = {k: np.asarray(v) for k, v in inp.items()}
    n_cores = 8
    B = inp['x'].shape[0]
    NB = B // n_cores
    xs = np.concatenate([inp['ctx'], inp['x']], axis=1)
    xT = np.ascontiguousarray(xs.transpose(0, 2, 1)).astype(np.float32)
    consts = make_consts()
    cTs = [make_cT(inp['c'][i * NB:(i + 1) * NB], inp['c_ctx']) for i in range(n_cores)]
    for l in range(DEPTH):
        prog = get_prog(NB, False)
        com = layer_inputs(inp, l, consts)
        in_maps = []
        for i in range(n_cores):
            m = dict(com)
            m['xT'] = np.ascontiguousarray(xT[i * NB:(i + 1) * NB])
            m['cT'] = cTs[i]
            in_maps.append(m)
        res = run_bass_kernel_spmd(prog.nc, in_maps, core_ids=list(range(n_cores)))
        xT = np.concatenate([r['xo'] for r in res.results], 0)
    out = xT[:, :, NCTX:].transpose(0, 2, 1)
    return np.ascontiguousarray(out).astype(np.float32)
```

```python
import numpy as np
from contextlib import ExitStack
import concourse.bass as bass
import concourse.mybir as mybir
from concourse.bass_utils import run_bass_kernel_spmd

F32 = mybir.dt.float32
BF16 = mybir.dt.bfloat16
AF = mybir.ActivationFunctionType
ALU = mybir.AluOpType

D = 1024
TT = 2304
NCTX = 256
TILES = [(0, 256)] + [(256 + 512 * i, 512) for i in range(4)]
NCH = TT // 64
NT128 = TT // 128
HC = 32
NHC = TT // HC
DEPTH = 4
KAPPA = float(np.exp(-0.5))

SM = {}
_o = 0
for _n, _w in [("ng", 32), ("adab", 48), ("lbl", 32), ("lsel", 4), ("hgng", 1), ("mu", 15), ("w0", 8), ("a0", 8),
               ("kk", 4), ("ka", 4), ("gng", 4), ("gnb", 4), ("rk", 4), ("cls", 6), ("eps6", 1), ("gneps", 1),
               ("tiny", 1)]:
    SM[_n] = _o
    _o += _w
NS = _o


class Sched:
    def __init__(self, nc, es, n_dma_sems=8, same_sync=True):
        self.nc = nc
        self.same_sync = same_sync
        self.engs = {'pe': nc.tensor, 'act': nc.scalar, 'dve': nc.vector, 'pool': nc.gpsimd, 'sp': nc.sync}
        self.semobj = {}
        self.cnt = {}
        for e in ('pe', 'act', 'dve', 'pool'):
            self.semobj[e] = es.enter_context(nc.semaphore("sem_" + e))
            self.cnt[e] = 0
        self.dma_sems = {}
        self.dma_rr = {}
        for q in ('sp', 'pool'):
            names = []
            for j in range(n_dma_sems):
                nm = "dma_%s_%d" % (q, j)
                self.semobj[nm] = es.enter_context(nc.semaphore(nm))
                self.cnt[nm] = 0
                names.append(nm)
            self.dma_sems[q] = names
            self.dma_rr[q] = 0
        self.seen = {e: {} for e in self.engs}
        self.last_w = {}
        self.readers = {}
        self.n_inst = 0
        self.n_wait = 0
        self.bank = 0
        self.bank7 = 0
        self.bank6 = 0

    def next_bank(self):
        b = self.bank
        self.bank = (self.bank + 1) % 8
        return b

    def next_bank6(self):
        b = self.bank6
        self.bank6 = (self.bank6 + 1) % 6
        return b

    def next_bank7(self):
        b = self.bank7
        self.bank7 = (self.bank7 + 1) % 7
        return b

    def _wait(self, eng, toks):
        best = {}
        for (s, v) in toks:
            if best.get(s, 0) < v:
                best[s] = v
        for s, v in best.items():
            if s == eng and (eng == 'pe' or not self.same_sync):
                continue
            if self.seen[eng].get(s, 0) < v:
                self.engs[eng].wait_ge(self.semobj[s], v)
                self.seen[eng][s] = v
                self.n_wait += 1

    def _deps(self, reads, writes):
        toks = []
        for k in reads:
            t = self.last_w.get(k)
            if t is not None:
                toks.append(t)
        for k in writes:
            t = self.last_w.get(k)
            if t is not None:
                toks.append(t)
            r = self.readers.get(k)
            if r:
                toks.extend(r.items())
        return toks

    def _commit(self, tok, reads, writes):
        for k in writes:
            self.last_w[k] = tok
            self.readers[k] = {}
        for k in reads:
            if k in writes:
                continue
            r = self.readers.setdefault(k, {})
            if r.get(tok[0], 0) < tok[1]:
                r[tok[0]] = tok[1]

    def op(self, eng, fn, reads=(), writes=()):
        psr = [k for k in reads if isinstance(k, tuple) and k and k[0] == 'ps']
        if psr:
            reads = [k for k in reads if k not in psr]
            writes = list(writes) + psr
        self._wait(eng, self._deps(reads, writes))
        inst = fn(self.engs[eng])
        self.cnt[eng] += 1
        inst.then_inc(self.semobj[eng], 1)
        self._commit((eng, self.cnt[eng]), reads, writes)
        self.n_inst += 1
        return inst

    def dma(self, q, out, in_, reads=(), writes=(), **kw):
        toks = self._deps(reads, writes)
        j = self.dma_rr[q]
        self.dma_rr[q] = (j + 1) % len(self.dma_sems[q])
        nm = self.dma_sems[q][j]
        if self.cnt[nm] > 0:
            toks.append((nm, self.cnt[nm]))
        self._wait(q, toks)
        self.engs[q].dma_start(out=out, in_=in_, **kw).then_inc(self.semobj[nm], 16)
        self.cnt[nm] += 16
        self._commit((nm, self.cnt[nm]), reads, writes)
        self.n_inst += 1

    def _all_toks(self):
        toks = [(e, self.cnt[e]) for e in ('pe', 'act', 'dve', 'pool') if self.cnt[e] > 0]
        for q in self.dma_sems:
            toks += [(nm, self.cnt[nm]) for nm in self.dma_sems[q] if self.cnt[nm] > 0]
        return toks

    def barrier(self):
        toks = self._all_toks()
        for e in ('sp', 'pool', 'pe', 'act', 'dve'):
            self._wait(e, toks)
        self.last_w = {k: v for k, v in self.last_w.items() if isinstance(k, tuple) and k and k[0] == 'dram'}
        self.readers = {k: v for k, v in self.readers.items() if k in self.last_w}

    def barrier_soft(self, engs=('sp', 'pool', 'pe', 'act', 'dve')):
        toks = self._all_toks()
        for e in engs:
            self._wait(e, toks)

    def finish(self):
        self._wait('sp', self._all_toks())


class StopPhase(Exception):
    pass


import os
BSTOP = int(os.environ.get('BSTOP', '99'))
STGBAR = os.environ.get('STGBAR', '0') == '1'
STGENG = tuple(os.environ.get('STGENG', 'sp,pool,pe,act,dve').split(','))


class Prog:
    def __init__(self, NB, last_layer=False, dbg=False, phases="ABCDE", same_sync=True):
        self.NB = NB
        self.last = last_layer
        self.dbg = dbg
        self.phases = phases
        self.nc = nc = bass.Bass("TRN2", target_bir_lowering=False)
        self.es = ExitStack()
        self.S = Sched(nc, self.es, same_sync=same_sync)
        ext = lambda n, s, dt=F32: nc.dram_tensor(n, s, dt, kind="ExternalInput").ap()
        scr_kind = "ExternalOutput" if dbg else "Internal"
        scr = lambda n, s, dt: nc.dram_tensor(n, s, dt, kind=scr_kind).ap()
        J = NB + 1
        self.J = J
        d = self.d = {}
        d['xT'] = ext("xT", [NB, D, TT])
        d['xo'] = nc.dram_tensor("xo", [NB, D, TT], F32, kind="ExternalOutput").ap()
        d['cT'] = ext("cT", [128, 8, J])
        d['smalls'] = ext("smalls", [128, NS])
        d['consts'] = ext("consts", [128, 3, 128])
        d['hmask'] = ext("hmask", [128, 2, 128])
        d['rmask'] = ext("rmask", [128, 2, 640])
        d['rst'] = ext("rst", [128, TT])
        d['rst32'] = ext("rst32", [128, TT])
        d['ada_w'] = ext("ada_w", [D, 6 * D])
        d['w_in'] = ext("w_in", [D, 6528])
        d['rw_w2'] = ext("rw_w2", [128, 512])
        d['rw_a2'] = ext("rw_a2", [128, 512])
        d['rw_g2'] = ext("rw_g2", [128, 512])
        d['proj_a'] = ext("proj_a", [512, D])
        d['proj_b'] = ext("proj_b", [512, D])
        d['w_out'] = ext("w_out", [D, D])
        d['ffn_w1'] = ext("ffn_w1", [D, 5632])
        d['ffn_w2'] = ext("ffn_w2", [2816, D])
        d['hgq'] = scr("hgq", [NB, 512, TT], BF16)
        d['hgf'] = scr("hgf", [NB, 1024, TT], F32)
        d['hgi'] = scr("hgi", [NB, TT, 512], BF16)
        d['hgo'] = scr("hgo", [NB, 512, TT], BF16)
        d['rwp'] = scr("rwp", [NB, 1920, TT], BF16)
        d['gat'] = scr("gat", [NB, 2048, TT], BF16)
        d['ohg'] = scr("ohg", [NB, 512, TT], BF16)
        d['orw'] = scr("orw", [NB, 512, TT], BF16)
        d['usc'] = scr("usc", [NB, 2816, TT], BF16)
        self.ps = self.es.enter_context(nc.psum_tensor("ps", [128, 8, 512], F32))

    def sb(self, st, name, shape, dt):
        return st.enter_context(self.nc.sbuf_tensor("s_" + name, shape, dt))

    def build(self):
        S, nc, d = self.S, self.nc, self.d
        es = self.es
        self.prologue()
        S.barrier()
        if 'A' in self.phases:
            self.phase_a()
            S.barrier()
        if 'B' in self.phases:
            try:
                self.phase_b()
            except StopPhase:
                pass
            S.barrier()
        if 'C' in self.phases:
            self.phase_c()
            S.barrier()
        if 'D' in self.phases:
            self.phase_d()
            S.barrier()
        if 'E' in self.phases:
            self.phase_e1()
            S.barrier()
            self.phase_e2()
            S.barrier()
        S.finish()
        es.close()
        return nc

    def prologue(self):
        S, nc, d, es, J = self.S, self.nc, self.d, self.es, self.J
        sm = self.sm = self.sb(es, "smalls", [128, NS], F32)
        S.dma('sp', sm[:], d['smalls'][:, :], writes=['smalls'])
        cT = self.sb(es, "cT", [128, 8, J], F32)
        S.dma('sp', cT[:], d['cT'][:, :, :], writes=['cT'])
        self.cbf = self.sb(es, "cbf", [128, 3, 128], BF16)
        S.dma('pool', self.cbf[:], d['consts'][:, :, :], writes=['cbf'])
        self.ident = self.cbf[:, 0, :]
        self.ones = self.cbf[:, 1, :]
        self.bones = self.cbf[:, 2, :]
        self.mod = mod = self.sb(es, "mod", [128, 48, J], F32)
        self.A1 = self.sb(es, "A1", [128, 8, J], F32)
        self.G1 = self.sb(es, "G1", [128, 8, J], F32)
        self.A2 = self.sb(es, "A2", [128, 8, J], F32)
        self.G2 = self.sb(es, "G2", [128, 8, J], F32)
        self.lbv = self.sb(es, "lbv", [128, 3, 8], F32)
        self.muv = self.sb(es, "muv", [128, 7, 15], F32)
        self.kav = self.sb(es, "kav", [128, 4], F32)
        with ExitStack() as ph:
            sc = self.sb(ph, "silu_c", [128, 8, J], F32)
            S.op('act', lambda e: e.activation(out=sc[:], in_=cT[:], func=AF.Silu), reads=['cT'], writes=['silu_c'])
            aw = self.sb(ph, "adaw", [128, 2, 8, 768], F32)
            for g in range(8):
                buf = g % 2
                for kc in range(8):
                    S.dma('sp' if kc % 2 == 0 else 'pool', aw[:, buf, kc, :],
                          d['ada_w'][kc * 128:(kc + 1) * 128, g * 768:(g + 1) * 768], writes=[('adaw', buf, kc)])
                bank = S.next_bank()
                for c6 in range(6):
                    for kc in range(8):
                        S.op('pe', lambda e, c6=c6, kc=kc: e.matmul(
                            self.ps[:, bank, c6 * J:(c6 + 1) * J], lhsT=aw[:, buf, kc, c6 * 128:(c6 + 1) * 128],
                            rhs=sc[:, kc, :], start=(kc == 0), stop=(kc == 7)),
                            reads=[('adaw', buf, kc), 'silu_c'], writes=[('ps', bank)])
                S.op('dve', lambda e: e.tensor_tensor(
                    out=mod[:, g * 6:(g + 1) * 6, :],
                    in0=self.ps[:, bank, 0:6 * J].rearrange("p (c j) -> p c j", j=J),
                    in1=sm[:, SM['adab'] + g * 6:SM['adab'] + (g + 1) * 6].unsqueeze(2).to_broadcast([128, 6, J]),
                    op=ALU.add), reads=[('ps', bank), 'smalls'], writes=['mod'])
            ng = lambda w: sm[:, SM['ng'] + w * 8:SM['ng'] + (w + 1) * 8].unsqueeze(2).to_broadcast([128, 8, J])
            tmp = self.sb(ph, "ptmp", [128, 8, J], F32)
            for (dst, mi, gi, plus1, nm) in [(self.A1, 1, 0, True, 'A1'), (self.G1, 2, 1, False, 'G1'),
                                             (self.A2, 4, 2, True, 'A2'), (self.G2, 5, 3, False, 'G2')]:
                src = mod[:, mi * 8:(mi + 1) * 8, :]
                if plus1:
                    S.op('dve', lambda e, src=src: e.tensor_scalar(out=tmp[:], in0=src, scalar1=1.0, scalar2=None,
                                                                   op0=ALU.add), reads=['mod'], writes=['ptmp'])
                    src = tmp[:]
                S.op('dve', lambda e, src=src, dst=dst, gi=gi: e.tensor_tensor(out=dst[:], in0=src, in1=ng(gi),
                                                                               op=ALU.mult),
                     reads=['mod', 'ptmp', 'smalls'], writes=[nm])
            ex = self.sb(ph, "lb_ex", [128, 4, 8], F32)
            den = self.sb(ph, "lb_den", [128, 8], F32)
            num = self.sb(ph, "lb_num", [128, 8], F32)
            S.op('act', lambda e: e.activation(out=ex[:], in_=sm[:, SM['lbl']:SM['lbl'] + 32].rearrange(
                "p (l c) -> p l c", c=8), func=AF.Exp), reads=['smalls'], writes=['lb_ex'])
            S.op('dve', lambda e: e.tensor_tensor(out=den[:], in0=ex[:, 0, :], in1=ex[:, 1, :], op=ALU.add),
                 reads=['lb_ex'], writes=['lb_den'])
            for l in (2, 3):
                S.op('dve', lambda e, l=l: e.tensor_tensor(out=den[:], in0=den[:], in1=ex[:, l, :], op=ALU.add),
                     reads=['lb_ex', 'lb_den'], writes=['lb_den'])
            S.op('dve', lambda e: e.tensor_scalar(out=num[:], in0=ex[:, 0, :], scalar1=sm[:, SM['lsel']:SM['lsel'] + 1],
                                                  scalar2=None, op0=ALU.mult), reads=['lb_ex', 'smalls'],
                 writes=['lb_num'])
            for l in (1, 2, 3):
                S.op('dve', lambda e, l=l: e.scalar_tensor_tensor(
                    out=num[:], in0=ex[:, l, :], scalar=sm[:, SM['lsel'] + l:SM['lsel'] + l + 1], in1=num[:],
                    op0=ALU.mult, op1=ALU.add), reads=['lb_ex', 'lb_num', 'smalls'], writes=['lb_num'])
            S.op('dve', lambda e: e.reciprocal(out=den[:], in_=den[:]), reads=['lb_den'], writes=['lb_den'])
            S.op('dve', lambda e: e.tensor_tensor(out=self.lbv[:, 0, :], in0=num[:], in1=den[:], op=ALU.mult),
                 reads=['lb_num', 'lb_den'], writes=['lbv'])
            S.op('dve', lambda e: e.tensor_scalar(out=self.lbv[:, 1, :], in0=self.lbv[:, 0, :], scalar1=-1.0,
                                                  scalar2=1.0, op0=ALU.mult, op1=ALU.add), reads=['lbv'],
                 writes=['lbv'])
            S.op('dve', lambda e: e.tensor_scalar(out=self.lbv[:, 2, :], in0=self.lbv[:, 0, :], scalar1=1.0,
                                                  scalar2=-1.0, op0=ALU.mult, op1=ALU.add), reads=['lbv'],
                 writes=['lbv'])
            mu = sm[:, SM['mu']:SM['mu'] + 15]
            S.op('dve', lambda e: e.tensor_scalar(out=self.muv[:, 0, :], in0=mu, scalar1=-1.0, scalar2=1.0,
                                                  op0=ALU.mult, op1=ALU.add), reads=['smalls'], writes=['muv'])
            for c in range(6):
                S.op('dve', lambda e, c=c: e.tensor_scalar(out=self.muv[:, 1 + c, :], in0=mu,
                                                           scalar1=sm[:, SM['cls'] + c:SM['cls'] + c + 1],
                                                           scalar2=None, op0=ALU.mult), reads=['smalls'],
                     writes=['muv'])
            S.op('dve', lambda e: e.tensor_scalar(out=self.kav[:], in0=sm[:, SM['ka']:SM['ka'] + 4], scalar1=-1.0,
                                                  scalar2=1.0, op0=ALU.mult, op1=ALU.add), reads=['smalls'],
                 writes=['kav'])
            S.barrier()

    def rms_mod_tile(self, ph, pfx, xt, xkeys, n, A, B, Akey, Bkey, j, bufs):
        S = self.S
        sq, rstd, h, hkey = bufs
        S.op('act', lambda e: e.activation(out=sq[:, :, :n], in_=xt[:, :, :n], func=AF.Square),
             reads=xkeys, writes=[pfx + 'sq'])
        bank = S.next_bank()
        for kc in range(8):
            S.op('pe', lambda e, kc=kc: e.matmul(self.ps[:, bank, :n], lhsT=self.ones, rhs=sq[:, kc, :n],
                                                 start=(kc == 0), stop=(kc == 7)),
                 reads=[pfx + 'sq', 'cbf'], writes=[('ps', bank)])
        S.op('act', lambda e: e.activation(out=rstd[:, :n], in_=self.ps[:, bank, :n], func=AF.Sqrt,
                                           scale=1.0 / D, bias=self.sm[:, SM['eps6']:SM['eps6'] + 1]),
             reads=[('ps', bank), 'smalls'], writes=[pfx + 'rstd'])
        S.op('dve', lambda e: e.reciprocal(out=rstd[:, :n], in_=rstd[:, :n]), reads=[pfx + 'rstd'],
             writes=[pfx + 'rstd'])
        S.op('dve', lambda e: e.tensor_tensor(out=xt[:, :, :n], in0=xt[:, :, :n],
                                              in1=rstd[:, :n].unsqueeze(1).to_broadcast([128, 8, n]), op=ALU.mult),
             reads=xkeys + [pfx + 'rstd'], writes=xkeys)
        for kc in range(8):
            S.op('pool', lambda e, kc=kc: e.tensor_scalar(out=h[:, kc, :n], in0=xt[:, kc, :n],
                                                          scalar1=A[:, kc, j:j + 1], scalar2=B[:, kc, j:j + 1],
                                                          op0=ALU.mult, op1=ALU.add),
                 reads=xkeys + [Akey, Bkey], writes=[hkey])

    def post_norm_resid(self, pfx, ysb, sq, rstd, xt, xkey, n, G, Gkey, j):
        S = self.S
        bank = S.next_bank()
        for kc in range(8):
            S.op('pe', lambda e, kc=kc: e.matmul(self.ps[:, bank, :n], lhsT=self.ones, rhs=sq[:, kc, :n],
                                                 start=(kc == 0), stop=(kc == 7)),
                 reads=[pfx + 'sq', 'cbf'], writes=[('ps', bank)])
        S.op('act', lambda e: e.activation(out=rstd[:, :n], in_=self.ps[:, bank, :n], func=AF.Sqrt,
                                           scale=1.0 / D, bias=self.sm[:, SM['eps6']:SM['eps6'] + 1]),
             reads=[('ps', bank), 'smalls'], writes=[pfx + 'rstd'])
        S.op('dve', lambda e: e.reciprocal(out=rstd[:, :n], in_=rstd[:, :n]), reads=[pfx + 'rstd'],
             writes=[pfx + 'rstd'])
        for cb in range(8):
            S.op('dve', lambda e, cb=cb: e.scalar_tensor_tensor(out=ysb[:, cb, :n], in0=ysb[:, cb, :n],
                                                                scalar=G[:, cb, j:j + 1], in1=rstd[:, :n],
                                                                op0=ALU.mult, op1=ALU.mult),
                 reads=[pfx + 'ysb', pfx + 'rstd', Gkey], writes=[pfx + 'ysb'])
        S.op('pool', lambda e: e.tensor_tensor(out=xt[:, :, :n], in0=xt[:, :, :n], in1=ysb[:, :, :n], op=ALU.add),
             reads=[pfx + 'ysb', xkey], writes=[xkey])

    def seq_tiles(self, skip_ctx=False):
        out = []
        for b in range(self.NB):
            for ti, (t0, n) in enumerate(TILES):
                if skip_ctx and ti == 0:
                    continue
                out.append((b, ti, t0, n, self.NB if ti == 0 else b))
        return out

    def load_w_bf(self, dst, dram, nk, key):
        for kc in range(nk):
            self.S.dma('pool', dst[:, kc, :], dram[kc * 128:(kc + 1) * 128, :], writes=[(key, kc)])

    def phase_a(self):
        S, nc, d = self.S, self.nc, self.d
        FUNC = {}
        for cb in range(51):
            if cb < 4 or 16 <= cb < 20:
                FUNC[cb] = AF.Silu
            elif 4 <= cb < 12 or cb >= 35:
                FUNC[cb] = AF.Sigmoid
            else:
                FUNC[cb] = AF.Copy
        groups = [('hgq', 0, BF16, [0, 1, 2, 3]), ('hgf', 0, F32, [4, 5, 6, 7]), ('hgf', 512, F32, [8, 9, 10, 11]),
                  ('hgo', 0, BF16, [16, 17, 18, 19])]
        for g in range(4):
            cbs = list(range(20 + 4 * g, min(20 + 4 * g + 4, 35)))
            groups.append(('rwp', 512 * g, BF16, cbs))
        for g in range(4):
            groups.append(('gat', 512 * g, BF16, list(range(35 + 4 * g, 39 + 4 * g))))
        with ExitStack() as ph:
            wbf = self.sb(ph, "a_w", [128, 8, 6528], BF16)
            self.load_w_bf(wbf, d['w_in'], 8, 'a_w')
            wkeys = [('a_w', kc) for kc in range(8)]
            xts = self.sb(ph, "a_x", [128, 1, 8, 512], F32)
            sq = self.sb(ph, "a_sq", [128, 8, 512], BF16)
            rstd = self.sb(ph, "a_rstd", [128, 512], F32)
            hs = self.sb(ph, "a_h", [128, 1, 8, 512], BF16)
            stb = self.sb(ph, "a_stb", [128, 3, 4, 512], BF16)
            stf = self.sb(ph, "a_stf", [128, 2, 4, 512], F32)
            sti = self.sb(ph, "a_sti", [128, 2, 4, 512], BF16)
            nb16 = 0
            nf32 = 0
            tiles = self.seq_tiles()
            for it, (b, ti, t0, n, j) in enumerate(tiles):
                xb = 0
                xkey = ('a_x', xb)
                hkey = ('a_h', xb)
                S.dma('sp', xts[:, xb, :, :n], d['xT'][b, :, t0:t0 + n].rearrange("(kc p) t -> p kc t", p=128),
                      reads=[('dram', 'xT', b, ti)], writes=[xkey])
                self.rms_mod_tile(ph, 'a_', xts[:, xb], [xkey], n, self.A1, self.mod, 'A1', 'mod', j,
                                  (sq, rstd, hs[:, xb], hkey))
                h = hs[:, xb]
                for (dn, roff, dt, cbs) in groups:
                    if dt == BF16:
                        sl = nb16 % 3
                        nb16 += 1
                        st = stb[:, sl]
                        skey = ('a_stb', sl)
                    else:
                        sl = nf32 % 2
                        nf32 += 1
                        st = stf[:, sl]
                        skey = ('a_stf', sl)
                    for ci, cb in enumerate(cbs):
                        bank = S.next_bank()
                        for kc in range(8):
                            S.op('pe', lambda e, kc=kc, cb=cb, bank=bank: e.matmul(
                                self.ps[:, bank, :n], lhsT=wbf[:, kc, cb * 128:(cb + 1) * 128], rhs=h[:, kc, :n],
                                start=(kc == 0), stop=(kc == 7)), reads=[('a_w', kc), hkey], writes=[('ps', bank)])
                        S.op('act', lambda e, ci=ci, cb=cb, bank=bank, st=st: e.activation(
                            out=st[:, ci, :n], in_=self.ps[:, bank, :n], func=FUNC[cb]),
                            reads=[('ps', bank)], writes=[skey])
                    nb = len(cbs)
                    S.dma('sp', d[dn][b, roff:roff + nb * 128, t0:t0 + n].rearrange("(c p) t -> p c t", p=128),
                          st[:, :nb, :n], reads=[skey], writes=[('dram', dn, b, ti)])
                sl = it % 2
                for s in range(n // 128):
                    bank = S.next_bank()
                    for kc in range(8):
                        S.op('pe', lambda e, kc=kc, s=s, bank=bank: e.matmul(
                            self.ps[:, bank, :], lhsT=h[:, kc, s * 128:(s + 1) * 128], rhs=wbf[:, kc, 1536:2048],
                            start=(kc == 0), stop=(kc == 7)), reads=[('a_w', kc), hkey], writes=[('ps', bank)])
                    S.op('dve', lambda e, s=s, bank=bank: e.tensor_copy(out=sti[:, sl, s, :], in_=self.ps[:, bank, :]),
                         reads=[('ps', bank)], writes=[('a_sti', sl)])
                ns = n // 128
                S.dma('sp', d['hgi'][b, t0:t0 + n, :].rearrange("(s p) c -> p s c", p=128), sti[:, sl, :ns, :],
                      reads=[('a_sti', sl)], writes=[('dram', 'hgi', b, ti)])

    def phase_b(self):
        S, d = self.S, self.d
        QS = float(128 ** -0.5)
        with ExitStack() as ph:
            sb = lambda n, s_, dt: self.sb(ph, n, s_, dt)
            hm = sb("b_hm", [128, 2, 128], F32)
            S.dma('sp', hm[:], d['hmask'][:, :, :], writes=['b_hm'])
            rst = sb("b_rst", [128, TT], F32)
            S.dma('sp', rst[:], d['rst32'][:, :], writes=['b_rst'])
            q = sb("b_q", [128, TT], BF16)
            og = sb("b_og", [128, TT], BF16)
            sf = sb("b_sf", [128, TT], F32)
            V = sb("b_V", [128, NT128, 128], BF16)
            lf = sb("b_lf", [128, TT], F32)
            kf = sb("b_kf", [128, TT], F32)
            bcs = sb("b_bcs", [128, TT], F32)
            tmp = sb("b_tmp", [128, TT], F32)
            eX = sb("b_eX", [128, TT], F32)
            qs = sb("b_qs", [128, 2, TT], BF16)
            kh = sb("b_kh", [128, 2, TT], BF16)
            qe = sb("b_qe", [128, 2, TT], BF16)
            ke = sb("b_ke", [128, 2, TT], BF16)
            khT = sb("b_khT", [128, 2, NT128, 128], BF16)
            Vbd = sb("b_Vbd", [128, NT128, 4, 128], BF16)
            S.op('pool', lambda e: e.memset(Vbd[:], 0.0), writes=['b_Vbd'])
            ebe = sb("b_ebe", [128, 2, NHC], F32)
            cend = sb("b_cend", [128, NHC], F32)
            Sst = sb("b_S", [128, 2, 2, 128], F32)
            Sall = sb("b_Sall", [128, 2, NHC, 128], BF16)
            P = sb("b_P", [128, 2, 4, 128], BF16)
            osb = sb("b_osb", [128, 512], F32)
            osq = sb("b_osq", [128, 512], BF16)
            rstd = sb("b_rstd", [128, 512], F32)
            ost = sb("b_ost", [128, 2, 512], BF16)
            v3 = lambda t: t[:].rearrange("p (c i) -> p c i", i=HC)
            nout = 0
            for b in range(self.NB):
                for h in range(4):
                    rows = slice(h * 128, (h + 1) * 128)
                    allt = [('dram', 'hgq', b, ti) for ti in range(5)]
                    S.dma('sp', q[:], d['hgq'][b, rows, :], reads=[('dram', 'hgq', b, ti) for ti in range(5)],
                          writes=['b_q'])
                    S.dma('sp', og[:], d['hgo'][b, rows, :], reads=[('dram', 'hgo', b, ti) for ti in range(5)],
                          writes=['b_og'])
                    S.dma('pool', V[:], d['hgi'][b, :, rows].rearrange("(t p) c -> p t c", p=128),
                          reads=[('dram', 'hgi', b, ti) for ti in range(5)], writes=['b_V'])
                    for cc in range(4):
                        S.dma('pool' if cc % 2 else 'sp', Vbd[cc * 32:(cc + 1) * 32, :, cc, :],
                              d['hgi'][b, :, rows].rearrange("(t p) c -> p t c", p=128)[cc * 32:(cc + 1) * 32],
                              reads=[('dram', 'hgi', b, ti) for ti in range(5)], writes=['b_Vbd'])
                    for dr in range(2):
                        ie = HC - 1 if dr == 0 else 0
                        im = HC // 2 - 1 if dr == 0 else HC // 2
                        li = dr * 4 + h
                        S.dma('sp', sf[:], d['hgf'][b, dr * 512 + h * 128:dr * 512 + (h + 1) * 128, :],
                              reads=[('dram', 'hgf', b, ti) for ti in range(5)], writes=['b_sf'])
                        S.op('act', lambda e: e.activation(out=lf[:], in_=sf[:], func=AF.Ln,
                                                           scale=self.lbv[:, 1, li:li + 1],
                                                           bias=self.lbv[:, 0, li:li + 1]),
                             reads=['b_sf', 'lbv'], writes=['b_lf'])
                        S.op('pool', lambda e: e.tensor_scalar(out=kf[:], in0=sf[:], scalar1=self.lbv[:, 2, li:li + 1],
                                                               scalar2=self.lbv[:, 1, li:li + 1], op0=ALU.mult,
                                                               op1=ALU.add), reads=['b_sf', 'lbv'], writes=['b_kf'])
                        S.op('dve', lambda e: e.tensor_tensor_scan(out=bcs[:], data0=rst[:], data1=lf[:], initial=0.0,
                                                                   op0=ALU.mult, op1=ALU.add),
                             reads=['b_rst', 'b_lf'], writes=['b_bcs'])
                        if dr == 0:
                            Bd, Bkey = bcs, 'b_bcs'
                        else:
                            S.op('act', lambda e: e.activation(out=cend[:], in_=v3(bcs)[:, :, HC - 1], func=AF.Copy),
                                 reads=['b_bcs'], writes=['b_cend'])
                            S.op('dve', lambda e: e.tensor_tensor(out=tmp[:], in0=lf[:], in1=bcs[:], op=ALU.subtract),
                                 reads=['b_lf', 'b_bcs'], writes=['b_tmp'])
                            S.op('dve', lambda e: e.tensor_tensor(
                                out=v3(bcs), in0=v3(tmp), in1=cend[:].unsqueeze(2).to_broadcast([128, NHC, HC]),
                                op=ALU.add), reads=['b_tmp', 'b_cend'], writes=['b_bcs'])
                            Bd, Bkey = bcs, 'b_bcs'
                        S.op('act', lambda e: e.activation(out=eX[:], in_=Bd[:], func=AF.Exp), reads=[Bkey],
                             writes=['b_eX'])
                        S.op('act', lambda e: e.activation(out=ebe[:, dr, :], in_=v3(eX)[:, :, ie], func=AF.Copy),
                             reads=['b_eX'], writes=[('b_ebe', dr)])
                        S.op('dve', lambda e: e.scalar_tensor_tensor(out=qs[:, dr, :], in0=q[:], scalar=QS, in1=eX[:],
                                                                     op0=ALU.mult, op1=ALU.mult),
                             reads=['b_q', 'b_eX'], writes=[('b_qs', dr)])
                        S.op('dve', lambda e: e.tensor_tensor(
                            out=v3(tmp), in0=v3(Bd)[:, :, ie:ie + 1].to_broadcast([128, NHC, HC]), in1=v3(Bd),
                            op=ALU.subtract), reads=[Bkey], writes=['b_tmp'])
                        S.op('act', lambda e: e.activation(out=eX[:], in_=tmp[:], func=AF.Exp), reads=['b_tmp'],
                             writes=['b_eX'])
                        S.op('pool', lambda e: e.tensor_tensor(out=kh[:, dr, :], in0=kf[:], in1=eX[:], op=ALU.mult),
                             reads=['b_kf', 'b_eX'], writes=[('b_kh', dr)])
                        S.op('dve', lambda e: e.tensor_tensor(
                            out=v3(tmp), in0=v3(Bd), in1=v3(Bd)[:, :, im:im + 1].to_broadcast([128, NHC, HC]),
                            op=ALU.subtract), reads=[Bkey], writes=['b_tmp'])
                        S.op('act', lambda e: e.activation(out=eX[:], in_=tmp[:], func=AF.Exp), reads=['b_tmp'],
                             writes=['b_eX'])
                        S.op('dve', lambda e: e.scalar_tensor_tensor(out=qe[:, dr, :], in0=q[:], scalar=QS, in1=eX[:],
                                                                     op0=ALU.mult, op1=ALU.mult),
                             reads=['b_q', 'b_eX'], writes=[('b_qe', dr)])
                        S.op('act', lambda e: e.activation(out=eX[:], in_=tmp[:], func=AF.Exp, scale=-1.0),
                             reads=['b_tmp'], writes=['b_eX'])
                        S.op('pool', lambda e: e.tensor_tensor(out=ke[:, dr, :], in0=kf[:], in1=eX[:], op=ALU.mult),
                             reads=['b_kf', 'b_eX'], writes=[('b_ke', dr)])
                        if BSTOP <= 1:
                            return
                        pbf = self.ps[:, 4, :].bitcast(BF16)
                        for g in range(3):
                            for tt_ in range(6):
                                n_ = g * 6 + tt_
                                S.op('pe', lambda e, n_=n_, tt_=tt_: e.transpose(
                                    out=pbf[:, tt_ * 128:(tt_ + 1) * 128], in_=kh[:, dr, n_ * 128:(n_ + 1) * 128],
                                    identity=self.ident), reads=[('b_kh', dr), 'cbf'], writes=[('ps', 4)])
                            S.op('act', lambda e, g=g: e.activation(
                                out=khT[:, dr, g * 6:(g + 1) * 6, :],
                                in_=pbf[:, 0:768].rearrange("p (t c) -> p t c", c=128), func=AF.Copy),
                                reads=[('ps', 4)], writes=[('b_khT', dr)])

                    if BSTOP <= 2:
                        return
                    S.op('pool', lambda e: e.memset(Sst[:], 0.0),
                         writes=[('b_S', 0, 0), ('b_S', 0, 1), ('b_S', 1, 0), ('b_S', 1, 1)])
                    nctx = NCTX // HC
                    orders = [list(range(NHC)), list(range(nctx - 1, -1, -1)) + list(range(NHC - 1, nctx - 1, -1))]
                    ntl = [0, 0]
                    cur_bank = [0, 0]
                    for i in range(NHC):
                        for dr in range(2):
                            c = orders[dr][i]
                            n_, cc = c // 4, c % 4
                            pi_, po_ = i % 2, (i + 1) % 2
                            S.op('act', lambda e, c=c, dr=dr, pi_=pi_: e.activation(
                                out=Sall[:, dr, c, :], in_=Sst[:, dr, pi_, :], func=AF.Copy),
                                reads=[('b_S', dr, pi_)], writes=[('b_Sall', dr, c)])
                            if cc == (0 if dr == 0 else 3):
                                bank = dr * 2 + ntl[dr] % 2
                                ntl[dr] += 1
                                cur_bank[dr] = bank
                                S.op('pe', lambda e, n_=n_, dr=dr, bank=bank: e.matmul(
                                    self.ps[:, bank, :], lhsT=khT[:, dr, n_, :],
                                    rhs=Vbd[:, n_, :, :].rearrange("p a b -> p (a b)"), start=True, stop=True),
                                    reads=[('b_khT', dr), 'b_Vbd'], writes=[('ps', bank)])
                            bank = cur_bank[dr]
                            S.op('dve', lambda e, c=c, dr=dr, bank=bank, cc=cc, pi_=pi_, po_=po_: e.scalar_tensor_tensor(
                                out=Sst[:, dr, po_, :], in0=Sst[:, dr, pi_, :], scalar=ebe[:, dr, c:c + 1],
                                in1=self.ps[:, bank, cc * 128:(cc + 1) * 128], op0=ALU.mult, op1=ALU.add),
                                reads=[('b_S', dr, pi_), ('b_ebe', dr), ('ps', bank)], writes=[('b_S', dr, po_)])
                    if BSTOP <= 3:
                        return
                    for ti, (t0, n) in enumerate(TILES):
                        ng_ = n // 128
                        n0 = t0 // 128
                        for dr in range(2):
                            bk = 5 + dr
                            for g in range(ng_):
                                cs_ = slice((n0 + g) * 128, (n0 + g + 1) * 128)
                                S.op('pe', lambda e, g=g, cs_=cs_, dr=dr, bk=bk: e.matmul(
                                    self.ps[:, bk, g * 128:(g + 1) * 128], lhsT=ke[:, dr, cs_], rhs=qe[:, dr, cs_],
                                    start=True, stop=True), reads=[('b_ke', dr), ('b_qe', dr)], writes=[('ps', bk)])
                            S.op('dve', lambda e, dr=dr, bk=bk: e.tensor_tensor(
                                out=P[:, dr, :ng_, :],
                                in0=self.ps[:, bk, :ng_ * 128].rearrange("p (g i) -> p g i", i=128),
                                in1=hm[:, dr, :].unsqueeze(1).to_broadcast([128, ng_, 128]), op=ALU.mult),
                                reads=[('ps', bk), 'b_hm'], writes=[('b_P', dr)])
                        if BSTOP <= 4:
                            continue
                        for g in range(ng_):
                            n_ = n0 + g
                            ocols = slice(g * 128, (g + 1) * 128)
                            S.op('pe', lambda e, g=g, n_=n_, ocols=ocols: e.matmul(
                                self.ps[:, 7, ocols], lhsT=V[:, n_, :], rhs=P[:, 0, g, :], start=True, stop=False),
                                reads=['b_V', ('b_P', 0)], writes=[('ps', 7)])
                            S.op('pe', lambda e, g=g, n_=n_, ocols=ocols: e.matmul(
                                self.ps[:, 7, ocols], lhsT=V[:, n_, :], rhs=P[:, 1, g, :], start=False, stop=False),
                                reads=['b_V', ('b_P', 1)], writes=[('ps', 7)])
                            for cc in range(4):
                                c = 4 * n_ + cc
                                for dr in range(2):
                                    S.op('pe', lambda e, g=g, c=c, cc=cc, dr=dr: e.matmul(
                                        self.ps[:, 7, g * 128 + cc * 32:g * 128 + (cc + 1) * 32],
                                        lhsT=Sall[:, dr, c, :], rhs=qs[:, dr, c * 32:(c + 1) * 32], start=False,
                                        stop=(cc == 3 and dr == 1)),
                                        reads=[('b_Sall', dr, c), ('b_qs', dr)], writes=[('ps', 7)])
                        if BSTOP <= 5:
                            continue
                        S.op('act', lambda e: e.activation(out=osb[:, :n], in_=self.ps[:, 7, :n], func=AF.Copy),
                             reads=[('ps', 7)], writes=['b_osb'])
                        S.op('act', lambda e: e.activation(out=osq[:, :n], in_=self.ps[:, 7, :n], func=AF.Square),
                             reads=[('ps', 7)], writes=['b_osq'])
                        S.op('pe', lambda e: e.matmul(self.ps[:, 4, :n], lhsT=self.ones, rhs=osq[:, :n], start=True,
                                                      stop=True), reads=['b_osq', 'cbf'], writes=[('ps', 4)])
                        S.op('act', lambda e: e.activation(out=rstd[:, :n], in_=self.ps[:, 4, :n], func=AF.Sqrt,
                                                           scale=1.0 / 128, bias=self.sm[:, SM['eps6']:SM['eps6'] + 1]),
                             reads=[('ps', 4), 'smalls'], writes=['b_rstd'])
                        S.op('dve', lambda e: e.reciprocal(out=rstd[:, :n], in_=rstd[:, :n]), reads=['b_rstd'],
                             writes=['b_rstd'])
                        S.op('dve', lambda e: e.scalar_tensor_tensor(
                            out=osb[:, :n], in0=osb[:, :n], scalar=self.sm[:, SM['hgng']:SM['hgng'] + 1],
                            in1=rstd[:, :n], op0=ALU.mult, op1=ALU.mult), reads=['b_osb', 'b_rstd', 'smalls'],
                            writes=['b_osb'])
                        sl = nout % 2
                        nout += 1
                        S.op('pool', lambda e, sl=sl: e.tensor_tensor(out=ost[:, sl, :n], in0=osb[:, :n],
                                                                      in1=og[:, t0:t0 + n], op=ALU.mult),
                             reads=['b_osb', 'b_og'], writes=[('b_ost', sl)])
                        S.dma('sp', d['ohg'][b, rows, t0:t0 + n], ost[:, sl, :n], reads=[('b_ost', sl)],
                              writes=[('dram', 'ohg', b, ti, h)])

    def phase_c(self):
        S, d = self.S, self.d
        K_ = KAPPA

        def cap(base, off, dims):
            return bass.AP(tensor=base.tensor, offset=base.offset + off, ap=[list(base.ap[0])] + [list(x) for x in dims])

        with ExitStack() as ph:
            sb = lambda n, s_, dt: self.sb(ph, n, s_, dt)
            rmk = sb("c_rmk", [128, 2, 640], F32)
            S.dma('sp', rmk[:], d['rmask'][:, :, :], writes=['c_rmk'])
            rst = sb("c_rst", [128, TT], F32)
            S.dma('sp', rst[:], d['rst'][:, :], writes=['c_rst'])
            w2z = sb("c_w2z", [128, 2, 512], BF16)
            a2z = sb("c_a2z", [128, 2, 512], BF16)
            g2b = sb("c_g2", [128, 512], BF16)
            S.op('pool', lambda e: e.memset(w2z[:], 0.0), writes=['c_w2z'])
            S.op('pool', lambda e: e.memset(a2z[:], 0.0), writes=['c_a2z'])
            for dr in range(2):
                S.dma('pool', w2z[dr * 64:(dr + 1) * 64, dr, :], d['rw_w2'][dr * 64:(dr + 1) * 64, :], writes=['c_w2z'])
                S.dma('pool', a2z[dr * 64:(dr + 1) * 64, dr, :], d['rw_a2'][dr * 64:(dr + 1) * 64, :], writes=['c_a2z'])
            S.dma('pool', g2b[:], d['rw_g2'][:, :], writes=['c_g2'])
            twd = sb("c_twd", [128, TT], BF16)
            adm = sb("c_adm", [128, TT], BF16)
            sgd = sb("c_sgd", [128, TT], BF16)
            rm = sb("c_rm", [128, TT], BF16)
            km = sb("c_km", [128, TT], BF16)
            vm = sb("c_vm", [128, TT], BF16)
            kkb = sb("c_kkb", [128, TT], BF16)
            ksum = sb("c_ksum", [128, TT], BF16)
            yacc = sb("c_yacc", [128, TT], F32)
            raw = sb("c_raw", [128, TT], BF16)
            Ts = sb("c_Ts", [128, TT], F32)
            Ta = sb("c_Ta", [128, TT], F32)
            TG = sb("c_TG", [128, TT], F32)
            Tx = sb("c_Tx", [128, TT], F32)
            kd = sb("c_kd", [128, TT], BF16)
            bd = sb("c_bd", [128, TT], BF16)
            ARt = sb("c_ARt", [128, NT128, 2, 128], BF16)
            btz = sb("c_btz", [128, 2, TT], BF16)
            ktz = sb("c_ktz", [128, 2, TT], BF16)
            bhn = sb("c_bhn", [128, TT], BF16)
            khh = sb("c_kh", [128, TT], BF16)
            gam = sb("c_gam", [128, NCH], F32)
            cend = sb("c_cend", [128, NCH], F32)
            S.op('pool', lambda e: e.memset(btz[:], 0.0), writes=['c_btz'])
            S.op('pool', lambda e: e.memset(ktz[:], 0.0), writes=['c_ktz'])
            W2 = sb("c_W2", [128, 2, 2, 2, 128], BF16)
            XX = sb("c_XX", [128, 2, 2, 2, 2, 128], BF16)
            MA = sb("c_MA", [128, 2, 2, 256], BF16)
            MB = sb("c_MB", [128, 2, 2, 256], BF16)
            NNb = sb("c_NN", [128, 2, 2, 128], BF16)
            PQ = sb("c_PQ", [128, 2, 2, 2, 2, 128], BF16)
            bhnbd = sb("c_bhnbd", [128, 2, 2, 2, 2, 128], BF16)
            khbd = sb("c_khbd", [128, 2, 2, 2, 2, 128], BF16)
            Vpad = sb("c_Vpad", [128, 2, 2, 2, 128], BF16)
            RpT = sb("c_RpT", [128, 2, 2, 128], BF16)
            GmT = sb("c_GmT", [128, 2, 2, 2, 128], BF16)
            Zst = sb("c_Z", [128, 128], F32)
            Zbd = sb("c_Zbd", [128, 128], BF16)
            for t_, k_ in [(PQ, 'c_PQ'), (bhnbd, 'c_bhnbd'), (khbd, 'c_khbd'), (Vpad, 'c_Vpad')]:
                S.op('pool', lambda e, t_=t_: e.memset(t_[:], 0.0), writes=[(k_, q_, w_) for q_ in range(2) for w_ in range(2)])
            e_yc = sb("c_eyc", [128, 512], F32)
            e_t = sb("c_et", [128, 512], F32)
            e_bf = sb("c_ebf", [128, 512], BF16)
            e_rstd = sb("c_erstd", [128, 512], F32)
            e_out = sb("c_eout", [128, 2, 512], BF16)
            v3 = lambda t: t[:].rearrange("p (c i) -> p c i", i=64)
            g3 = lambda t: t[:, NCTX:].rearrange("p (r c) -> p r c", c=64)
            sm = self.sm
            muv = self.muv

            def mix(blk, acc, acckey):
                col = lambda c: muv[:, c, blk:blk + 1]
                S.op('dve', lambda e: e.tensor_scalar(out=acc[:], in0=raw[:], scalar1=col(0), scalar2=None,
                                                      op0=ALU.mult), reads=['c_raw', 'muv'], writes=[acckey])
                ga, gr = g3(acc), g3(raw)
                views = [(ga[:, :, 1:64], gr[:, :, 0:63], 1), (ga[:, :, 0:63], gr[:, :, 1:64], 2),
                         (ga[:, 1:32, :], gr[:, 0:31, :], 3), (ga[:, 0:31, :], gr[:, 1:32, :], 4),
                         (acc[:, 1:NCTX], raw[:, 0:NCTX - 1], 5), (acc[:, 0:NCTX - 1], raw[:, 1:NCTX], 6)]
                for (o_, i_, c) in views:
                    S.op('dve', lambda e, o_=o_, i_=i_, c=c: e.scalar_tensor_tensor(
                        out=o_, in0=i_, scalar=col(c), in1=o_, op0=ALU.mult, op1=ALU.add),
                        reads=['c_raw', 'muv', acckey], writes=[acckey])

            def load_raw(b, blk):
                S.dma('sp', raw[:], d['rwp'][b, blk * 128:(blk + 1) * 128, :],
                      reads=[('dram', 'rwp', b, ti) for ti in range(5)], writes=['c_raw'])

            nout = 0
            for b in range(self.NB):
                for (blk, dst, fn, key) in [(12, twd, AF.Tanh, 'c_twd'), (13, adm, AF.Copy, 'c_adm'),
                                            (14, sgd, AF.Sigmoid, 'c_sgd')]:
                    load_raw(b, blk)
                    mix(blk, Ts, 'c_Ts')
                    S.op('act', lambda e, dst=dst, fn=fn: e.activation(out=dst[:], in_=Ts[:], func=fn),
                         reads=['c_Ts'], writes=[key])
                for hp in range(4):
                    hcols = slice(hp * 128, (hp + 1) * 128)
                    load_raw(b, hp)
                    mix(hp, Ts, 'c_Ts')
                    S.op('act', lambda e: e.activation(out=rm[:], in_=Ts[:], func=AF.Copy), reads=['c_Ts'],
                         writes=['c_rm'])
                    load_raw(b, 8 + hp)
                    mix(8 + hp, Ts, 'c_Ts')
                    S.op('act', lambda e: e.activation(out=vm[:], in_=Ts[:], func=AF.Copy), reads=['c_Ts'],
                         writes=['c_vm'])
                    load_raw(b, 4 + hp)
                    mix(4 + hp, Ts, 'c_Ts')
                    S.op('act', lambda e: e.activation(out=km[:], in_=Ts[:], func=AF.Copy), reads=['c_Ts'],
                         writes=['c_km'])
                    S.op('dve', lambda e: e.tensor_scalar(out=Ta[:], in0=Ts[:], scalar1=sm[:, SM['kk'] + hp:SM['kk'] + hp + 1],
                                                          scalar2=None, op0=ALU.mult), reads=['c_Ts', 'smalls'],
                         writes=['c_Ta'])
                    S.op('act', lambda e: e.activation(out=raw[:], in_=Ta[:], func=AF.Square), reads=['c_Ta'],
                         writes=['c_raw'])
                    for (t0, n) in TILES:
                        bank = S.next_bank7()
                        S.op('pe', lambda e, bank=bank: e.matmul(self.ps[:, bank, :n], lhsT=self.bones,
                                                                 rhs=raw[:, t0:t0 + n], start=True, stop=True),
                             reads=['c_raw', 'cbf'], writes=[('ps', bank)])
                        S.op('act', lambda e, bank=bank: e.activation(out=TG[:, t0:t0 + n], in_=self.ps[:, bank, :n],
                                                                      func=AF.Sqrt), reads=[('ps', bank)],
                             writes=['c_TG'])
                    S.op('dve', lambda e: e.tensor_scalar(out=TG[:], in0=TG[:], scalar1=1e-12, scalar2=None,
                                                          op0=ALU.max), reads=['c_TG'], writes=['c_TG'])
                    S.op('dve', lambda e: e.reciprocal(out=TG[:], in_=TG[:]), reads=['c_TG'], writes=['c_TG'])
                    S.op('dve', lambda e: e.tensor_tensor(out=kkb[:], in0=Ta[:], in1=TG[:], op=ALU.mult),
                         reads=['c_Ta', 'c_TG'], writes=['c_kkb'])
                    for dr in range(2):
                        ie = 63 if dr == 0 else 0
                        wi = dr * 4 + hp
                        for (wz, wkey, src, skey, bcol, dst, dkey) in [
                                (w2z, 'c_w2z', twd, 'c_twd', SM['w0'] + wi, Ts, 'c_Ts'),
                                (a2z, 'c_a2z', adm, 'c_adm', SM['a0'] + wi, Ta, 'c_Ta')]:
                            for (t0, n) in TILES:
                                bank = S.next_bank7()
                                S.op('pe', lambda e, bank=bank, wz=wz, src=src: e.matmul(
                                    self.ps[:, bank, :n], lhsT=wz[:, dr, hcols], rhs=src[:, t0:t0 + n], start=True,
                                    stop=True), reads=[wkey, skey], writes=[('ps', bank)])
                                S.op('act', lambda e, bank=bank, dst=dst, bcol=bcol: e.activation(
                                    out=dst[:, t0:t0 + n], in_=self.ps[:, bank, :n], func=AF.Sigmoid,
                                    bias=sm[:, bcol:bcol + 1]), reads=[('ps', bank), 'smalls'], writes=[dkey])
                        S.op('dve', lambda e: e.tensor_scalar(out=Tx[:], in0=Ta[:],
                                                              scalar1=sm[:, SM['ka'] + hp:SM['ka'] + hp + 1],
                                                              scalar2=self.kav[:, hp:hp + 1], op0=ALU.mult,
                                                              op1=ALU.add), reads=['c_Ta', 'smalls', 'kav'],
                             writes=['c_Tx'])
                        S.op('dve', lambda e: e.tensor_tensor(out=kd[:], in0=km[:], in1=Tx[:], op=ALU.mult),
                             reads=['c_km', 'c_Tx'], writes=['c_kd'])
                        S.op('pool', lambda e: e.tensor_tensor(out=bd[:], in0=kkb[:], in1=Ta[:], op=ALU.mult),
                             reads=['c_kkb', 'c_Ta'], writes=['c_bd'])
                        if dr == 0:
                            S.op('pool', lambda e: e.tensor_copy(out=ksum[:], in_=kd[:]), reads=['c_kd'],
                                 writes=['c_ksum'])
                        else:
                            S.op('pool', lambda e: e.tensor_tensor(out=ksum[:], in0=ksum[:], in1=kd[:], op=ALU.add),
                                 reads=['c_kd', 'c_ksum'], writes=['c_ksum'])
                        S.op('dve', lambda e: e.tensor_tensor_scan(out=TG[:], data0=rst[:], data1=Ts[:], initial=0.0,
                                                                   op0=ALU.mult, op1=ALU.add),
                             reads=['c_rst', 'c_Ts'], writes=['c_TG'])
                        if dr == 1:
                            S.op('act', lambda e: e.activation(out=cend[:], in_=v3(TG)[:, :, 63], func=AF.Copy),
                                 reads=['c_TG'], writes=['c_cend'])
                            S.op('dve', lambda e: e.tensor_tensor(out=Tx[:], in0=Ts[:], in1=TG[:], op=ALU.subtract),
                                 reads=['c_Ts', 'c_TG'], writes=['c_Tx'])
                            S.op('dve', lambda e: e.tensor_tensor(
                                out=v3(TG), in0=v3(Tx), in1=cend[:].unsqueeze(2).to_broadcast([128, NCH, 64]),
                                op=ALU.add), reads=['c_Tx', 'c_cend'], writes=['c_TG'])
                        S.op('act', lambda e: e.activation(out=gam[:], in_=v3(TG)[:, :, ie], func=AF.Exp, scale=-K_),
                             reads=['c_TG'], writes=['c_gam'])
                        S.op('dve', lambda e: e.tensor_tensor(out=Tx[:], in0=TG[:], in1=Ts[:], op=ALU.subtract),
                             reads=['c_TG', 'c_Ts'], writes=['c_Tx'])
                        S.op('act', lambda e: e.activation(out=Tx[:], in_=Tx[:], func=AF.Exp, scale=-K_),
                             reads=['c_Tx'], writes=['c_Tx'])
                        t128 = lambda t: t[:].rearrange("p (n i) -> p n i", i=128)
                        S.op('dve', lambda e: e.tensor_tensor(out=ARt[:, :, 0, :], in0=t128(kkb), in1=t128(Tx),
                                                              op=ALU.mult), reads=['c_kkb', 'c_Tx'],
                             writes=['c_ARt'])
                        S.op('act', lambda e: e.activation(out=Tx[:], in_=TG[:], func=AF.Exp, scale=-K_),
                             reads=['c_TG'], writes=['c_Tx'])
                        S.op('pool', lambda e: e.tensor_tensor(out=ARt[:, :, 1, :], in0=t128(rm), in1=t128(Tx),
                                                               op=ALU.mult), reads=['c_rm', 'c_Tx'],
                             writes=['c_ARt'])
                        S.op('act', lambda e: e.activation(out=Tx[:], in_=TG[:], func=AF.Exp, scale=K_),
                             reads=['c_TG'], writes=['c_Tx'])
                        for hh in range(2):
                            pr = slice(hh * 64, (hh + 1) * 64)
                            S.op('dve', lambda e, hh=hh, pr=pr: e.tensor_tensor(out=btz[pr, hh, :], in0=bd[pr, :],
                                                                                in1=Tx[pr, :], op=ALU.mult),
                                 reads=['c_bd', 'c_Tx'], writes=['c_btz'])
                            S.op('pool', lambda e, hh=hh, pr=pr: e.tensor_tensor(out=ktz[pr, hh, :], in0=kd[pr, :],
                                                                                 in1=Tx[pr, :], op=ALU.mult),
                                 reads=['c_kd', 'c_Tx'], writes=['c_ktz'])
                        S.op('dve', lambda e: e.tensor_tensor(
                            out=v3(Tx), in0=v3(TG)[:, :, ie:ie + 1].to_broadcast([128, NCH, 64]), in1=v3(TG),
                            op=ALU.subtract), reads=['c_TG'], writes=['c_Tx'])
                        S.op('act', lambda e: e.activation(out=Tx[:], in_=Tx[:], func=AF.Exp, scale=-K_),
                             reads=['c_Tx'], writes=['c_Tx'])
                        S.op('dve', lambda e: e.scalar_tensor_tensor(out=bhn[:], in0=bd[:], scalar=-1.0, in1=Tx[:],
                                                                     op0=ALU.mult, op1=ALU.mult),
                             reads=['c_bd', 'c_Tx'], writes=['c_bhn'])
                        S.op('pool', lambda e: e.tensor_tensor(out=khh[:], in0=kd[:], in1=Tx[:], op=ALU.mult),
                             reads=['c_kd', 'c_Tx'], writes=['c_kh'])
                        if BSTOP <= 1:
                            continue
                        S.op('pool', lambda e: e.memset(Zst[:], 0.0), writes=['c_Z'])
                        S.op('pool', lambda e: e.memset(Zbd[:], 0.0), writes=['c_Zbd'])
                        tiles = list(range(NT128)) if dr == 0 else [1, 0] + list(range(NT128 - 1, 1, -1))
                        YB = [7, 6]
                        nb6 = lambda: S.next_bank6()

                        def st_T(n_, w, q):
                            tc = slice(n_ * 128, (n_ + 1) * 128)
                            bt_ = nb6()
                            pbf = self.ps[:, bt_, :].bitcast(BF16)
                            for qi, (src, skey) in enumerate([(ARt[:, n_, 0, :], 'c_ARt'), (bhn[:, tc], 'c_bhn'),
                                                              (khh[:, tc], 'c_kh'), (vm[:, tc], 'c_vm')]):
                                S.op('pe', lambda e, qi=qi, src=src: e.transpose(
                                    out=pbf[:, qi * 128:(qi + 1) * 128], in_=src, identity=self.ident),
                                    reads=[skey, 'cbf'], writes=[('ps', bt_)])
                            S.op('act', lambda e: e.activation(
                                out=W2[:, w, 0, :, 0:64], in_=pbf[:, 0:128].rearrange("p (h k) -> p h k", h=2),
                                func=AF.Copy), reads=[('ps', bt_)], writes=[('c_W2', w, 0)])
                            for cc in range(2):
                                pr = slice(cc * 64, (cc + 1) * 64)
                                S.op('dve', lambda e, cc=cc, pr=pr: e.tensor_copy(
                                    out=cap(bhnbd[pr, q, w], cc * 128, [[320, 2], [1, 64]]),
                                    in_=pbf[pr, 128:256].rearrange("p (h k) -> p h k", h=2)),
                                    reads=[('ps', bt_)], writes=[('c_bhnbd', q, w)])
                                S.op('act', lambda e, cc=cc, pr=pr: e.activation(
                                    out=cap(khbd[pr, q, w], cc * 128, [[320, 2], [1, 64]]),
                                    in_=pbf[pr, 256:384].rearrange("p (h k) -> p h k", h=2), func=AF.Copy),
                                    reads=[('ps', bt_)], writes=[('c_khbd', q, w)])
                            S.op('dve', lambda e: e.tensor_copy(
                                out=cap(Vpad[:, q, w], 0, [[192, 2], [1, 64]]),
                                in_=pbf[:, 384:512].rearrange("p (h k) -> p h k", h=2)),
                                reads=[('ps', bt_)], writes=[('c_Vpad', q, w)])

                        def st_S1(n_, w, q):
                            tc = slice(n_ * 128, (n_ + 1) * 128)
                            bA, bB, bC = nb6(), nb6(), nb6()
                            ar = ARt[:, n_, :, :].rearrange("p a b -> p (a b)")
                            for hh in range(2):
                                S.op('pe', lambda e, hh=hh: e.matmul(self.ps[:, bA, hh * 256:(hh + 1) * 256],
                                                                     lhsT=btz[:, hh, tc], rhs=ar, start=True, stop=True),
                                     reads=['c_btz', 'c_ARt'], writes=[('ps', bA)])
                            for hh in range(2):
                                S.op('pe', lambda e, hh=hh: e.matmul(self.ps[:, bB, hh * 256:(hh + 1) * 256],
                                                                     lhsT=ktz[:, hh, tc], rhs=ar, start=True, stop=True),
                                     reads=['c_ktz', 'c_ARt'], writes=[('ps', bB)])
                            for hh in range(2):
                                S.op('pe', lambda e, hh=hh: e.matmul(self.ps[:, bC, hh * 128:(hh + 1) * 128],
                                                                     lhsT=ARt[:, n_, 0, :], rhs=btz[:, hh, tc],
                                                                     start=True, stop=True),
                                     reads=['c_btz', 'c_ARt'], writes=[('ps', bC)])
                            S.op('dve', lambda e: e.tensor_tensor(
                                out=MA[:, w], in0=self.ps[:, bA, :].rearrange("p (h c) -> p h c", h=2),
                                in1=rmk[:, dr, 0:256].unsqueeze(1).to_broadcast([128, 2, 256]), op=ALU.mult),
                                reads=[('ps', bA), 'c_rmk'], writes=[('c_MA', w)])
                            S.op('dve', lambda e: e.tensor_tensor(
                                out=MB[:, w], in0=self.ps[:, bB, :].rearrange("p (h c) -> p h c", h=2),
                                in1=rmk[:, dr, 256:512].unsqueeze(1).to_broadcast([128, 2, 256]), op=ALU.mult),
                                reads=[('ps', bB), 'c_rmk'], writes=[('c_MB', w)])
                            S.op('dve', lambda e: e.tensor_tensor(
                                out=NNb[:, w], in0=self.ps[:, bC, 0:256].rearrange("p (h c) -> p h c", h=2),
                                in1=rmk[:, dr, 512:640].unsqueeze(1).to_broadcast([128, 2, 128]), op=ALU.mult),
                                reads=[('ps', bC), 'c_rmk'], writes=[('c_NN', w)])

                        def st_S2(n_, w, q):
                            bL = nb6()
                            for hh in range(2):
                                S.op('pe', lambda e, hh=hh: e.matmul(
                                    self.ps[:, bL, 0:128], lhsT=MB[:, w, hh, 0:128], rhs=Vpad[:, q, w, hh, :],
                                    start=(hh == 0), stop=(hh == 1)),
                                    reads=[('c_MB', w), ('c_Vpad', q, w)], writes=[('ps', bL)])
                            S.op('act', lambda e: e.activation(
                                out=W2[:, w, 0, :, 64:128],
                                in_=self.ps[:, bL, 0:128].rearrange("p (h k) -> p h k", h=2),
                                func=AF.Copy), reads=[('ps', bL)], writes=[('c_W2', w, 0)])

                        def st_S3(n_, w, q, lv):
                            wi_, wo_ = lv % 2, (lv + 1) % 2
                            if lv == 0:
                                Xh = lambda hh: NNb[:, w, hh, :]
                                XTh = lambda hh: MA[:, w, hh, 0:128]
                                xkeys = [('c_NN', w), ('c_MA', w)]
                            else:
                                xi = lv % 2
                                Xh = lambda hh, xi=xi: XX[:, w, xi, hh, 0, :]
                                XTh = lambda hh, xi=xi: XX[:, w, xi, hh, 1, :]
                                xkeys = [('c_XX', w, xi)]
                            bW = nb6()
                            for hh in range(2):
                                S.op('pe', lambda e, hh=hh: e.matmul(
                                    self.ps[:, bW, hh * 128:(hh + 1) * 128], lhsT=XTh(hh), rhs=W2[:, w, wi_, hh, :],
                                    start=True, stop=True), reads=xkeys + [('c_W2', w, wi_)], writes=[('ps', bW)])
                            if lv < 5:
                                S.op('dve', lambda e: e.tensor_tensor(
                                    out=W2[:, w, wo_, :, :],
                                    in0=self.ps[:, bW, 0:256].rearrange("p (h c) -> p h c", h=2),
                                    in1=W2[:, w, wi_, :, :], op=ALU.add), reads=[('ps', bW), ('c_W2', w, wi_)],
                                    writes=[('c_W2', w, wo_)])
                                bX = nb6()
                                xo_ = (lv + 1) % 2
                                for hh in range(2):
                                    if lv < 4:
                                        S.op('pe', lambda e, hh=hh: e.matmul(
                                            self.ps[:, bX, hh * 256:hh * 256 + 128], lhsT=XTh(hh), rhs=Xh(hh),
                                            start=True, stop=True), reads=xkeys, writes=[('ps', bX)])
                                    S.op('pe', lambda e, hh=hh: e.matmul(
                                        self.ps[:, bX, hh * 256 + 128:hh * 256 + 256], lhsT=Xh(hh), rhs=XTh(hh),
                                        start=True, stop=True), reads=xkeys, writes=[('ps', bX)])
                                if lv < 4:
                                    S.op('act', lambda e: e.activation(
                                        out=XX[:, w, xo_].rearrange("p h x c -> p (h x c)"), in_=self.ps[:, bX, :],
                                        func=AF.Copy), reads=[('ps', bX)], writes=[('c_XX', w, xo_)])
                                else:
                                    S.op('act', lambda e: e.activation(
                                        out=XX[:, w, xo_, :, 1, :],
                                        in_=self.ps[:, bX, :].rearrange("p (h x c) -> p h x c", h=2, x=2)[:, :, 1, :],
                                        func=AF.Copy), reads=[('ps', bX)], writes=[('c_XX', w, xo_)])
                            else:
                                for hh in range(2):
                                    S.op('dve', lambda e, hh=hh: e.tensor_tensor(
                                        out=cap(PQ[:, q, w], hh * 128 + hh * 64, [[256, 2], [1, 64]]),
                                        in0=self.ps[:, bW, hh * 128:(hh + 1) * 128].rearrange(
                                            "p (q k) -> p q k", q=2),
                                        in1=W2[:, w, wi_, hh, :].rearrange("p (q k) -> p q k", q=2), op=ALU.add),
                                        reads=[('ps', bW), ('c_W2', w, wi_)], writes=[('c_PQ', q, w)])

                        def st_S4(n_, w, q):
                            bR = nb6()
                            for hh in range(2):
                                S.op('pe', lambda e, hh=hh: e.matmul(self.ps[:, bR, 0:128], lhsT=PQ[:, q, w, 0, hh, :],
                                                                     rhs=MA[:, w, hh, 128:256], start=(hh == 0),
                                                                     stop=(hh == 1)),
                                     reads=[('c_PQ', q, w), ('c_MA', w)], writes=[('ps', bR)])
                            S.op('dve', lambda e: e.tensor_tensor(out=RpT[:, q, w, :], in0=self.ps[:, bR, 0:128],
                                                                  in1=ARt[:, n_, 1, :], op=ALU.add),
                                 reads=[('ps', bR), 'c_ARt'], writes=[('c_RpT', q, w)])
                            bG = nb6()
                            for hh in range(2):
                                S.op('pe', lambda e, hh=hh: e.matmul(
                                    self.ps[:, bG, 0:256], lhsT=PQ[:, q, w, 0, hh, :],
                                    rhs=bhnbd[:, q, w, hh, :, :].rearrange("p a b -> p (a b)"), start=(hh == 0),
                                    stop=(hh == 1)), reads=[('c_PQ', q, w), ('c_bhnbd', q, w)], writes=[('ps', bG)])
                            S.op('act', lambda e: e.activation(
                                out=GmT[:, q, w].rearrange("p a b -> p (a b)"), in_=self.ps[:, bG, 0:256], func=AF.Copy),
                                reads=[('ps', bG)], writes=[('c_GmT', q, w)])
                            yb = YB[w]
                            for hh in range(2):
                                S.op('pe', lambda e, hh=hh: e.matmul(self.ps[:, yb, 0:128], lhsT=Vpad[:, q, w, hh, :],
                                                                     rhs=MB[:, w, hh, 128:256], start=(hh == 0),
                                                                     stop=False),
                                     reads=[('c_Vpad', q, w), ('c_MB', w)], writes=[('ps', yb)])
                                S.op('pe', lambda e, hh=hh: e.matmul(self.ps[:, yb, 0:128], lhsT=PQ[:, q, w, 1, hh, :],
                                                                     rhs=MA[:, w, hh, 128:256], start=False, stop=False),
                                     reads=[('c_PQ', q, w), ('c_MA', w)], writes=[('ps', yb)])

                        def chain_step(n_, w, q, ci):
                            tc = slice(n_ * 128, (n_ + 1) * 128)
                            yb = YB[w]
                            cc = ([0, 1] if dr == 0 else [1, 0])[ci]
                            c = 2 * n_ + cc
                            S.op('pe', lambda e: e.matmul(
                                self.ps[:, yb, cc * 64:(cc + 1) * 64], lhsT=Zbd[:],
                                rhs=RpT[:, q, w, cc * 64:(cc + 1) * 64],
                                start=False, stop=(ci == 1)), reads=['c_Zbd', ('c_RpT', q, w)], writes=[('ps', yb)])
                            bZ = nb6()
                            for hh in range(2):
                                S.op('pe', lambda e, hh=hh: e.matmul(
                                    self.ps[:, bZ, 0:128], lhsT=khbd[:, q, w, hh, cc, :], rhs=Vpad[:, q, w, hh, :],
                                    start=(hh == 0), stop=False), reads=[('c_khbd', q, w), ('c_Vpad', q, w)],
                                    writes=[('ps', bZ)])
                                S.op('pe', lambda e, hh=hh: e.matmul(
                                    self.ps[:, bZ, 0:128], lhsT=bhnbd[:, q, w, hh, cc, :], rhs=PQ[:, q, w, 1, hh, :],
                                    start=False, stop=False), reads=[('c_bhnbd', q, w), ('c_PQ', q, w)],
                                    writes=[('ps', bZ)])
                            S.op('pe', lambda e: e.matmul(self.ps[:, bZ, 0:128], lhsT=GmT[:, q, w, cc, :],
                                                          rhs=Zbd[:], start=False, stop=True),
                                 reads=[('c_GmT', q, w), 'c_Zbd'], writes=[('ps', bZ)])
                            S.op('dve', lambda e: e.scalar_tensor_tensor(
                                out=Zst[:], in0=Zst[:], scalar=gam[:, c:c + 1], in1=self.ps[:, bZ, 0:128],
                                op0=ALU.mult, op1=ALU.add), reads=['c_Z', 'c_gam', ('ps', bZ)], writes=['c_Z'])
                            S.op('act', lambda e: e.activation(out=Zbd[:], in_=Zst[:], func=AF.Copy),
                                 reads=['c_Z'], writes=['c_Zbd'])
                            if ci == 1:
                                if dr == 0:
                                    S.op('act', lambda e: e.activation(out=yacc[:, tc], in_=self.ps[:, yb, 0:128],
                                                                       func=AF.Copy), reads=[('ps', yb)],
                                         writes=['c_yacc'])
                                else:
                                    S.op('dve', lambda e: e.tensor_tensor(out=yacc[:, tc], in0=yacc[:, tc],
                                                                          in1=self.ps[:, yb, 0:128], op=ALU.add),
                                         reads=[('ps', yb), 'c_yacc'], writes=['c_yacc'])

                        pending = []

                        def pop_chain():
                            if pending:
                                a = pending.pop(0)
                                chain_step(*a)

                        for pi2, i_ in enumerate(range(0, len(tiles), 2)):
                            pair = tiles[i_:i_ + 2]
                            q = pi2 % 2
                            for w, n_ in enumerate(pair):
                                st_T(n_, w, q)
                            pop_chain()
                            for w, n_ in enumerate(pair):
                                st_S1(n_, w, q)
                            pop_chain()
                            for w, n_ in enumerate(pair):
                                st_S2(n_, w, q)
                            pop_chain()
                            for lv in range(6):
                                for w, n_ in enumerate(pair):
                                    st_S3(n_, w, q, lv)
                                if lv == 0:
                                    pop_chain()
                            while pending:
                                pop_chain()
                            for w, n_ in enumerate(pair):
                                st_S4(n_, w, q)
                            for w, n_ in enumerate(pair):
                                pending.append((n_, w, q, 0))
                                pending.append((n_, w, q, 1))
                        while pending:
                            pop_chain()
                    if BSTOP <= 8:
                        continue
                    for ti, (t0, n) in enumerate(TILES):
                        ts_ = slice(t0, t0 + n)
                        S.op('act', lambda e: e.activation(out=e_bf[:, :n], in_=yacc[:, ts_], func=AF.Copy),
                             reads=['c_yacc'], writes=['c_ebf'])
                        bk = S.next_bank7()
                        S.op('pe', lambda e: e.matmul(self.ps[:, bk, :n], lhsT=self.bones, rhs=e_bf[:, :n], start=True,
                                                      stop=True), reads=['c_ebf', 'cbf'], writes=[('ps', bk)])
                        S.op('dve', lambda e: e.scalar_tensor_tensor(out=e_yc[:, :n], in0=self.ps[:, bk, :n],
                                                                     scalar=-1.0 / 64, in1=yacc[:, ts_], op0=ALU.mult,
                                                                     op1=ALU.add), reads=[('ps', bk), 'c_yacc'],
                             writes=['c_eyc'])
                        S.op('act', lambda e: e.activation(out=e_bf[:, :n], in_=e_yc[:, :n], func=AF.Square),
                             reads=['c_eyc'], writes=['c_ebf'])
                        bk2 = S.next_bank7()
                        S.op('pe', lambda e: e.matmul(self.ps[:, bk2, :n], lhsT=self.bones, rhs=e_bf[:, :n], start=True,
                                                      stop=True), reads=['c_ebf', 'cbf'], writes=[('ps', bk2)])
                        S.op('act', lambda e: e.activation(out=e_rstd[:, :n], in_=self.ps[:, bk2, :n], func=AF.Sqrt,
                                                           scale=1.0 / 64, bias=sm[:, SM['gneps']:SM['gneps'] + 1]),
                             reads=[('ps', bk2), 'smalls'], writes=['c_erstd'])
                        S.op('dve', lambda e: e.reciprocal(out=e_rstd[:, :n], in_=e_rstd[:, :n]), reads=['c_erstd'],
                             writes=['c_erstd'])
                        S.op('dve', lambda e: e.tensor_tensor(out=e_yc[:, :n], in0=e_yc[:, :n], in1=e_rstd[:, :n],
                                                              op=ALU.mult), reads=['c_eyc', 'c_erstd'],
                             writes=['c_eyc'])
                        S.op('pool', lambda e: e.tensor_scalar(out=e_yc[:, :n], in0=e_yc[:, :n],
                                                               scalar1=sm[:, SM['gng'] + hp:SM['gng'] + hp + 1],
                                                               scalar2=sm[:, SM['gnb'] + hp:SM['gnb'] + hp + 1],
                                                               op0=ALU.mult, op1=ALU.add), reads=['c_eyc', 'smalls'],
                             writes=['c_eyc'])
                        S.op('dve', lambda e: e.scalar_tensor_tensor(out=e_bf[:, :n], in0=rm[:, ts_],
                                                                     scalar=sm[:, SM['rk'] + hp:SM['rk'] + hp + 1],
                                                                     in1=ksum[:, ts_], op0=ALU.mult, op1=ALU.mult),
                             reads=['c_rm', 'c_ksum', 'smalls', 'c_ebf'], writes=['c_ebf'])
                        bk3 = S.next_bank7()
                        S.op('pe', lambda e: e.matmul(self.ps[:, bk3, :n], lhsT=self.bones, rhs=e_bf[:, :n], start=True,
                                                      stop=True), reads=['c_ebf', 'cbf'], writes=[('ps', bk3)])
                        S.op('dve', lambda e: e.tensor_tensor(out=e_t[:, :n], in0=self.ps[:, bk3, :n], in1=vm[:, ts_],
                                                              op=ALU.mult), reads=[('ps', bk3), 'c_vm'],
                             writes=['c_et'])
                        S.op('pool', lambda e: e.tensor_tensor(out=e_t[:, :n], in0=e_t[:, :n], in1=e_yc[:, :n],
                                                               op=ALU.add), reads=['c_et', 'c_eyc'], writes=['c_et'])
                        bk4 = S.next_bank7()
                        S.op('pe', lambda e: e.matmul(self.ps[:, bk4, :n], lhsT=g2b[:, hcols], rhs=sgd[:, ts_],
                                                      start=True, stop=True), reads=['c_g2', 'c_sgd'],
                             writes=[('ps', bk4)])
                        sl = nout % 2
                        nout += 1
                        S.op('dve', lambda e, sl=sl: e.tensor_tensor(out=e_out[:, sl, :n], in0=e_t[:, :n],
                                                                     in1=self.ps[:, bk4, :n], op=ALU.mult),
                             reads=['c_et', ('ps', bk4)], writes=[('c_eout', sl)])
                        S.dma('sp', d['orw'][b, hcols, t0:t0 + n], e_out[:, sl, :n], reads=[('c_eout', sl)],
                              writes=[('dram', 'orw', b, ti, hp)])

    def phase_d(self):
        S, nc, d = self.S, self.nc, self.d
        with ExitStack() as ph:
            pa = self.sb(ph, "d_pa", [128, 4, D], BF16)
            pb = self.sb(ph, "d_pb", [128, 4, D], BF16)
            wo = self.sb(ph, "d_wo", [128, 8, D], BF16)
            self.load_w_bf(pa, d['proj_a'], 4, 'd_pa')
            self.load_w_bf(pb, d['proj_b'], 4, 'd_pb')
            self.load_w_bf(wo, d['w_out'], 8, 'd_wo')
            xts = self.sb(ph, "d_x", [128, 2, 8, 512], F32)
            oh = self.sb(ph, "d_oh", [128, 2, 4, 512], BF16)
            orr = self.sb(ph, "d_or", [128, 2, 4, 512], BF16)
            gt = self.sb(ph, "d_gt", [128, 2, 16, 512], BF16)
            t1 = self.sb(ph, "d_t1", [128, 2, 512], F32)
            t2 = self.sb(ph, "d_t2", [128, 2, 512], F32)
            m = self.sb(ph, "d_m", [128, 8, 512], BF16)
            ysb = self.sb(ph, "d_ysb", [128, 8, 512], F32)
            sq = self.sb(ph, "d_sq", [128, 8, 512], BF16)
            rstd = self.sb(ph, "d_rstd", [128, 512], F32)
            tiles = self.seq_tiles(skip_ctx=self.last)
            for it, (b, ti, t0, n, j) in enumerate(tiles):
                xb = it % 2
                xkey = ('d_x', xb)
                S.dma('sp', xts[:, xb, :, :n], d['xT'][b, :, t0:t0 + n].rearrange("(kc p) t -> p kc t", p=128),
                      reads=[('dram', 'xT', b, ti)], writes=[xkey])
                S.dma('sp', oh[:, xb, :, :n], d['ohg'][b, :, t0:t0 + n].rearrange("(c p) t -> p c t", p=128),
                      reads=[('dram', 'ohg', b, ti, hh) for hh in range(4)], writes=[('d_oh', xb)])
                S.dma('sp', orr[:, xb, :, :n], d['orw'][b, :, t0:t0 + n].rearrange("(c p) t -> p c t", p=128),
                      reads=[('dram', 'orw', b, ti, hh) for hh in range(4)], writes=[('d_or', xb)])
                S.dma('pool', gt[:, xb, :, :n], d['gat'][b, :, t0:t0 + n].rearrange("(c p) t -> p c t", p=128),
                      reads=[('dram', 'gat', b, ti)], writes=[('d_gt', xb)])
                for cb in range(8):
                    ba = S.next_bank()
                    for k in range(4):
                        S.op('pe', lambda e, k=k, cb=cb, ba=ba: e.matmul(
                            self.ps[:, ba, :n], lhsT=pa[:, k, cb * 128:(cb + 1) * 128], rhs=oh[:, xb, k, :n],
                            start=(k == 0), stop=(k == 3)), reads=[('d_pa', k), ('d_oh', xb)], writes=[('ps', ba)])
                    bb = S.next_bank()
                    for k in range(4):
                        S.op('pe', lambda e, k=k, cb=cb, bb=bb: e.matmul(
                            self.ps[:, bb, :n], lhsT=pb[:, k, cb * 128:(cb + 1) * 128], rhs=orr[:, xb, k, :n],
                            start=(k == 0), stop=(k == 3)), reads=[('d_pb', k), ('d_or', xb)], writes=[('ps', bb)])
                    tb = cb % 2
                    S.op('dve', lambda e, cb=cb, ba=ba, tb=tb: e.tensor_tensor(
                        out=t1[:, tb, :n], in0=self.ps[:, ba, :n], in1=gt[:, xb, cb, :n], op=ALU.mult),
                        reads=[('ps', ba), ('d_gt', xb)], writes=[('d_t1', tb)])
                    S.op('dve', lambda e, cb=cb, bb=bb, tb=tb: e.tensor_tensor(
                        out=t2[:, tb, :n], in0=self.ps[:, bb, :n], in1=gt[:, xb, 8 + cb, :n], op=ALU.mult),
                        reads=[('ps', bb), ('d_gt', xb)], writes=[('d_t2', tb)])
                    S.op('pool', lambda e, cb=cb, tb=tb: e.tensor_tensor(
                        out=m[:, cb, :n], in0=t1[:, tb, :n], in1=t2[:, tb, :n], op=ALU.add),
                        reads=[('d_t1', tb), ('d_t2', tb)], writes=[('d_m', cb)])
                for cb in range(8):
                    bank = S.next_bank()
                    for kc in range(8):
                        S.op('pe', lambda e, kc=kc, cb=cb, bank=bank: e.matmul(
                            self.ps[:, bank, :n], lhsT=wo[:, kc, cb * 128:(cb + 1) * 128], rhs=m[:, kc, :n],
                            start=(kc == 0), stop=(kc == 7)), reads=[('d_wo', kc), ('d_m', kc)],
                            writes=[('ps', bank)])
                    S.op('act', lambda e, cb=cb, bank=bank: e.activation(out=ysb[:, cb, :n], in_=self.ps[:, bank, :n],
                                                                         func=AF.Copy),
                         reads=[('ps', bank)], writes=['d_ysb'])
                    S.op('act', lambda e, cb=cb, bank=bank: e.activation(out=sq[:, cb, :n], in_=self.ps[:, bank, :n],
                                                                         func=AF.Square),
                         reads=[('ps', bank)], writes=['d_sq'])
                self.post_norm_resid('d_', ysb, sq, rstd, xts[:, xb], xkey, n, self.G1, 'G1', j)
                S.dma('sp', d['xo'][b, :, t0:t0 + n].rearrange("(kc p) t -> p kc t", p=128), xts[:, xb, :, :n],
                      reads=[xkey], writes=[('dram', 'xo', b, ti)])

    def phase_e1(self):
        S, nc, d = self.S, self.nc, self.d
        with ExitStack() as ph:
            w1 = self.sb(ph, "e_w1", [128, 8, 5632], BF16)
            self.load_w_bf(w1, d['ffn_w1'], 8, 'e_w1')
            xts = self.sb(ph, "e_x", [128, 2, 8, 512], F32)
            sq = self.sb(ph, "e_sq", [128, 8, 512], BF16)
            rstd = self.sb(ph, "e_rstd", [128, 512], F32)
            hs = self.sb(ph, "e_h", [128, 2, 8, 512], BF16)
            sg = self.sb(ph, "e_sg", [128, 2, 512], F32)
            us = self.sb(ph, "e_us", [128, 2, 11, 512], BF16)
            tiles = self.seq_tiles(skip_ctx=self.last)
            nu = 0
            for it, (b, ti, t0, n, j) in enumerate(tiles):
                xb = it % 2
                xkey = ('e_x', xb)
                hkey = ('e_h', xb)
                S.dma('sp', xts[:, xb, :, :n], d['xo'][b, :, t0:t0 + n].rearrange("(kc p) t -> p kc t", p=128),
                      reads=[('dram', 'xo', b, ti)], writes=[xkey])
                self.rms_mod_tile(ph, 'e_', xts[:, xb], [xkey], n, self.A2, self.mod[:, 24:32, :], 'A2', 'mod', j,
                                  (sq, rstd, hs[:, xb], hkey))
                h = hs[:, xb]
                for half in range(2):
                    sl = nu % 2
                    nu += 1
                    for fi in range(11):
                        fb = half * 11 + fi
                        bg = S.next_bank()
                        for kc in range(8):
                            S.op('pe', lambda e, kc=kc, fb=fb, bg=bg: e.matmul(
                                self.ps[:, bg, :n], lhsT=w1[:, kc, fb * 128:(fb + 1) * 128], rhs=h[:, kc, :n],
                                start=(kc == 0), stop=(kc == 7)), reads=[('e_w1', kc), hkey], writes=[('ps', bg)])
                        bu = S.next_bank()
                        for kc in range(8):
                            S.op('pe', lambda e, kc=kc, fb=fb, bu=bu: e.matmul(
                                self.ps[:, bu, :n], lhsT=w1[:, kc, 2816 + fb * 128:2816 + (fb + 1) * 128],
                                rhs=h[:, kc, :n], start=(kc == 0), stop=(kc == 7)),
                                reads=[('e_w1', kc), hkey], writes=[('ps', bu)])
                        gb = fb % 2
                        S.op('act', lambda e, bg=bg, gb=gb: e.activation(out=sg[:, gb, :n], in_=self.ps[:, bg, :n],
                                                                         func=AF.Silu),
                             reads=[('ps', bg)], writes=[('e_sg', gb)])
                        S.op('dve', lambda e, bu=bu, gb=gb, fi=fi, sl=sl: e.tensor_tensor(
                            out=us[:, sl, fi, :n], in0=sg[:, gb, :n], in1=self.ps[:, bu, :n], op=ALU.mult),
                            reads=[('ps', bu), ('e_sg', gb)], writes=[('e_us', sl)])
                    S.dma('sp', d['usc'][b, half * 1408:(half + 1) * 1408, t0:t0 + n].rearrange(
                        "(c p) t -> p c t", p=128), us[:, sl, :, :n], reads=[('e_us', sl)],
                        writes=[('dram', 'usc', b, ti)])

    def phase_e2(self):
        S, nc, d = self.S, self.nc, self.d
        with ExitStack() as ph:
            w2 = self.sb(ph, "f_w2", [128, 22, D], BF16)
            self.load_w_bf(w2, d['ffn_w2'], 22, 'f_w2')
            xts = self.sb(ph, "f_x", [128, 2, 8, 512], F32)
            us = self.sb(ph, "f_us", [128, 2, 22, 512], BF16)
            ysb = self.sb(ph, "f_ysb", [128, 8, 512], F32)
            sq = self.sb(ph, "f_sq", [128, 8, 512], BF16)
            rstd = self.sb(ph, "f_rstd", [128, 512], F32)
            tiles = self.seq_tiles(skip_ctx=self.last)
            for it, (b, ti, t0, n, j) in enumerate(tiles):
                xb = it % 2
                xkey = ('f_x', xb)
                S.dma('sp', xts[:, xb, :, :n], d['xo'][b, :, t0:t0 + n].rearrange("(kc p) t -> p kc t", p=128),
                      reads=[('dram', 'xo', b, ti)], writes=[xkey])
                for half in range(2):
                    S.dma('pool' if half else 'sp', us[:, xb, half * 11:(half + 1) * 11, :n],
                          d['usc'][b, half * 1408:(half + 1) * 1408, t0:t0 + n].rearrange("(c p) t -> p c t", p=128),
                          reads=[('dram', 'usc', b, ti)], writes=[('f_us', xb, half)])
                for cb in range(8):
                    bank = S.next_bank()
                    for k in range(22):
                        S.op('pe', lambda e, k=k, cb=cb, bank=bank: e.matmul(
                            self.ps[:, bank, :n], lhsT=w2[:, k, cb * 128:(cb + 1) * 128], rhs=us[:, xb, k, :n],
                            start=(k == 0), stop=(k == 21)), reads=[('f_w2', k), ('f_us', xb, k // 11)],
                            writes=[('ps', bank)])
                    S.op('act', lambda e, cb=cb, bank=bank: e.activation(out=ysb[:, cb, :n], in_=self.ps[:, bank, :n],
                                                                         func=AF.Copy),
                         reads=[('ps', bank)], writes=['f_ysb'])
                    S.op('act', lambda e, cb=cb, bank=bank: e.activation(out=sq[:, cb, :n], in_=self.ps[:, bank, :n],
                                                                         func=AF.Square),
                         reads=[('ps', bank)], writes=['f_sq'])
                self.post_norm_resid('f_', ysb, sq, rstd, xts[:, xb], xkey, n, self.G2, 'G2', j)
                S.dma('sp', d['xo'][b, :, t0:t0 + n].rearrange("(kc p) t -> p kc t", p=128), xts[:, xb, :, :n],
                      reads=[xkey], writes=[('dram', 'xo', b, ti)])


def _blk(v):
    v = np.asarray(v, np.float32).reshape(-1, 128)
    return np.ascontiguousarray(v.T)


def make_consts():
    ident = np.eye(128, dtype=np.float32)
    ones = np.ones((128, 128), np.float32)
    bones = np.zeros((128, 128), np.float32)
    bones[:64, :64] = 1
    bones[64:, 64:] = 1
    consts = np.ascontiguousarray(np.stack([ident, ones, bones], 1))
    j = np.arange(128)[:, None]
    i = np.arange(128)[None, :]
    same = (j // 64) == (i // 64)
    same32 = (j // HC) == (i // HC)
    hmask = np.stack([(same32 & (i >= j)), (same32 & (i <= j))], 1).astype(np.float32)
    rm = np.zeros((128, 2, 640), np.float32)
    for dr in range(2):
        strict = same & ((i > j) if dr == 0 else (i < j))
        incl = same & ((i >= j) if dr == 0 else (i <= j))
        rm[:, dr, 0:128] = -strict.astype(np.float32)
        rm[:, dr, 128:256] = -incl.astype(np.float32)
        rm[:, dr, 256:384] = strict
        rm[:, dr, 384:512] = incl
        rm[:, dr, 512:640] = -strict.T.astype(np.float32)
    rst = np.ones((128, TT), np.float32)
    rst[:, ::64] = 0
    rst32 = np.ones((128, TT), np.float32)
    rst32[:, ::HC] = 0
    return consts, np.ascontiguousarray(hmask), rm, rst, rst32


def make_smalls(inp, l):
    sm = np.zeros((128, NS), np.float32)

    def put(name, arr):
        arr = np.asarray(arr, np.float32)
        sm[:, SM[name]:SM[name] + arr.shape[1]] = arr
    put('ng', np.concatenate([_blk(inp['norm_g'][l, w]) for w in range(4)], 1))
    put('adab', _blk(inp['ada_b'][l]))
    lbl = inp['hg_lb_logits']
    put('lbl', np.concatenate([_blk(lbl[ll].reshape(-1)) for ll in range(DEPTH)], 1))
    lsel = np.zeros((128, 4), np.float32)
    lsel[:, 1:l + 1] = 1.0
    put('lsel', lsel)
    put('hgng', _blk(inp['hg_norm_g'][l]))
    put('mu', _blk(inp['rw_mu'][l]))
    put('w0', _blk(inp['rw_w0'][l].reshape(-1)))
    put('a0', _blk(inp['rw_a0'][l].reshape(-1)))
    put('kk', _blk(inp['rw_kk'][l]))
    put('ka', _blk(inp['rw_ka'][l]))
    put('gng', _blk(inp['rw_gn_g'][l]))
    put('gnb', _blk(inp['rw_gn_b'][l]))
    put('rk', _blk(inp['rw_rk'][l].reshape(-1)))
    p = np.arange(128)
    cls = np.stack([(p % 4 == 0), (p % 4 == 1), (p % 4 == 2), (p % 4 == 3), (p % 2 == 0), (p % 2 == 1)], 1)
    put('cls', cls.astype(np.float32))
    sm[:, SM['eps6']] = 1e-6
    sm[:, SM['gneps']] = 64e-5
    sm[:, SM['tiny']] = 1e-12
    return sm


def layer_inputs(inp, l, consts):
    c = {}
    c['smalls'] = make_smalls(inp, l)
    c['consts'], c['hmask'], c['rmask'], c['rst'], c['rst32'] = consts
    c['ada_w'] = np.ascontiguousarray(inp['ada_w'][l])
    c['w_in'] = np.ascontiguousarray(inp['w_in'][l])
    c['rw_w2'] = np.ascontiguousarray(inp['rw_w2'][l].reshape(128, 512))
    c['rw_a2'] = np.ascontiguousarray(inp['rw_a2'][l].reshape(128, 512))
    c['rw_g2'] = np.ascontiguousarray(inp['rw_g2'][l])
    c['proj_a'] = np.ascontiguousarray(inp['proj_a'][l])
    c['proj_b'] = np.ascontiguousarray(inp['proj_b'][l])
    c['w_out'] = np.ascontiguousarray(inp['w_out'][l])
    c['ffn_w1'] = np.ascontiguousarray(inp['ffn_w1'][l])
    c['ffn_w2'] = np.ascontiguousarray(inp['ffn_w2'][l])
    return c


def make_cT(c_rows, c_ctx):
    m = np.concatenate([c_rows, c_ctx[None, :]], 0).astype(np.float32)
    return np.ascontiguousarray(m.T.reshape(8, 128, -1).transpose(1, 0, 2))


_PROG_CACHE = {}


def get_prog(NB, last):
    key = (NB, last)
    if key not in _PROG_CACHE:
        p = Prog(NB, last_layer=last)
        p.build()
        _PROG_CACHE[key] = p
    return _PROG_CACHE[key]


def kernel(**inp):
    inp = {k: np.asarray(v) for k, v in inp.items()}
    n_cores = 8
    B = inp['x'].shape[0]
    NB = B // n_cores
    xs = np.concatenate([inp['ctx'], inp['x']], axis=1)
    xT = np.ascontiguousarray(xs.transpose(0, 2, 1)).astype(np.float32)
    consts = make_consts()
    cTs = [make_cT(inp['c'][i * NB:(i + 1) * NB], inp['c_ctx']) for i in range(n_cores)]
    for l in range(DEPTH):
        prog = get_prog(NB, False)
        com = layer_inputs(inp, l, consts)
        in_maps = []
        for i in range(n_cores):
            m = dict(com)
            m['xT'] = np.ascontiguousarray(xT[i * NB:(i + 1) * NB])
            m['cT'] = cTs[i]
            in_maps.append(m)
        res = run_bass_kernel_spmd(prog.nc, in_maps, core_ids=list(range(n_cores)))
        xT = np.concatenate([r['xo'] for r in res.results], 0)
    out = xT[:, :, NCTX:].transpose(0, 2, 1)
    return np.ascontiguousarray(out).astype(np.float32)
```
